# Optimizing a Trainium2 kernel written in Bass

```python
import jax, jax.numpy as jnp
from jax import lax
import numpy as np

D_MODEL = 1024
BATCH = 8
SEQ = 2048
DEPTH = 2

N_EVEN = (DEPTH + 1) // 2
N_ODD = DEPTH // 2
EPS = 1e-6

CONV_CH = 512
CONV_WIDTH = 31
HEAD_DIM = 64
HEADS_PER_GROUP = 8
DILATED_PAIRS = ((128, 1), (512, 4), (2048, 16))
N_GROUPS = len(DILATED_PAIRS)
ATTN_HEADS = N_GROUPS * HEADS_PER_GROUP
ATTN_WIDTH = ATTN_HEADS * HEAD_DIM
ATTN_OUT = HEADS_PER_GROUP * HEAD_DIM
ROPE_THETA = 10000.0
EVEN_IN = 2 * CONV_CH + 3 * ATTN_WIDTH
EVEN_OUT = CONV_CH + ATTN_OUT
SCONV_CH = 512
SCONV_WIDTH = 3
SG_GROUPS = 4
SG_HEAD = 128
SG_CH = SG_GROUPS * SG_HEAD
CHUNK = 128
ODD_IN = 3 * SCONV_CH + 2 * SG_CH
ODD_OUT = SCONV_CH + SG_CH
D_FF = 4 * D_MODEL

kernel_name = "hybrid_conformer_dilated_shortconv_gmlp_trunk"


def rms_norm(x, g):
    xf = x.astype(jnp.float32)
    y = xf * lax.rsqrt(jnp.mean(xf * xf, axis=-1, keepdims=True) + EPS)
    return (y * g.astype(jnp.float32)).astype(x.dtype)


def layer_norm(x, g, b):
    xf = x.astype(jnp.float32)
    mu = jnp.mean(xf, axis=-1, keepdims=True)
    xc = xf - mu
    y = xc * lax.rsqrt(jnp.mean(xc * xc, axis=-1, keepdims=True) + EPS)
    return (y * g.astype(jnp.float32) + b.astype(jnp.float32)).astype(x.dtype)


def rope_tables(seq):
    half = HEAD_DIM // 2
    inv = ROPE_THETA ** (-jnp.arange(half, dtype=jnp.float32) / half)
    ang = jnp.arange(seq, dtype=jnp.float32)[:, None] * inv[None, :]
    return jnp.cos(ang), jnp.sin(ang)


def apply_rope(x, cos, sin):
    x1, x2 = jnp.split(x, 2, axis=-1)
    c = cos[None, :, None, :]
    s = sin[None, :, None, :]
    return jnp.concatenate([x1 * c - x2 * s, x2 * c + x1 * s], axis=-1).astype(x.dtype)


def causal_depthwise_conv(x, kern):
    w = kern.shape[0]
    return lax.conv_general_dilated(
        x, kern[:, None, :].astype(x.dtype), window_strides=(1,),
        padding=[(w - 1, 0)], dimension_numbers=('NWC', 'WIO', 'NWC'),
        feature_group_count=x.shape[-1])


def banded_causal_attention(q, k, v, band):
    b, r, l, h, dh = q.shape
    nb = -(-l // band)
    lp = nb * band
    pad = ((0, 0), (0, 0), (0, lp - l), (0, 0), (0, 0))
    qb = jnp.pad(q, pad).reshape(b, r, nb, band, h, dh)
    kb = jnp.pad(k, pad).reshape(b, r, nb, band, h, dh)
    vb = jnp.pad(v, pad).reshape(b, r, nb, band, h, dh)
    zk = jnp.zeros_like(kb[:, :, :1])
    k2 = jnp.concatenate([jnp.concatenate([zk, kb[:, :, :-1]], axis=2), kb], axis=3)
    v2 = jnp.concatenate([jnp.concatenate([zk, vb[:, :, :-1]], axis=2), vb], axis=3)
    s = jnp.einsum('brnqhd,brnkhd->brnhqk', qb, k2,
                   preferred_element_type=jnp.float32) * (HEAD_DIM ** -0.5)
    qi = jnp.arange(band)[:, None] + band
    ki = jnp.arange(2 * band)[None, :]
    dist = qi - ki
    local = (dist >= 0) & (dist <= band)
    kvalid = (jnp.arange(nb)[:, None] * band - band + ki) >= 0
    mask = local[None, :, :] & kvalid[:, None, :]
    s = jnp.where(mask[None, None, :, None], s, -jnp.inf)
    lse = jax.nn.logsumexp(s, axis=-1, keepdims=True)
    p = jnp.exp(s - lse)
    o = jnp.einsum('brnhqk,brnkhd->brnqhd', p, v2.astype(jnp.float32))
    o = o.reshape(b, r, lp, h, dh)[:, :, :l]
    lse = lse[..., 0].transpose(0, 1, 2, 4, 3).reshape(b, r, lp, h)[:, :, :l]
    return o, lse


def dilated_group_attention(q, k, v, window, dilation):
    b, s, h, dh = q.shape
    l = s // dilation

    def to_res(t):
        return t.reshape(b, l, dilation, h, dh).transpose(0, 2, 1, 3, 4)

    o, lse = banded_causal_attention(to_res(q), to_res(k), to_res(v), window // dilation)
    o = o.transpose(0, 2, 1, 3, 4).reshape(b, s, h, dh)
    lse = lse.transpose(0, 2, 1, 3).reshape(b, s, h)
    return o, lse


def even_mixer(h, w_in, conv_k, conv_b, ln_g, ln_b, w_out, cos, sin):
    b, s, _ = h.shape
    z = h @ w_in
    a_lin, a_gate, qkv = jnp.split(z, [CONV_CH, 2 * CONV_CH], axis=-1)
    a = a_lin * jax.nn.sigmoid(a_gate)
    a = causal_depthwise_conv(a, conv_k) + conv_b.astype(a.dtype)
    a = jax.nn.silu(layer_norm(a, ln_g, ln_b))
    q, k, v = jnp.split(qkv, 3, axis=-1)
    q = apply_rope(q.reshape(b, s, ATTN_HEADS, HEAD_DIM), cos, sin)
    k = apply_rope(k.reshape(b, s, ATTN_HEADS, HEAD_DIM), cos, sin)
    v = v.reshape(b, s, ATTN_HEADS, HEAD_DIM)
    outs, lses = [], []
    for g, (window, dilation) in enumerate(DILATED_PAIRS):
        sl = slice(g * HEADS_PER_GROUP, (g + 1) * HEADS_PER_GROUP)
        o, lse = dilated_group_attention(q[:, :, sl], k[:, :, sl], v[:, :, sl], window, dilation)
        outs.append(o)
        lses.append(lse)
    wts = jax.nn.softmax(jnp.stack(lses, axis=0), axis=0)
    att = jnp.sum(wts[..., None] * jnp.stack(outs, axis=0), axis=0)
    att = att.reshape(b, s, ATTN_OUT).astype(a.dtype)
    return jnp.concatenate([a, att], axis=-1) @ w_out


def odd_mixer(h, w_in, sconv_k, sg_ln_g, sg_ln_b, sg_w, sg_b, w_out):
    b, s, _ = h.shape
    z = h @ w_in
    gb, gc, xs, uv = jnp.split(z, [SCONV_CH, 2 * SCONV_CH, 3 * SCONV_CH], axis=-1)
    c_out = gb * causal_depthwise_conv(gc * xs, sconv_k)
    u, v = jnp.split(jax.nn.gelu(uv), 2, axis=-1)
    v = layer_norm(v, sg_ln_g, sg_ln_b)
    v = v.reshape(b, s // CHUNK, CHUNK, SG_GROUPS, SG_HEAD)
    ws = sg_w * jnp.tril(jnp.ones((CHUNK, CHUNK), dtype=sg_w.dtype))[None]
    v = jnp.einsum('gts,bnsgc->bntgc', ws.astype(v.dtype), v) + sg_b.T.astype(v.dtype)[None, None, :, :, None]
    d_out = u * v.reshape(b, s, SG_CH)
    return jnp.concatenate([c_out, d_out], axis=-1) @ w_out


def channel_mixer(h, w1, w2):
    return jnp.square(jax.nn.relu(h @ w1)) @ w2


def setup_inputs(seed: int = 0) -> dict:
    key = jax.random.key(seed)
    ks = jax.random.split(key, 20)
    f32 = jnp.float32

    def nrm(k, shape, scale):
        return jax.random.normal(k, shape, f32) * scale

    return {
        "x": nrm(ks[0], (BATCH, SEQ, D_MODEL), 1.0),
        "norm_mix_g": 1.0 + nrm(ks[1], (DEPTH, D_MODEL), 0.02),
        "norm_ffn_g": 1.0 + nrm(ks[2], (DEPTH, D_MODEL), 0.02),
        "even_w_in": nrm(ks[3], (N_EVEN, D_MODEL, EVEN_IN), D_MODEL ** -0.5),
        "even_conv_k": nrm(ks[4], (N_EVEN, CONV_WIDTH, CONV_CH), CONV_WIDTH ** -0.5),
        "even_conv_b": nrm(ks[5], (N_EVEN, CONV_CH), 0.02),
        "even_ln_g": 1.0 + nrm(ks[6], (N_EVEN, CONV_CH), 0.02),
        "even_ln_b": nrm(ks[7], (N_EVEN, CONV_CH), 0.02),
        "even_w_out": nrm(ks[8], (N_EVEN, EVEN_OUT, D_MODEL), EVEN_OUT ** -0.5),
        "odd_w_in": nrm(ks[9], (N_ODD, D_MODEL, ODD_IN), D_MODEL ** -0.5),
        "odd_conv_k": nrm(ks[10], (N_ODD, SCONV_WIDTH, SCONV_CH), SCONV_WIDTH ** -0.5),
        "odd_ln_g": 1.0 + nrm(ks[11], (N_ODD, SG_CH), 0.02),
        "odd_ln_b": nrm(ks[12], (N_ODD, SG_CH), 0.02),
        "odd_sg_w": nrm(ks[13], (N_ODD, SG_GROUPS, CHUNK, CHUNK), CHUNK ** -0.5),
        "odd_sg_b": 1.0 + nrm(ks[14], (N_ODD, SG_GROUPS, CHUNK), 0.02),
        "odd_w_out": nrm(ks[15], (N_ODD, ODD_OUT, D_MODEL), ODD_OUT ** -0.5),
        "ffn_w1": nrm(ks[16], (DEPTH, D_MODEL, D_FF), D_MODEL ** -0.5),
        "ffn_w2": nrm(ks[17], (DEPTH, D_FF, D_MODEL), D_FF ** -0.5),
        "final_g": 1.0 + nrm(ks[18], (D_MODEL,), 0.02),
    }


def reference(x, norm_mix_g, norm_ffn_g, even_w_in, even_conv_k, even_conv_b, even_ln_g,
              even_ln_b, even_w_out, odd_w_in, odd_conv_k, odd_ln_g, odd_ln_b, odd_sg_w,
              odd_sg_b, odd_w_out, ffn_w1, ffn_w2, final_g):
    cos, sin = rope_tables(x.shape[1])
    h = x
    for i in range(DEPTH):
        hn = rms_norm(h, norm_mix_g[i])
        if i % 2 == 0:
            j = i // 2
            mix = even_mixer(hn, even_w_in[j], even_conv_k[j], even_conv_b[j], even_ln_g[j],
                             even_ln_b[j], even_w_out[j], cos, sin)
        else:
            j = i // 2
            mix = odd_mixer(hn, odd_w_in[j], odd_conv_k[j], odd_ln_g[j], odd_ln_b[j],
                            odd_sg_w[j], odd_sg_b[j], odd_w_out[j])
        h = h + mix.astype(h.dtype)
        h = h + channel_mixer(rms_norm(h, norm_ffn_g[i]), ffn_w1[i], ffn_w2[i]).astype(h.dtype)
    return rms_norm(h, final_g)
```

```python
import numpy as np
import concourse.bass as bass
import concourse.mybir as mybir
from concourse.bass_utils import run_bass_kernel_spmd
from contextlib import ExitStack

F32 = mybir.dt.float32
BF16 = mybir.dt.bfloat16
ALU = mybir.AluOpType
AF = mybir.ActivationFunctionType

D = 1024
S = 2048
NT = 4
TT = 512
KC = 8
EPS = 1e-6
D_FF = 4096
EVEN_IN = 5632
ODD_IN = 2560
N_CORES = 8


class Buf:
    __slots__ = ("name", "w", "r", "excl")

    def __init__(self, name, excl=False):
        self.name = name
        self.w = []
        self.r = []
        self.excl = excl


class Ent:
    __slots__ = ("eng", "fn", "deps", "dma", "inc", "cum", "idx", "waits")

    def __init__(self, eng, fn, dma):
        self.eng = eng
        self.fn = fn
        self.deps = []
        self.dma = dma
        self.inc = False
        self.cum = 0
        self.idx = 0
        self.waits = []


class Sched:
    ENGS = ("pe", "act", "dve", "pool", "sp")

    def __init__(self, nc, stack):
        self.nc = nc
        self.stack = stack
        self.q = {e: [] for e in self.ENGS}
        self.esem = {e: stack.enter_context(nc.semaphore("s_" + e)) for e in self.ENGS}
        self.dsem = {}
        self.dcount = {}
        self.all_ents = []

    def _dsem(self, key):
        if key not in self.dsem:
            self.dsem[key] = self.stack.enter_context(self.nc.semaphore("d_" + key))
            self.dcount[key] = 0
        return self.dsem[key]

    def add(self, eng, fn, reads=(), writes=(), dma=None, strict=False):
        e = Ent(eng, fn, dma)
        deps = []
        for b in reads:
            deps.extend(b.w)
            if b.excl:
                deps.extend(b.r)
        for b in writes:
            deps.extend(b.w)
            deps.extend(b.r)
        seen = set()
        for d in deps:
            if id(d) in seen:
                continue
            seen.add(id(d))
            if d.dma is None and d.eng == eng and dma is None and not strict:
                if eng == "pe":
                    continue
                is_raw = any(d in b.w for b in reads)
                if not is_raw:
                    continue
            e.deps.append(d)
        for b in writes:
            b.w = [e]
            b.r = []
        for b in reads:
            if b.excl:
                b.w = [e]
                b.r = []
            else:
                b.r.append(e)
        e.idx = len(self.q[eng])
        self.q[eng].append(e)
        if dma is not None:
            self._dsem(dma)
            self.dcount[dma] += 16
            e.cum = self.dcount[dma]
        self.all_ents.append(e)
        return e

    def barrier(self, bufs):
        last = [self.q[e][-1] for e in self.ENGS if self.q[e]]
        for b in bufs:
            b.w = list(last)
            b.r = []

    def finalize(self):
        for e in self.all_ents:
            for d in e.deps:
                if d.dma is None:
                    d.inc = True
        for eng in self.ENGS:
            c = 0
            for e in self.q[eng]:
                if e.dma is None:
                    if e.inc:
                        c += 1
                    e.cum = c
        for eng in self.ENGS:
            seen = {}
            for e in self.q[eng]:
                need = {}
                for d in e.deps:
                    key = ("d", d.dma) if d.dma is not None else ("e", d.eng)
                    if d.cum > need.get(key, 0):
                        need[key] = d.cum
                for key, v in need.items():
                    if seen.get(key, 0) >= v:
                        continue
                    seen[key] = v
                    sem = self.dsem[key[1]] if key[0] == "d" else self.esem[key[1]]
                    e.waits.append((sem, v))

    def emit(self, final_waits):
        nc = self.nc
        self.finalize()

        def run(eng, h):
            for e in self.q[eng]:
                for sem, v in e.waits:
                    h.wait_ge(sem, v)
                ins = e.fn(h)
                if e.dma is not None:
                    ins.then_inc(self.dsem[e.dma], 16)
                elif e.inc:
                    ins.then_inc(self.esem[eng], 1)
            if eng == "sp":
                for key in final_waits:
                    h.wait_ge(self.dsem[key], self.dcount[key])

        with nc.Block() as block:
            @block.tensor
            def _(h):
                run("pe", h)

            @block.scalar
            def _(h):
                run("act", h)

            @block.vector
            def _(h):
                run("dve", h)

            @block.gpsimd
            def _(h):
                run("pool", h)

            @block.sync
            def _(h):
                run("sp", h)


HD = 64
DIL = (1, 4, 16)
NBLK = (16, 4, 1)
CST_COLS = 128 + 128 + 512 + 4 + 128 + 128 + 192
GELU_K = 1.5957691216057308


def _prod(xs):
    r = 1
    for v in xs:
        r *= v
    return r


class Arena:
    def __init__(self, ap32):
        self.ap = ap32
        self.n = ap32.shape[1]
        self.off = 0

    def reset(self, off=0):
        self.off = off

    def get(self, shape, dt):
        n = _prod(shape)
        nb = n * (4 if dt == F32 else 2)
        n32 = (nb + 31) // 32 * 8
        assert self.off + n32 <= self.n, ("arena overflow", self.off, n32, self.n)
        v = self.ap[:, self.off:self.off + n32]
        self.off += n32
        if dt != F32:
            v = v.bitcast(dt)
        v = v[:, 0:n]
        if len(shape) == 2:
            v = v.rearrange("p (a b) -> p a b", a=shape[0])
        elif len(shape) == 3:
            v = v.rearrange("p (a b c) -> p a b c", a=shape[0], b=shape[1])
        return v


class Builder:
    def __init__(self, cfg):
        self.cfg = cfg
        self.nc = bass.Bass("TRN2", target_bir_lowering=False)
        self.stack = ExitStack()
        self.sc = None
        self.bank_rr = {}
        self.pools = {"gen": list(range(8))}
        self.plan = []
        self.plan_i = 0
        self.issued = 0
        self.NW = 3
        self.x_gate = None
        self.live_lo = 0
        self.rr = {}

    def dram_in(self, name, shape):
        return self.nc.dram_tensor(name, list(shape), F32, kind="ExternalInput").ap()

    def sb(self, name, shape, dt):
        return self.stack.enter_context(self.nc.sbuf_tensor(name, list(shape), dt))

    def next_bank(self, pool="gen"):
        lst = self.pools[pool]
        i = self.bank_rr.get(pool, 0)
        self.bank_rr[pool] = (i + 1) % len(lst)
        return lst[i]

    def rot(self, key, n):
        i = self.rr.get(key, 0)
        self.rr[key] = (i + 1) % n
        return i

    def add(self, *a, **k):
        return self.sc.add(*a, **k)

    def _issue(self, upto):
        while self.issued < min(upto, len(self.plan)) and self.issued - self.NW < self.live_lo:
            i = self.issued
            s = i % self.NW
            src = self.plan[i][1].rearrange("(kc p) c -> p kc c", p=128)
            dst = self.wring[s]
            ent = self.add("pool", lambda h, dst=dst, src=src: h.dma_start(out=dst[:], in_=src),
                           writes=[self.wbuf[s]], dma="w%d" % s)
            if i == 0 and self.x_gate is not None and self.cfg.get("xgate", True):
                ent.deps = ent.deps + [self.x_gate]
            self.issued += 1

    def use_block(self, key, group_start=True):
        i = self.plan_i
        assert self.plan[i][0] == key, (self.plan[i][0], key)
        if group_start:
            self.live_lo = i
        self._issue(i + 3)
        self.plan_i += 1
        return i % self.NW

    def full_barrier(self):
        sc = self.sc
        last = {e: (sc.q[e][-1] if sc.q[e] else None) for e in sc.ENGS}
        for e in sc.ENGS:
            ent = sc.add(e, lambda h: h.nop())
            ent.deps = [last[o] for o in sc.ENGS if last[o] is not None]

    def build(self):
        nc = self.nc
        cfg = self.cfg
        st = self.stack
        with st:
            self.sc = sc = Sched(nc, st)
            add = self.add
            x = self.dram_in("x", [S, D])
            gains = self.dram_in("gains", [128, 5 * KC])
            identd = self.dram_in("ident_d", [128, 128])
            cst_d = self.dram_in("cst_d", [128, CST_COLS])
            w_in0 = self.dram_in("w_in0", [D, EVEN_IN])
            w_out0 = self.dram_in("w_out0", [D, D])
            w_in1 = self.dram_in("w_in1", [D, ODD_IN])
            w_out1 = self.dram_in("w_out1", [D, D])
            w1 = [self.dram_in("ffn_w1_%d" % l, [D, D_FF]) for l in range(2)]
            w2 = [self.dram_in("ffn_w2_%d" % l, [D_FF, D]) for l in range(2)]
            small_d = self.dram_in("small_d", [128, 160])
            sel_d = self.dram_in("sel_d", [2, 128])
            grow_d = self.dram_in("grow_d", [128, D])
            rope_d = self.dram_in("rope_d", [128, 2, S])
            wsT_d = self.dram_in("wsT_d", [128, 4, 128])
            sgb_d = self.dram_in("sgb_d", [128, 4, 128])
            out = nc.dram_tensor("out", [S, D], F32, kind="ExternalOutput").ap()

            plan = []
            if cfg.get("l0mix", True):
                if 'a' in cfg.get('l0p', 'ab'):
                    plan += [(("a", 0), w_in0[:, 0:512]), (("a", 1), w_in0[:, 512:1024])]
                plan += [(("wo0", h), w_out0[:, h * 512:(h + 1) * 512]) for h in range(2)]
            def ffn_plan(l):
                r = []
                for fg in range(4):
                    for blk in range(2):
                        c0 = fg * 1024 + blk * 512
                        r.append((("w1", l, fg, blk), w1[l][:, c0:c0 + 512]))
                    for half in range(2):
                        r.append((("w2", l, fg, half), w2[l][fg * 1024:(fg + 1) * 1024, half * 512:(half + 1) * 512]))
                return r
            if cfg.get("ffn0", True):
                plan += ffn_plan(0)
            if cfg.get("l1mix", True):
                plan += [(("i1", j), w_in1[:, j * 512:(j + 1) * 512]) for j in range(5)]
                plan += [(("wo1", h), w_out1[:, h * 512:(h + 1) * 512]) for h in range(2)]
            if cfg.get("ffn1", True):
                plan += ffn_plan(1)
            self.plan = plan

            ps = st.enter_context(nc.psum_tensor("ps", [128, 8, 512], F32))
            self.ps = ps
            pbuf = self.pbuf = [Buf("bank%d" % b, excl=True) for b in range(8)]
            arena_t = self.sb("arena", [128, KC * S], F32)
            hT = arena_t[:, :].rearrange("p (k t) -> p k t", k=KC)
            ar = Arena(arena_t[:, :])
            hTb = [[Buf("hT%d_%d" % (k, n)) for n in range(NT)] for k in range(KC)]
            hn = self.sb("hn", [128, KC, S], BF16)
            hnb = [[Buf("hn%d_%d" % (k, n)) for n in range(NT)] for k in range(KC)]
            mixcat = self.sb("mixcat", [128, KC, S], BF16)
            mcb = [[Buf("mc%d_%d" % (k, n)) for n in range(NT)] for k in range(KC)]
            u = mixcat
            ub = mcb
            self.wring = [self.sb("wr%d" % s_, [128, 8, 512], BF16) for s_ in range(self.NW)]
            self.wbuf = [Buf("wr%d" % s_) for s_ in range(self.NW)]
            arena2_t = self.sb("arena2", [128, 7680], F32)
            ar2 = Arena(arena2_t[:, :])
            ident = self.sb("ident", [128, 128], F32)
            identb = Buf("ident")
            cst = self.sb("cst", [128, CST_COLS], BF16)
            cstb = Buf("cst")
            identh = cst[:, 0:128]
            pswap = cst[:, 128:256]
            mask2 = cst[:, 256:768].rearrange("p (h q) -> p h q", h=2)
            Emat = cst[:, 768:772]
            trilT = cst[:, 772:900]
            sel = cst[0:2, 900:1028]
            OZ = cst[:, 1028:1220]
            ones_m = self.sb("ones_m", [128, 128], BF16)
            ones5 = self.sb("ones5", [128, 128], BF16)
            ones1 = self.sb("ones1", [128, 128], BF16)
            onesb = Buf("ones")
            g_sb = self.sb("g_sb", [128, 5 * KC], F32)
            gb_ = Buf("g")
            small = self.sb("small", [128, 160], F32)
            smallb = Buf("small")
            sel32 = self.sb("sel32", [2, 128], F32)
            sel32b = Buf("sel32")
            eps_sb = self.sb("eps_sb", [128, 1], F32)
            epsb = Buf("eps")
            NXT = 6
            xt = [arena2_t[:, i * D:(i + 1) * D] for i in range(NXT)]
            xtb = [Buf("xt%d" % i) for i in range(NXT)]
            ot, otb = xt, xtb
            sq = [self.sb("sq%d" % i, [128, TT], BF16) for i in range(3)]
            sqb = [Buf("sq%d" % i) for i in range(3)]
            rstd = [self.sb("rstd%d" % i, [128, TT], F32) for i in range(2)]
            rstdb = [Buf("rstd%d" % i) for i in range(2)]
            rl = [self.sb("rl%d" % i, [128, TT], BF16) for i in range(3)]
            rlb = [Buf("rl%d" % i) for i in range(3)]
            fss = [self.sb("fss%d" % i, [128, 4], F32) for i in range(2)]
            fssb = [Buf("fss%d" % i) for i in range(2)]
            fjunk = [rl[i] for i in range(3)]
            fjunkb = [rlb[i] for i in range(3)]
            grow = arena2_t[:, 4096:4096 + D]
            growb = Buf("grow")
            self.fin_ready = False

            add("sp", lambda h: h.dma_start(out=g_sb[:], in_=gains), writes=[gb_], dma="g")
            add("sp", lambda h: h.dma_start(out=ident[:], in_=identd), writes=[identb], dma="id")
            add("sp", lambda h: h.dma_start(out=small[:], in_=small_d), writes=[smallb], dma="sm")
            add("sp", lambda h: h.dma_start(out=sel32[:], in_=sel_d), writes=[sel32b], dma="sel")
            add("pool", lambda h: h.dma_start(out=cst[:], in_=cst_d), writes=[cstb], dma="cst")
            add("dve", lambda h: h.memset(ones_m[:], 1.0 / 1024.0), writes=[onesb])
            add("dve", lambda h: h.memset(ones5[:], 1.0 / 512.0), writes=[onesb])
            add("dve", lambda h: h.memset(ones1[:], 1.0), writes=[onesb])
            add("dve", lambda h: h.memset(eps_sb[:], EPS), writes=[epsb])

            wsT = self.sb("wsT", [128, 4, 128], BF16)
            wsTb = Buf("wsT")
            Gt = self.sb("Gt", [128, 4, 128], F32)
            Bt = self.sb("Bt", [128, 4, 128], F32)
            gbtb_ = Buf("GtBt")
            diag3 = self.sb("diag3", [128, 4, 3, 128], BF16)
            diag3b = Buf("diag3")
            if cfg.get("l1mix", True):
                oddk = small[:, 136:148].rearrange("p (c j) -> p c j", c=4)
                oddg = small[:, 148:152]
                oddb = small[:, 152:156]
                sgb = arena2_t[:, 7040:7040 + 512].rearrange("p (g t) -> p g t", g=4)
                sgbb = Buf("sgb")
                add("pool", lambda h: h.dma_start(out=wsT[:], in_=wsT_d), writes=[wsTb], dma="wsT")
                add("sp", lambda h: h.dma_start(out=sgb, in_=sgb_d), writes=[sgbb], dma="sgb")
                add("dve", lambda h: h.tensor_tensor(wsT[:], wsT[:], trilT.unsqueeze(1).broadcast_to([128, 4, 128]), ALU.mult),
                    reads=[cstb], writes=[wsTb])
                bws = self.next_bank()
                add("pe", lambda h, bws=bws: h.matmul(ps[:, bws, :], ones1[:], wsT[:].rearrange("p g t -> p (g t)"), start=True, stop=True),
                    reads=[wsTb, onesb], writes=[pbuf[bws]])
                for g_ in range(4):
                    add("dve", lambda h, g_=g_, bws=bws: h.scalar_tensor_tensor(
                        Bt[:, g_, :], ps[:, bws, g_ * 128:(g_ + 1) * 128], oddb[:, g_:g_ + 1], sgb[:, g_, :], ALU.mult, ALU.add),
                        reads=[pbuf[bws], smallb, sgbb], writes=[gbtb_])
                add("dve", lambda h: h.tensor_copy(Gt[:], oddg.unsqueeze(2).broadcast_to([128, 4, 128])), reads=[smallb], writes=[gbtb_])
                ckb3 = arena2_t[:, 7552:7552 + 8].bitcast(BF16)[:, 0:12].rearrange("p (c j) -> p c j", c=4)
                ckb3b = Buf("ckb3")
                add("dve", lambda h: h.tensor_copy(ckb3, oddk), reads=[smallb], writes=[ckb3b])
                for c in range(4):
                    add("dve", lambda h, c=c: h.tensor_tensor(
                        diag3[:, c, :, :], identh.unsqueeze(1).broadcast_to([128, 3, 128]),
                        ckb3[:, c, :].unsqueeze(2).broadcast_to([128, 3, 128]), ALU.mult),
                        reads=[cstb, ckb3b], writes=[diag3b])

            def load_x_to_hT(after_n=None, prefetched=0):
                for t in range(16):
                    i = t % NXT
                    if t >= prefetched:
                        ent_x = add("sp", lambda h, i=i, t=t: h.dma_start(out=xt[i], in_=x[t * 128:(t + 1) * 128, :]),
                                    writes=[xtb[i]], dma="xt%d" % i)
                        if t == 3 and self.x_gate is None:
                            self.x_gate = ent_x
                    n = t // 4
                    for half in range(2):
                        b = self.next_bank()
                        for j in range(4):
                            k = half * 4 + j
                            add("pe", lambda h, b=b, j=j, k=k, i=i: h.transpose(
                                ps[:, b, j * 128:(j + 1) * 128], xt[i][:, k * 128:(k + 1) * 128], ident[:]),
                                reads=[xtb[i], identb], writes=[pbuf[b]])
                        dstv = hT[:, half * 4:half * 4 + 4, t * 128:(t + 1) * 128]
                        srcv = ps[:, b, :].rearrange("p (j c) -> p j c", j=4)
                        wr = [hTb[half * 4 + j][n] for j in range(4)]
                        if half == 0:
                            add("act", lambda h, dstv=dstv, srcv=srcv: h.copy(dstv, srcv), reads=[pbuf[b]], writes=wr)
                        else:
                            add("dve", lambda h, dstv=dstv, srcv=srcv: h.tensor_copy(dstv, srcv), reads=[pbuf[b]], writes=wr)
                    if after_n is not None and t % 4 == 3:
                        after_n(t // 4)

            def rstd_from_bank(b, ri):
                if cfg.get("lnexp", True):
                    add("act", lambda h, b=b, ri=ri: h.activation(rstd[ri][:], ps[:, b, :], AF.Ln, bias=eps_sb[:, 0:1]),
                        reads=[pbuf[b], epsb], writes=[rstdb[ri]])
                    add("act", lambda h, ri=ri: h.activation(rstd[ri][:], rstd[ri][:], AF.Exp, scale=-0.5),
                        reads=[rstdb[ri]], writes=[rstdb[ri]])
                else:
                    add("act", lambda h, b=b, ri=ri: h.activation(rstd[ri][:], ps[:, b, :], AF.Sqrt, bias=eps_sb[:, 0:1]),
                        reads=[pbuf[b], epsb], writes=[rstdb[ri]])
                    add("dve", lambda h, ri=ri: h.reciprocal(rstd[ri][:], rstd[ri][:]),
                        reads=[rstdb[ri]], writes=[rstdb[ri]])

            def sumsq_bank(n):
                tsl = slice(n * TT, (n + 1) * TT)
                b = self.next_bank()
                for k in range(KC):
                    i = self.rot("sq", 3)
                    if k % 2 == 0:
                        add("act", lambda h, i=i, k=k, tsl=tsl: h.activation(sq[i][:], hT[:, k, tsl], AF.Square),
                            reads=[hTb[k][n]], writes=[sqb[i]])
                    else:
                        add("pool", lambda h, i=i, k=k, tsl=tsl: h.tensor_tensor(sq[i][:], hT[:, k, tsl], hT[:, k, tsl], ALU.mult),
                            reads=[hTb[k][n]], writes=[sqb[i]])
                    add("pe", lambda h, b=b, i=i, k=k: h.matmul(ps[:, b, :], ones_m[:], sq[i][:],
                                                              start=(k == 0), stop=(k == KC - 1)),
                        reads=[sqb[i], onesb], writes=[pbuf[b]])
                return b

            def rmsnorm_tile(gidx, n):
                tsl = slice(n * TT, (n + 1) * TT)
                b = sumsq_bank(n)
                ri = self.rot("rstd", 2)
                rstd_from_bank(b, ri)
                for k in range(KC):
                    add("dve", lambda h, k=k, tsl=tsl, ri=ri: h.scalar_tensor_tensor(
                        hn[:, k, tsl], hT[:, k, tsl], g_sb[:, gidx * KC + k:gidx * KC + k + 1], rstd[ri][:],
                        ALU.mult, ALU.mult),
                        reads=[hTb[k][n], rstdb[ri], gb_], writes=[hnb[k][n]])

            def rmsnorm(gidx):
                for n in range(NT):
                    rmsnorm_tile(gidx, n)

            def proj_add(keys, src, srcb, tail=None):
                pend_tail = [None]
                for half in range(2):
                    s_ = self.use_block(keys[half])
                    order = [(mc, n) for mc in range(4) for n in range(NT)]
                    if half == 1 and tail is not None:
                        order = [(mc, n) for n in range(NT) for mc in range(4)]
                    for (mc, n) in order:
                        m = half * 4 + mc
                        if True:
                            tsl = slice(n * TT, (n + 1) * TT)
                            b = self.next_bank()
                            for fc in range(8):
                                add("pe", lambda h, b=b, s_=s_, fc=fc, mc=mc, tsl=tsl: h.matmul(
                                    ps[:, b, :], self.wring[s_][:, fc, mc * 128:(mc + 1) * 128], src[:, fc, tsl],
                                    start=(fc == 0), stop=(fc == 7)),
                                    reads=[self.wbuf[s_], srcb[fc][n]], writes=[pbuf[b]])
                            add("dve", lambda h, b=b, m=m, tsl=tsl: h.tensor_tensor(
                                hT[:, m, tsl], ps[:, b, :], hT[:, m, tsl], ALU.add),
                                reads=[pbuf[b], hTb[m][n]], writes=[hTb[m][n]])
                            if half == 1 and tail is not None:
                                if mc == 1 and pend_tail[0] is not None:
                                    tail(pend_tail[0])
                                    pend_tail[0] = None
                                if mc == 3:
                                    pend_tail[0] = n
                if tail is not None and pend_tail[0] is not None:
                    tail(pend_tail[0])
                    pend_tail[0] = None

            def ffn(l, prenormed=False, tail=None):
                if not prenormed:
                    rmsnorm(2 + l)
                for fg in range(4):
                    for blk in range(2):
                        s_ = self.use_block(("w1", l, fg, blk))
                        for mc in range(4):
                            fc = blk * 4 + mc
                            for n in range(NT):
                                tsl = slice(n * TT, (n + 1) * TT)
                                b = self.next_bank()
                                for k in range(KC):
                                    add("pe", lambda h, b=b, s_=s_, k=k, mc=mc, tsl=tsl: h.matmul(
                                        ps[:, b, :], self.wring[s_][:, k, mc * 128:(mc + 1) * 128], hn[:, k, tsl],
                                        start=(k == 0), stop=(k == KC - 1)),
                                        reads=[self.wbuf[s_], hnb[k][n]], writes=[pbuf[b]])
                                ri_ = self.rot("rl", 3)
                                add("act", lambda h, b=b, ri_=ri_: h.activation(rl[ri_][:], ps[:, b, :], AF.Relu),
                                    reads=[pbuf[b]], writes=[rlb[ri_]])
                                add("pool", lambda h, fc=fc, tsl=tsl, ri_=ri_: h.tensor_tensor(
                                    u[:, fc, tsl], rl[ri_][:], rl[ri_][:], ALU.mult),
                                    reads=[rlb[ri_]], writes=[ub[fc][n]])
                    proj_add([("w2", l, fg, 0), ("w2", l, fg, 1)], u, ub, tail=(tail if fg == 3 else None))

            def final_store(only=None):
                if not self.fin_ready:
                    self.fin_ready = True
                    lastq = [sc.q[e_][-1] for e_ in ("pe", "act", "dve", "pool") if sc.q[e_]]
                    ent = add("sp", lambda h: h.dma_start(out=grow, in_=grow_d), writes=[growb], dma="grow")
                    ent.deps = ent.deps + lastq
                for n in (range(NT) if only is None else [only]):
                    for tq in range(4):
                        oi = self.rot("ot", 2)
                        t0 = n * TT + tq * 128
                        si = self.rot("fss", 2)
                        banks = []
                        for half in range(2):
                            pb = self.next_bank()
                            banks.append(pb)
                            for j in range(4):
                                k = half * 4 + j
                                add("pe", lambda h, pb=pb, j=j, k=k, t0=t0: h.transpose(
                                    ps[:, pb, j * 128:(j + 1) * 128], hT[:, k, t0:t0 + 128], ident[:]),
                                    reads=[hTb[k][n], identb], writes=[pbuf[pb]])
                            ji = self.rot("fjunk", 3)
                            add("act", lambda h, pb=pb, si=si, half=half, ji=ji: h.activation(
                                fjunk[ji][:], ps[:, pb, :], AF.Square, accum_out=fss[si][:, half:half + 1]),
                                reads=[pbuf[pb], fjunkb[ji]], writes=[fssb[si], fjunkb[ji]])
                        add("dve", lambda h, si=si: h.tensor_tensor(fss[si][:, 2:3], fss[si][:, 0:1], fss[si][:, 1:2], ALU.add),
                            reads=[fssb[si]], writes=[fssb[si]])
                        add("act", lambda h, si=si: h.activation(fss[si][:, 3:4], fss[si][:, 2:3], AF.Sqrt,
                                                                 bias=eps_sb[:, 0:1], scale=1.0 / 1024.0),
                            reads=[fssb[si], epsb], writes=[fssb[si]])
                        add("dve", lambda h, si=si: h.reciprocal(fss[si][:, 3:4], fss[si][:, 3:4]), reads=[fssb[si]], writes=[fssb[si]])
                        for half in range(2):
                            pb = banks[half]
                            add("dve", lambda h, pb=pb, oi=oi, half=half, si=si: h.scalar_tensor_tensor(
                                ot[oi][:, half * 512:(half + 1) * 512], ps[:, pb, :], fss[si][:, 3:4],
                                grow[:, half * 512:(half + 1) * 512], ALU.mult, ALU.mult),
                                reads=[pbuf[pb], fssb[si], growb], writes=[otb[oi]])
                        add("sp", lambda h, oi=oi, t0=t0: h.dma_start(out=out[t0:t0 + 128, :], in_=ot[oi]),
                            reads=[otb[oi]], writes=[], dma="xt%d" % oi)

            def l0_mixer():
                convk = small[:, 0:124].rearrange("p (c j) -> p c j", c=4)
                convb = small[:, 124:128]
                lng = small[:, 128:132]
                lnb = small[:, 132:136]
                def claim(lo_el, hi_el):
                    ks = range(lo_el // S, (hi_el - 1) // S + 1)
                    return [hTb[k][n] for k in ks for n in range(NT)]
                ar.reset()
                rope = ar.get([2, S], F32)
                ropeb = Buf("rope")
                add("sp", lambda h: h.dma_start(out=rope, in_=rope_d), writes=[ropeb] + claim(0, ar.off), dma="rope")
                base_off = ar.off
                off0 = ar.off
                a_pad = ar.get([4, 30 + S], BF16)
                apclaim = claim(off0, ar.off)
                off1 = ar.off
                apb = [[Buf("ap%d_%d" % (c, n)) for n in range(NT)] for c in range(4)]
                apz = Buf("apz")
                diag = ar.get([4, 31, 128], BF16)
                diagb = Buf("diag")
                dgclaim = claim(off1, ar.off)
                ar2.reset()
                cv = [ar2.get([TT], F32) for _ in range(4)]
                cvb = [Buf("cv%d" % i) for i in range(4)]
                ybf = [ar2.get([TT], BF16) for _ in range(4)]
                ybfb = [Buf("ybf%d" % i) for i in range(4)]
                ysq = [ar2.get([TT], BF16) for _ in range(4)]
                ysqb = [Buf("ysq%d" % i) for i in range(4)]
                sig = [ar2.get([TT], F32) for _ in range(2)]
                sigb = [Buf("sig%d" % i) for i in range(2)]
                m2 = ar2.get([TT], F32)
                m2b = Buf("m2")
                lrs = ar2.get([TT], F32)
                lrsb = Buf("lrs")
                tn = [ar2.get([TT], F32) for _ in range(2)]
                tnb = [Buf("tn%d" % i) for i in range(2)]
                add("dve", lambda h: h.memset(a_pad[:, :, 0:30], 0.0), writes=[apz] + apclaim, strict=True)
                doA = 'a' in cfg.get('l0p', 'ab')
                ckb = ar2.get([4, 31], BF16)
                ckbb = Buf("ckb")
                if doA:
                    s_lin = self.use_block(("a", 0))
                    s_gate = self.use_block(("a", 1), False)
                add("dve", lambda h: h.tensor_copy(ckb, convk), reads=[smallb], writes=[ckbb])
                def build_diag(c):
                    add("dve", lambda h, c=c: h.tensor_tensor(
                        diag[:, c, :, :], identh.unsqueeze(1).broadcast_to([128, 31, 128]),
                        ckb[:, c, :].unsqueeze(2).broadcast_to([128, 31, 128]), ALU.mult),
                        reads=[cstb, ckbb], writes=[Buf("dg")] + dgclaim, strict=True)
                    diagb.w.append(sc.q["dve"][-1])
                for c in range(4 if doA else 0):
                    if c > 0:
                        build_diag(c - 1)
                    for n in range(NT):
                        tsl = slice(n * TT, (n + 1) * TT)
                        b1 = self.next_bank()
                        b2 = self.next_bank()
                        for (bb, s_) in ((b1, s_lin), (b2, s_gate)):
                            for k in range(KC):
                                add("pe", lambda h, bb=bb, s_=s_, k=k, c=c, tsl=tsl: h.matmul(
                                    ps[:, bb, :], self.wring[s_][:, k, c * 128:(c + 1) * 128], hn[:, k, tsl],
                                    start=(k == 0), stop=(k == KC - 1)),
                                    reads=[self.wbuf[s_], hnb[k][n]], writes=[pbuf[bb]])
                        si = self.rot("sig", 2)
                        add("act", lambda h, b2=b2, si=si: h.activation(sig[si], ps[:, b2, :], AF.Sigmoid),
                            reads=[pbuf[b2]], writes=[sigb[si]])
                        add("dve", lambda h, b1=b1, si=si, c=c, n=n: h.tensor_tensor(
                            a_pad[:, c, 30 + n * TT:30 + (n + 1) * TT], ps[:, b1, :], sig[si], ALU.mult),
                            reads=[pbuf[b1], sigb[si]], writes=[apb[c][n]])
                if doA:
                    build_diag(3)
                for n in range(NT if doA else 0):
                    tsl = slice(n * TT, (n + 1) * TT)
                    for c in range(4):
                        if n == 0:
                            pass
                        b = self.next_bank()
                        rd = [apb[c][n], apz, diagb] + ([apb[c][n - 1]] if n > 0 else [])
                        for j in range(31):
                            add("pe", lambda h, b=b, j=j, c=c, n=n: h.matmul(
                                ps[:, b, :], diag[:, c, j, :], a_pad[:, c, n * TT + j:n * TT + j + TT],
                                start=(j == 0), stop=(j == 30)),
                                reads=rd, writes=[pbuf[b]])
                        add("act", lambda h, b=b, c=c: h.activation(cv[c], ps[:, b, :], AF.Identity, bias=convb[:, c:c + 1]),
                            reads=[pbuf[b], smallb], writes=[cvb[c]])
                        add("pool", lambda h, c=c: h.tensor_copy(ybf[c], cv[c]), reads=[cvb[c]], writes=[ybfb[c]])
                        add("act", lambda h, c=c: h.activation(ysq[c], cv[c], AF.Square), reads=[cvb[c]], writes=[ysqb[c]])
                    bm = self.next_bank()
                    bq = self.next_bank()
                    for c in range(4):
                        add("pe", lambda h, bm=bm, c=c: h.matmul(ps[:, bm, :], ones5[:], ybf[c], start=(c == 0), stop=(c == 3)),
                            reads=[ybfb[c], onesb], writes=[pbuf[bm]])
                    for c in range(4):
                        add("pe", lambda h, bq=bq, c=c: h.matmul(ps[:, bq, :], ones5[:], ysq[c], start=(c == 0), stop=(c == 3)),
                            reads=[ysqb[c], onesb], writes=[pbuf[bq]])
                    add("act", lambda h, bm=bm: h.activation(m2, ps[:, bm, :], AF.Square), reads=[pbuf[bm]], writes=[m2b])
                    add("dve", lambda h, bq=bq: h.tensor_tensor(lrs, ps[:, bq, :], m2, ALU.subtract),
                        reads=[pbuf[bq], m2b], writes=[lrsb])
                    add("act", lambda h: h.activation(lrs, lrs, AF.Sqrt, bias=eps_sb[:, 0:1]), reads=[lrsb, epsb], writes=[lrsb])
                    add("dve", lambda h: h.reciprocal(lrs, lrs), reads=[lrsb], writes=[lrsb])
                    for c in range(4):
                        ti = self.rot("tn", 2)
                        add("dve", lambda h, c=c, bm=bm, ti=ti: h.tensor_tensor(tn[ti], cv[c], ps[:, bm, :], ALU.subtract),
                            reads=[cvb[c], pbuf[bm]], writes=[tnb[ti]])
                        add("pool", lambda h, ti=ti: h.tensor_tensor(tn[ti], tn[ti], lrs, ALU.mult),
                            reads=[tnb[ti], lrsb], writes=[tnb[ti]])
                        add("act", lambda h, c=c, ti=ti, tsl=tsl: h.activation(
                            mixcat[:, c, tsl], tn[ti], AF.Silu, bias=lnb[:, c:c + 1], scale=lng[:, c:c + 1]),
                            reads=[tnb[ti], smallb], writes=[mcb[c][n]])

                self.full_barrier()
                ar.reset(base_off)
                ar2.reset()
                self.pools = {"S": [0, 1], "O": [3, 4], "DEN": [5], "gen": [2, 6, 7], "proj": [2, 6, 7, 0, 1]}
                self.bank_rr = {}
                acc_o = ar.get([S], F32)
                acc_ob = Buf("acc_o")
                den = ar.get([S], F32)
                denb = Buf("den")
                rden = den
                rdenb = denb
                qT = [ar.get([S], BF16) for _ in range(2)]
                kT = [ar.get([S], BF16) for _ in range(2)]
                qTb = [[Buf("qT%d_%d" % (i, n)) for n in range(NT)] for i in range(2)]
                kTb = [[Buf("kT%d_%d" % (i, n)) for n in range(NT)] for i in range(2)]
                vT = ar.get([S], BF16)
                vTb = [Buf("vT%d" % n) for n in range(NT)]
                vx = [ar.get([16, 192], BF16) for _ in range(2)]
                vxb = [[Buf("vx%d_%d" % (i, t)) for t in range(4)] for i in range(2)]
                vxz = [Buf("vxz%d" % i) for i in range(2)]
                wq = [ar2.get([8, 3, 128], BF16) for _ in range(2)]
                wqb = [[Buf("wq%d_%d" % (i, j)) for j in range(3)] for i in range(2)]
                zb = [ar2.get([TT], BF16) for _ in range(3)]
                zbb = [Buf("zb%d" % i) for i in range(3)]
                t1 = [ar2.get([TT], F32) for _ in range(2)]
                t1b = [Buf("t1%d" % i) for i in range(2)]
                t2 = [ar2.get([TT], F32) for _ in range(2)]
                t2b = [Buf("t2%d" % i) for i in range(2)]
                rcp = [ar2.get([TT], F32) for _ in range(2)]
                rcpb = [Buf("rcp%d" % i) for i in range(2)]
                pt = [ar2.get([TT], BF16) for _ in range(3)]
                ptb = [Buf("pt%d" % i) for i in range(3)]
                pthb = [[Buf("pth%d_%d" % (i, hh)) for hh in range(2)] for i in range(3)]
                for i in range(2):
                    add("dve", lambda h, i=i: h.memset(vx[i][:, :, 64:128], 0.0), writes=[vxz[i]])

                units = [(p, g) for p in range(4) for g in range(3)]

                def load_unit_w(ui):
                    p, g = units[ui]
                    wi = ui % 2
                    hd0 = g * 8 + 2 * p
                    for j, base in enumerate((1024, 2560, 4096)):
                        c0 = base + hd0 * HD
                        src = w_in0[:, c0:c0 + 128].rearrange("(kc q) c -> q kc c", q=128)
                        add("pool", lambda h, wi=wi, j=j, src=src: h.dma_start(out=wq[wi][:, :, j, :], in_=src),
                            writes=[wqb[wi][j]], dma="wq%d_%d" % (wi, j))

                if 'b' not in cfg.get('l0p', 'ab'):
                    units = units[:cfg.get('nunits', 0)]

                def unit_ctx(ui):
                    p, g = units[ui]
                    wi = ui % 2
                    d, nb = DIL[g], NBLK[g]
                    def tok(T, cnt=128):
                        r, jj = T // nb, T % nb
                        st0 = r + d * 128 * jj
                        return slice(st0, st0 + d * (cnt - 1) + 1, d)
                    return p, g, wi, d, nb, tok

                def proj_items(ui, pool):
                    p, g, wi, d, nb, tok = unit_ctx(ui)
                    tiles = [(j, n) for j in range(2) for n in range(NT)]
                    state = {}
                    def proj_stage(j, n):
                        tsl = slice(n * TT, (n + 1) * TT)
                        b = self.next_bank(pool)
                        for k in range(KC):
                            add("pe", lambda h, b=b, k=k, j=j, tsl=tsl: h.matmul(
                                ps[:, b, :], wq[wi][:, k, j, :], hn[:, k, tsl], start=(k == 0), stop=(k == KC - 1)),
                                reads=[wqb[wi][j], hnb[k][n]], writes=[pbuf[b]])
                        zi = self.rot("zb", 3)
                        add("act", lambda h, b=b, zi=zi: h.copy(zb[zi], ps[:, b, :]), reads=[pbuf[b]], writes=[zbb[zi]])
                        ti = self.rot("t12", 2)
                        add("dve", lambda h, b=b, ti=ti, tsl=tsl: h.tensor_tensor(t1[ti], ps[:, b, :], rope[:, 0, tsl], ALU.mult),
                            reads=[pbuf[b], ropeb], writes=[t1b[ti]])
                        return b, zi, ti
                    def swap_stage(j, n, b, zi, ti):
                        tsl = slice(n * TT, (n + 1) * TT)
                        dstT, dstb = (qT[wi], qTb[wi]) if j == 0 else (kT[wi], kTb[wi])
                        b2 = self.next_bank(pool)
                        add("pe", lambda h, b2=b2, zi=zi: h.matmul(ps[:, b2, :], pswap, zb[zi], start=True, stop=True),
                            reads=[zbb[zi], cstb], writes=[pbuf[b2]])
                        add("dve", lambda h, b2=b2, ti=ti, tsl=tsl: h.tensor_tensor(t2[ti], ps[:, b2, :], rope[:, 1, tsl], ALU.mult),
                            reads=[pbuf[b2], ropeb], writes=[t2b[ti]])
                        add("pool", lambda h, ti=ti, dstT=dstT, tsl=tsl: h.tensor_tensor(dstT[:, tsl], t1[ti], t2[ti], ALU.add),
                            reads=[t1b[ti], t2b[ti]], writes=[dstb[n]])
                    def qk_item(i):
                        def fn():
                            if i < len(tiles):
                                cur = tiles[i] + proj_stage(*tiles[i])
                            if i > 0:
                                swap_stage(*state["prev"])
                            if i < len(tiles):
                                state["prev"] = cur
                        return fn
                    def vT_item(n):
                        def fn():
                            tsl = slice(n * TT, (n + 1) * TT)
                            b = self.next_bank(pool)
                            for k in range(KC):
                                add("pe", lambda h, b=b, k=k, tsl=tsl: h.matmul(
                                    ps[:, b, :], wq[wi][:, k, 2, :], hn[:, k, tsl], start=(k == 0), stop=(k == KC - 1)),
                                    reads=[wqb[wi][2], hnb[k][n]], writes=[pbuf[b]])
                            add("act", lambda h, b=b, tsl=tsl: h.copy(vT[:, tsl], ps[:, b, :]), reads=[pbuf[b]], writes=[vTb[n]])
                        return fn
                    def v_item(T4):
                        def fn():
                            b = self.next_bank(pool)
                            psb = ps[:, b, :].bitcast(BF16)
                            for tt_ in range(4):
                                T = T4 * 4 + tt_
                                tk = tok(T)
                                add("pe", lambda h, psb=psb, tk=tk, tt_=tt_: h.transpose(
                                    psb[:, tt_ * 128:(tt_ + 1) * 128], vT[:, tk], identh),
                                    reads=vTb + [cstb], writes=[pbuf[b]])
                            dstv = vx[wi][:, T4 * 4:T4 * 4 + 4, :].rearrange("p t (x c) -> p t x c", c=64)[:, :, 0:3:2, :]
                            srcv = psb[:, 0:512].rearrange("p (t x c) -> p t x c", t=4, x=2)
                            add("act", lambda h, dstv=dstv, srcv=srcv: h.copy(dstv, srcv), reads=[pbuf[b]], writes=[vxb[wi][T4]])
                        return fn
                    return ([qk_item(i) for i in range(len(tiles) + 1)] + [vT_item(n) for n in range(NT)]
                            + [v_item(T4) for T4 in range(4)])

                def make_norm(p):
                    def fn(ns=range(NT)):
                        for n in ns:
                            tsl = slice(n * TT, (n + 1) * TT)
                            ti = self.rot("rcp", 2)
                            if cfg.get("lnexp", True):
                                add("act", lambda h, ti=ti, tsl=tsl: h.activation(rcp[ti], den[:, tsl], AF.Ln),
                                    reads=[denb], writes=[rcpb[ti]])
                                add("act", lambda h, ti=ti: h.activation(rcp[ti], rcp[ti], AF.Exp, scale=-1.0),
                                    reads=[rcpb[ti]], writes=[rcpb[ti]])
                            else:
                                add("dve", lambda h, ti=ti, tsl=tsl: h.reciprocal(rcp[ti], den[:, tsl]),
                                    reads=[denb], writes=[rcpb[ti]])
                            add("dve", lambda h, ti=ti, tsl=tsl, p=p: h.tensor_tensor(
                                mixcat[:, 4 + p, tsl], acc_o[:, tsl], rcp[ti], ALU.mult),
                                reads=[rcpb[ti], acc_ob], writes=[mcb[4 + p][n]])
                    return fn

                def attention(ui, extra, pending):
                    p, g, wi, d, nb, tok = unit_ctx(ui)
                    def evac(Bk, ob, db):
                        if g == 0:
                            dsl = slice(Bk * 512, (Bk + 1) * 512)
                            ov, dv = acc_o[:, dsl], den[:, dsl]
                            pso, psd = ps[:, ob, :], ps[:, db, :]
                        elif g == 1:
                            dsl = slice(Bk, Bk + 4 * 511 + 1, 4)
                            ov, dv = acc_o[:, dsl], den[:, dsl]
                            pso, psd = ps[:, ob, :], ps[:, db, :]
                        else:
                            ov = acc_o.rearrange("p (i r) -> p r i", r=16)[:, 4 * Bk:4 * Bk + 4, :]
                            dv = den.rearrange("p (i r) -> p r i", r=16)[:, 4 * Bk:4 * Bk + 4, :]
                            pso = ps[:, ob, :].rearrange("p (r i) -> p r i", r=4)
                            psd = ps[:, db, :].rearrange("p (r i) -> p r i", r=4)
                        if g == 0:
                            add("act", lambda h, dv=dv, psd=psd: h.copy(dv, psd), reads=[pbuf[db]], writes=[denb])
                            add("act", lambda h, ov=ov, pso=pso: h.copy(ov, pso), reads=[pbuf[ob]], writes=[acc_ob])
                        else:
                            add("dve", lambda h, dv=dv, psd=psd: h.tensor_tensor(dv, psd, dv, ALU.add),
                                reads=[pbuf[db], denb], writes=[denb])
                            add("dve", lambda h, ov=ov, pso=pso: h.tensor_tensor(ov, pso, ov, ALU.add),
                                reads=[pbuf[ob], acc_ob], writes=[acc_ob])

                    def qk_stage(T):
                        r, jj = T // nb, T % nb
                        nqb = 2 if jj + 1 < nb else 1
                        nq = 128 * nqb
                        ktk = tok(T)
                        qtk = tok(T, nq)
                        qn = sorted(set(range(qtk.start // TT, (qtk.stop - 1) // TT + 1)))
                        kn = sorted(set(range(ktk.start // TT, (ktk.stop - 1) // TT + 1)))
                        pi = self.rot("pt", 3)
                        sbs = [self.next_bank("S"), self.next_bank("S")]
                        for hh in range(2):
                            sb_ = sbs[hh]
                            add("pe", lambda h, sb_=sb_, hh=hh, ktk=ktk, qtk=qtk, nq=nq: h.matmul(
                                ps[:, sb_, 0:nq], kT[wi][hh * 64:(hh + 1) * 64, ktk],
                                qT[wi][hh * 64:(hh + 1) * 64, qtk], start=True, stop=True),
                                reads=[kTb[wi][n_] for n_ in kn] + [qTb[wi][n_] for n_ in qn], writes=[pbuf[sb_]])
                        for hh in range(2):
                            sb_ = sbs[hh]
                            add("act", lambda h, sb_=sb_, pi=pi, nq=nq, hh=hh: h.activation(
                                pt[pi][:, hh * nq:(hh + 1) * nq], ps[:, sb_, 0:nq], AF.Exp, scale=0.125),
                                reads=[pbuf[sb_]], writes=[pthb[pi][hh]])
                        pv = pt[pi][:, 0:2 * nq].rearrange("p (h q) -> p h q", h=2)
                        add("dve", lambda h, pv=pv, nq=nq: h.tensor_tensor(pv, pv, mask2[:, :, 0:nq], ALU.mult),
                            reads=[cstb], writes=[pthb[pi][0], pthb[pi][1]])
                        return pi, nq, nqb

                    obdb = [None, None]
                    def pv_stage(T, pi, nq, nqb):
                        r, jj = T // nb, T % nb
                        if nqb == 2 and (T % 4) != 3 and cfg.get("pv256", True):
                            slot = T % 4
                            if slot == 0 and jj == 0:
                                obdb[0] = self.next_bank("O")
                                obdb[1] = self.next_bank("DEN")
                            ob, db = obdb
                            for hh in range(2):
                                stf = (slot == 0 and jj == 0 and hh == 0)
                                add("pe", lambda h, ob=ob, slot=slot, hh=hh, T=T, pi=pi, nq=nq, stf=stf: h.matmul(
                                    ps[:, ob, slot * 128:slot * 128 + 256], vx[wi][:, T, hh * 64:hh * 64 + 128],
                                    pt[pi][:, hh * nq:hh * nq + 256],
                                    start=stf, stop=False, skip_group_check=True),
                                    reads=[vxb[wi][T // 4], vxz[wi], pthb[pi][hh]], writes=[pbuf[ob]])
                                add("pe", lambda h, db=db, slot=slot, hh=hh, pi=pi, nq=nq, stf=stf: h.matmul(
                                    ps[:, db, slot * 128:slot * 128 + 256], OZ[:, hh * 64:hh * 64 + 128],
                                    pt[pi][:, hh * nq:hh * nq + 256],
                                    start=stf, stop=False, skip_group_check=True),
                                    reads=[pthb[pi][hh], cstb], writes=[pbuf[db]])
                            return
                        for qb in range(nqb):
                            Bq = T + qb
                            slot = Bq % 4
                            fresh_block = (qb == 1) or (jj == 0)
                            if slot == 0 and fresh_block:
                                obdb[0] = self.next_bank("O")
                                obdb[1] = self.next_bank("DEN")
                            ob, db = obdb
                            for hh in range(2):
                                stf = (slot == 0 and fresh_block and hh == 0)
                                add("pe", lambda h, ob=ob, slot=slot, hh=hh, T=T, pi=pi, nq=nq, qb=qb, stf=stf: h.matmul(
                                    ps[:, ob, slot * 128:(slot + 1) * 128], vx[wi][:, T, hh * 64:hh * 64 + 128],
                                    pt[pi][:, hh * nq + qb * 128:hh * nq + qb * 128 + 128],
                                    start=stf, stop=False, skip_group_check=True),
                                    reads=[vxb[wi][T // 4], vxz[wi], pthb[pi][hh]], writes=[pbuf[ob]])
                                add("pe", lambda h, db=db, slot=slot, hh=hh, pi=pi, nq=nq, qb=qb, stf=stf: h.matmul(
                                    ps[:, db, slot * 128:(slot + 1) * 128], OZ[:, hh * 64:hh * 64 + 128],
                                    pt[pi][:, hh * nq + qb * 128:hh * nq + qb * 128 + 128],
                                    start=stf, stop=False, skip_group_check=True),
                                    reads=[pthb[pi][hh], cstb], writes=[pbuf[db]])
                            if qb == 0 and slot == 3:
                                evac(Bq // 4, ob, db)

                    extra = list(extra)
                    if cfg.get('nointer'):
                        while extra:
                            extra.pop(0)()
                    prev = None
                    for T in range(16):
                        cur = qk_stage(T)
                        if prev is not None:
                            pv_stage(T - 1, *prev)
                        prev = cur
                        if T % 4 == 1 and pending is not None:
                            pending([T // 4])
                        if extra:
                            extra.pop(0)()
                    pv_stage(15, *prev)
                    while extra:
                        extra.pop(0)()

                if units:
                    load_unit_w(0)
                    if len(units) > 1:
                        load_unit_w(1)
                    for it in proj_items(0, "proj"):
                        it()
                    pending = None
                    for ui, (p, g) in enumerate(units):
                        if ui == len(units) - 1 and len(units) > 2 and cfg.get("xpre", True):
                            lastq = [sc.q[e_][-1] for e_ in ("pe", "act", "dve", "pool") if sc.q[e_]]
                            for t in range(5):
                                ent = add("sp", lambda h, t=t: h.dma_start(out=xt[t], in_=x[t * 128:(t + 1) * 128, :]),
                                          writes=[xtb[t]], dma="xt%d" % t)
                                ent.deps = ent.deps + lastq
                            self.x_prefetched = 5
                        if ui + 2 < len(units):
                            load_unit_w(ui + 2)
                        extra = proj_items(ui + 1, "gen") if ui + 1 < len(units) else []
                        attention(ui, extra, pending)
                        pending = make_norm(p) if g == 2 else None
                    if pending is not None:
                        pending()
                self.pools = {"gen": list(range(8))}
                self.bank_rr = {}
                self.full_barrier()
                load_x_to_hT(prefetched=getattr(self, "x_prefetched", 0))
                proj_add([("wo0", 0), ("wo0", 1)], mixcat, mcb, tail=self.tails.get("l0"))

            def l1_mixer(prenormed=False):
                if not prenormed:
                    rmsnorm(1)
                ar2.reset()
                mark = ar2.off
                ppad = [ar2.get([2 + S], BF16) for _ in range(2)]
                ppb = [[Buf("pp%d_%d" % (i, n)) for n in range(NT)] for i in range(2)]
                ppz = [Buf("ppz%d" % i) for i in range(2)]
                gct = [ar2.get([TT], F32) for _ in range(2)]
                gctb = [Buf("gct%d" % i) for i in range(2)]
                gbt = [ar2.get([TT], F32) for _ in range(3)]
                gbtb = [Buf("gbt%d" % i) for i in range(3)]
                cend = ar2.off
                ar2.reset(mark)
                usb = ar2.get([4, TT], F32)
                usbb = [Buf("usb%d" % c) for c in range(4)]
                ga = [ar2.get([TT], F32) for _ in range(2)]
                gab = [Buf("ga%d" % i) for i in range(2)]
                vsb = [ar2.get([512], F32) for _ in range(4)]
                vsbb = [Buf("vsb%d" % i) for i in range(4)]
                vln = [ar2.get([512], BF16) for _ in range(4)]
                vlnb = [Buf("vln%d" % i) for i in range(4)]
                stt = ar2.get([4, 8], F32)
                sttb = [Buf("stt%d" % i) for i in range(4)]
                dtm = [ar2.get([4, 128], F32) for _ in range(2)]
                dtmb = [Buf("dtm%d" % i) for i in range(2)]
                for i in range(2):
                    add("pool", lambda h, i=i: h.memset(ppad[i][:, 0:2], 0.0), writes=[ppz[i]])

                def gelu(eng_out_fn, b, outv, reads_extra, wbufs):
                    gi = self.rot("ga", 2)
                    add("act", lambda h, b=b, gi=gi: h.activation(ga[gi], ps[:, b, :], AF.Square), reads=[pbuf[b]], writes=[gab[gi]])
                    add("pool", lambda h, gi=gi: h.tensor_scalar(ga[gi], ga[gi], 0.044715, 1.0, ALU.mult, ALU.add),
                        reads=[gab[gi]], writes=[gab[gi]])
                    add("dve", lambda h, b=b, gi=gi: h.tensor_tensor(ga[gi], ga[gi], ps[:, b, :], ALU.mult),
                        reads=[gab[gi], pbuf[b]], writes=[gab[gi]])
                    add("act", lambda h, gi=gi: h.activation(ga[gi], ga[gi], AF.Sigmoid, scale=GELU_K), reads=[gab[gi]], writes=[gab[gi]])
                    add("dve", lambda h, b=b, gi=gi, outv=outv: h.tensor_tensor(outv, ga[gi], ps[:, b, :], ALU.mult),
                        reads=[gab[gi], pbuf[b]], writes=wbufs)

                s_gb = self.use_block(("i1", 0))
                s_gc = self.use_block(("i1", 1), False)
                s_xs = self.use_block(("i1", 2), False)
                pend_c = [None]
                for c in range(4 if 'c' in cfg.get('l1p', 'cd') else 0):
                    pi_ = c % 2
                    for n in range(NT):
                        tsl = slice(n * TT, (n + 1) * TT)
                        bgb, bgc, bxs = self.next_bank(), self.next_bank(), self.next_bank()
                        for (bb, s_) in ((bgc, s_gc), (bxs, s_xs), (bgb, s_gb)):
                            for k in range(KC):
                                add("pe", lambda h, bb=bb, s_=s_, k=k, c=c, tsl=tsl: h.matmul(
                                    ps[:, bb, :], self.wring[s_][:, k, c * 128:(c + 1) * 128], hn[:, k, tsl],
                                    start=(k == 0), stop=(k == KC - 1)),
                                    reads=[self.wbuf[s_], hnb[k][n]], writes=[pbuf[bb]])
                        gi = self.rot("gct", 2)
                        add("act", lambda h, bgc=bgc, gi=gi: h.copy(gct[gi], ps[:, bgc, :]), reads=[pbuf[bgc]], writes=[gctb[gi]])
                        add("dve", lambda h, bxs=bxs, gi=gi, pi_=pi_, n=n: h.tensor_tensor(
                            ppad[pi_][:, 2 + n * TT:2 + (n + 1) * TT], gct[gi], ps[:, bxs, :], ALU.mult),
                            reads=[gctb[gi], pbuf[bxs]], writes=[ppb[pi_][n]])
                        bi = self.rot("gbt", 3)
                        add("act", lambda h, bgb=bgb, bi=bi: h.copy(gbt[bi], ps[:, bgb, :]), reads=[pbuf[bgb]], writes=[gbtb[bi]])
                        if pend_c[0] is not None:
                            pend_c[0]()
                        def conv_stage(c=c, n=n, pi_=pi_, bi=bi, tsl=tsl):
                            bc = self.next_bank()
                            rd = [ppb[pi_][n], ppz[pi_], diag3b] + ([ppb[pi_][n - 1]] if n > 0 else [])
                            for j in range(3):
                                add("pe", lambda h, bc=bc, j=j: h.matmul(
                                    ps[:, bc, :], diag3[:, c, j, :], ppad[pi_][:, n * TT + j:n * TT + j + TT],
                                    start=(j == 0), stop=(j == 2)),
                                    reads=rd, writes=[pbuf[bc]])
                            add("dve", lambda h, bc=bc: h.tensor_tensor(
                                mixcat[:, c, tsl], gbt[bi], ps[:, bc, :], ALU.mult),
                                reads=[gbtb[bi], pbuf[bc]], writes=[mcb[c][n]])
                        pend_c[0] = conv_stage
                if pend_c[0] is not None:
                    pend_c[0]()
                self.full_barrier()
                s_u = self.use_block(("i1", 3))
                s_v = self.use_block(("i1", 4), False)
                for n in range(NT if 'd' in cfg.get('l1p', 'cd') else 0):
                    tsl = slice(n * TT, (n + 1) * TT)
                    for tq in range(4):
                        t0 = n * TT + tq * 128
                        b = self.next_bank()
                        for k in range(KC):
                            add("pe", lambda h, b=b, k=k, t0=t0: h.matmul(
                                ps[:, b, :], hn[:, k, t0:t0 + 128], self.wring[s_v][:, k, :],
                                start=(k == 0), stop=(k == KC - 1)),
                                reads=[self.wbuf[s_v], hnb[k][n]], writes=[pbuf[b]])
                        add("act", lambda h, b=b, tq=tq: h.activation(vsb[tq], ps[:, b, :], AF.Gelu_apprx_tanh),
                            reads=[pbuf[b]], writes=[vsbb[tq]])
                        add("dve", lambda h, tq=tq: h.bn_stats(stt[:, tq, 0:6], vsb[tq]), reads=[vsbb[tq]], writes=[sttb[tq]])
                        add("dve", lambda h, tq=tq: h.bn_aggr(stt[:, tq, 6:8], stt[:, tq, 0:6]), reads=[sttb[tq]], writes=[sttb[tq]])
                    for c in range(4):
                        b = self.next_bank()
                        for k in range(KC):
                            add("pe", lambda h, b=b, k=k, c=c, tsl=tsl: h.matmul(
                                ps[:, b, :], self.wring[s_u][:, k, c * 128:(c + 1) * 128], hn[:, k, tsl],
                                start=(k == 0), stop=(k == KC - 1)),
                                reads=[self.wbuf[s_u], hnb[k][n]], writes=[pbuf[b]])
                        add("act", lambda h, b=b, c=c: h.activation(usb[:, c, :], ps[:, b, :], AF.Gelu_apprx_tanh),
                            reads=[pbuf[b]], writes=[usbb[c]])
                    add("act", lambda h: h.activation(stt[:, :, 7], stt[:, :, 7], AF.Sqrt, bias=eps_sb[:, 0:1]),
                        reads=sttb + [epsb], writes=sttb)
                    add("dve", lambda h: h.reciprocal(stt[:, :, 7], stt[:, :, 7]), reads=sttb, writes=sttb)
                    sp_banks = []
                    for tq in range(4):
                        vi = tq
                        add("dve", lambda h, tq=tq, vi=vi: h.tensor_scalar(vln[vi], vsb[tq], stt[:, tq, 6:7], stt[:, tq, 7:8], ALU.subtract, ALU.mult),
                            reads=[vsbb[tq], sttb[tq]], writes=[vlnb[vi]])
                        b2 = self.next_bank()
                        sp_banks.append(b2)
                        for g_ in range(4):
                            add("pe", lambda h, b2=b2, g_=g_, vi=vi: h.matmul(
                                ps[:, b2, g_ * 128:(g_ + 1) * 128], vln[vi][:, g_ * 128:(g_ + 1) * 128], wsT[:, g_, :],
                                start=True, stop=True, skip_group_check=True),
                                reads=[vlnb[vi], wsTb], writes=[pbuf[b2]])
                    for tq in range(4):
                        t0 = n * TT + tq * 128
                        b2 = sp_banks[tq]
                        di = self.rot("dtm", 2)
                        add("dve", lambda h, b2=b2, di=di: h.tensor_tensor(
                            dtm[di], ps[:, b2, :].rearrange("p (g t) -> p g t", g=4), Gt[:], ALU.mult),
                            reads=[pbuf[b2], gbtb_], writes=[dtmb[di]])
                        add("pool", lambda h, di=di: h.tensor_tensor(
                            dtm[di].rearrange("p g t -> p (g t)"), dtm[di].rearrange("p g t -> p (g t)"),
                            Bt[:].rearrange("p g t -> p (g t)"), ALU.add),
                            reads=[dtmb[di], gbtb_], writes=[dtmb[di]])
                        add("dve", lambda h, di=di, tq=tq, t0=t0: h.tensor_tensor(
                            mixcat[:, 4:8, t0:t0 + 128], dtm[di], usb[:, :, tq * 128:(tq + 1) * 128], ALU.mult),
                            reads=[dtmb[di]] + usbb, writes=[mcb[4 + g_][n] for g_ in range(4)])
                proj_add([("wo1", 0), ("wo1", 1)], mixcat, mcb, tail=self.tails.get("l1"))

            en = [cfg.get("l0mix", True), cfg.get("ffn0", True), cfg.get("l1mix", True), cfg.get("ffn1", True)]
            fuse = cfg.get("fuse_tails", True)
            self.tails = {}
            if fuse and en[0] and en[1]:
                self.tails["l0"] = lambda n: rmsnorm_tile(2, n)
            if fuse and en[2] and en[3]:
                self.tails["l1"] = lambda n: rmsnorm_tile(3, n)
            if en[0]:
                load_x_to_hT(after_n=lambda n: rmsnorm_tile(0, n))
                l0_mixer()
            else:
                load_x_to_hT()
            if en[1]:
                t0_ = (lambda n: rmsnorm_tile(1, n)) if (fuse and en[2]) else None
                ffn(0, prenormed=("l0" in self.tails), tail=t0_)
            if en[2]:
                l1_mixer(prenormed=(fuse and en[1]))
            fused_final = False
            if en[3]:
                fused_final = fuse
                ffn(1, prenormed=("l1" in self.tails), tail=((lambda n: final_store(only=n)) if fuse else None))
            if not fused_final:
                self.full_barrier()
                final_store()
            sc.emit(final_waits=["xt%d" % i for i in range(NXT)])
        return nc


def _consts():
    cst = np.zeros((128, CST_COLS), np.float32)
    cst[:, 0:128] = np.eye(128, dtype=np.float32)
    m = np.arange(128)
    sw = np.where((m % 64) < 32, m + 32, m - 32)
    cst[sw, 128 + m] = 1.0
    k = np.arange(128)[:, None]
    i = np.arange(128)[None, :]
    diag = (i >= k).astype(np.float32)
    nxt = (i <= k).astype(np.float32)
    m2 = np.concatenate([diag, nxt], axis=1)
    cst[:, 256:512] = m2
    cst[:, 512:768] = m2
    cst[:, 768] = 1.0
    cst[:, 771] = 1.0
    cst[:, 772:900] = (k <= i).astype(np.float32)
    cst[0, 900:964] = 1.0
    cst[1, 964:1028] = 1.0
    cst[:, 1028:1092] = 1.0
    cst[:, 1156:1220] = 1.0
    half = 32
    cos = sin = None
    try:
        import jax
        import jax.numpy as jnp
        with jax.default_device(jax.devices("cpu")[0]):
            inv_j = 10000.0 ** (-jnp.arange(half, dtype=jnp.float32) / half)
            ang_j = jnp.arange(S, dtype=jnp.float32)[:, None] * inv_j[None, :]
            cos = np.asarray(jnp.cos(ang_j), dtype=np.float32).T
            sin = np.asarray(jnp.sin(ang_j), dtype=np.float32).T
    except Exception:
        cos = sin = None
    if cos is None:
        inv = (np.float32(10000.0) ** (-np.arange(half, dtype=np.float32) / np.float32(half))).astype(np.float32)
        ang = (np.arange(S, dtype=np.float32)[:, None] * inv[None, :]).astype(np.float32)
        cos = np.cos(ang).astype(np.float32).T
        sin = np.sin(ang).astype(np.float32).T
    rope = np.zeros((128, 2, S), np.float32)
    for p in range(128):
        rope[p, 0] = cos[p % 32]
        rope[p, 1] = -sin[p % 32] if (p % 64) < 32 else sin[p % 32]
    return cst, rope


def prep_inputs(inputs):
    f = lambda a: np.ascontiguousarray(np.asarray(a, dtype=np.float32))
    g_all = np.stack([f(inputs["norm_mix_g"])[0], f(inputs["norm_mix_g"])[1],
                      f(inputs["norm_ffn_g"])[0], f(inputs["norm_ffn_g"])[1],
                      f(inputs["final_g"])], axis=0)
    gains = np.ascontiguousarray(g_all.reshape(5, KC, 128).transpose(2, 0, 1).reshape(128, 5 * KC))
    chunk4 = lambda v: f(v).reshape(4, 128).T
    convk_t = f(inputs["even_conv_k"])[0].reshape(31, 4, 128).transpose(2, 1, 0).reshape(128, 124)
    oddk_t = f(inputs["odd_conv_k"])[0].reshape(3, 4, 128).transpose(2, 1, 0).reshape(128, 12)
    small = np.concatenate([convk_t, chunk4(inputs["even_conv_b"][0]), chunk4(inputs["even_ln_g"][0]),
                            chunk4(inputs["even_ln_b"][0]), oddk_t, chunk4(inputs["odd_ln_g"][0]), chunk4(inputs["odd_ln_b"][0]),
                            np.zeros((128, 4), np.float32)], axis=1)
    wsT = f(inputs["odd_sg_w"])[0].transpose(2, 0, 1)
    sgb = np.broadcast_to(f(inputs["odd_sg_b"])[0][None], (128, 4, 128))
    cst, rope = _consts()
    shared = {
        "gains": gains, "ident_d": np.eye(128, dtype=np.float32), "cst_d": cst, "rope_d": rope,
        "small_d": np.ascontiguousarray(small), "sel_d": np.ascontiguousarray(cst[0:2, 900:1028]),
        "grow_d": np.ascontiguousarray(np.broadcast_to(f(inputs["final_g"])[None, :], (128, D))),
        "wsT_d": np.ascontiguousarray(wsT), "sgb_d": np.ascontiguousarray(sgb),
        "w_in0": f(inputs["even_w_in"][0]), "w_out0": f(inputs["even_w_out"][0]),
        "w_in1": f(inputs["odd_w_in"][0]), "w_out1": f(inputs["odd_w_out"][0]),
        "ffn_w1_0": f(inputs["ffn_w1"][0]), "ffn_w1_1": f(inputs["ffn_w1"][1]),
        "ffn_w2_0": f(inputs["ffn_w2"][0]), "ffn_w2_1": f(inputs["ffn_w2"][1]),
    }
    x = f(inputs["x"])
    maps = []
    for c in range(N_CORES):
        m = dict(shared)
        m["x"] = x[c]
        maps.append(m)
    return maps


_CFG = {}


def run(inputs, cfg=None, trace=False, cores=N_CORES):
    cfg = _CFG if cfg is None else cfg
    nc = Builder(cfg).build()
    maps = prep_inputs(inputs)[:cores]
    res = run_bass_kernel_spmd(nc, maps, core_ids=list(range(cores)), trace=trace)
    outs = np.stack([np.asarray(r["out"]) for r in res.results], axis=0)
    return outs, res


def kernel(**inputs):
    outs, _ = run(inputs)
    return outs.astype(np.float32)
```

```python
import numpy as np
import concourse.bass as bass
import concourse.mybir as mybir
from concourse.bass_utils import run_bass_kernel_spmd
from contextlib import ExitStack

F32 = mybir.dt.float32
BF16 = mybir.dt.bfloat16
ALU = mybir.AluOpType
AF = mybir.ActivationFunctionType

D = 1024
S = 2048
NT = 4
TT = 512
KC = 8
EPS = 1e-6
D_FF = 4096
EVEN_IN = 5632
ODD_IN = 2560
N_CORES = 8


class Buf:
    __slots__ = ("name", "w", "r", "excl")

    def __init__(self, name, excl=False):
        self.name = name
        self.w = []
        self.r = []
        self.excl = excl


class Ent:
    __slots__ = ("eng", "fn", "deps", "dma", "inc", "cum", "idx", "waits")

    def __init__(self, eng, fn, dma):
        self.eng = eng
        self.fn = fn
        self.deps = []
        self.dma = dma
        self.inc = False
        self.cum = 0
        self.idx = 0
        self.waits = []


class Sched:
    ENGS = ("pe", "act", "dve", "pool", "sp")

    def __init__(self, nc, stack):
        self.nc = nc
        self.stack = stack
        self.q = {e: [] for e in self.ENGS}
        self.esem = {e: stack.enter_context(nc.semaphore("s_" + e)) for e in self.ENGS}
        self.dsem = {}
        self.dcount = {}
        self.all_ents = []

    def _dsem(self, key):
        if key not in self.dsem:
            self.dsem[key] = self.stack.enter_context(self.nc.semaphore("d_" + key))
            self.dcount[key] = 0
        return self.dsem[key]

    def add(self, eng, fn, reads=(), writes=(), dma=None, strict=False):
        e = Ent(eng, fn, dma)
        deps = []
        for b in reads:
            deps.extend(b.w)
            if b.excl:
                deps.extend(b.r)
        for b in writes:
            deps.extend(b.w)
            deps.extend(b.r)
        seen = set()
        for d in deps:
            if id(d) in seen:
                continue
            seen.add(id(d))
            if d.dma is None and d.eng == eng and dma is None and not strict:
                if eng == "pe":
                    continue
                is_raw = any(d in b.w for b in reads)
                if not is_raw:
                    continue
            e.deps.append(d)
        for b in writes:
            b.w = [e]
            b.r = []
        for b in reads:
            if b.excl:
                b.w = [e]
                b.r = []
            else:
                b.r.append(e)
        e.idx = len(self.q[eng])
        self.q[eng].append(e)
        if dma is not None:
            self._dsem(dma)
            self.dcount[dma] += 16
            e.cum = self.dcount[dma]
        self.all_ents.append(e)
        return e

    def barrier(self, bufs):
        last = [self.q[e][-1] for e in self.ENGS if self.q[e]]
        for b in bufs:
            b.w = list(last)
            b.r = []

    def finalize(self):
        for e in self.all_ents:
            for d in e.deps:
                if d.dma is None:
                    d.inc = True
        for eng in self.ENGS:
            c = 0
            for e in self.q[eng]:
                if e.dma is None:
                    if e.inc:
                        c += 1
                    e.cum = c
        for eng in self.ENGS:
            seen = {}
            for e in self.q[eng]:
                need = {}
                for d in e.deps:
                    key = ("d", d.dma) if d.dma is not None else ("e", d.eng)
                    if d.cum > need.get(key, 0):
                        need[key] = d.cum
                for key, v in need.items():
                    if seen.get(key, 0) >= v:
                        continue
                    seen[key] = v
                    sem = self.dsem[key[1]] if key[0] == "d" else self.esem[key[1]]
                    e.waits.append((sem, v))

    def emit(self, final_waits):
        nc = self.nc
        self.finalize()

        def run(eng, h):
            for e in self.q[eng]:
                for sem, v in e.waits:
                    h.wait_ge(sem, v)
                ins = e.fn(h)
                if e.dma is not None:
                    ins.then_inc(self.dsem[e.dma], 16)
                elif e.inc:
                    ins.then_inc(self.esem[eng], 1)
            if eng == "sp":
                for key in final_waits:
                    h.wait_ge(self.dsem[key], self.dcount[key])

        with nc.Block() as block:
            @block.tensor
            def _(h):
                run("pe", h)

            @block.scalar
            def _(h):
                run("act", h)

            @block.vector
            def _(h):
                run("dve", h)

            @block.gpsimd
            def _(h):
                run("pool", h)

            @block.sync
            def _(h):
                run("sp", h)


HD = 64
DIL = (1, 4, 16)
NBLK = (16, 4, 1)
CST_COLS = 128 + 128 + 512 + 4 + 128 + 128 + 192
GELU_K = 1.5957691216057308


def _prod(xs):
    r = 1
    for v in xs:
        r *= v
    return r


class Arena:
    def __init__(self, ap32):
        self.ap = ap32
        self.n = ap32.shape[1]
        self.off = 0

    def reset(self, off=0):
        self.off = off

    def get(self, shape, dt):
        n = _prod(shape)
        nb = n * (4 if dt == F32 else 2)
        n32 = (nb + 31) // 32 * 8
        assert self.off + n32 <= self.n, ("arena overflow", self.off, n32, self.n)
        v = self.ap[:, self.off:self.off + n32]
        self.off += n32
        if dt != F32:
            v = v.bitcast(dt)
        v = v[:, 0:n]
        if len(shape) == 2:
            v = v.rearrange("p (a b) -> p a b", a=shape[0])
        elif len(shape) == 3:
            v = v.rearrange("p (a b c) -> p a b c", a=shape[0], b=shape[1])
        return v


class Builder:
    def __init__(self, cfg):
        self.cfg = cfg
        self.nc = bass.Bass("TRN2", target_bir_lowering=False)
        self.stack = ExitStack()
        self.sc = None
        self.bank_rr = {}
        self.pools = {"gen": list(range(8))}
        self.plan = []
        self.plan_i = 0
        self.issued = 0
        self.NW = 3
        self.x_gate = None
        self.live_lo = 0
        self.rr = {}

    def dram_in(self, name, shape):
        return self.nc.dram_tensor(name, list(shape), F32, kind="ExternalInput").ap()

    def sb(self, name, shape, dt):
        return self.stack.enter_context(self.nc.sbuf_tensor(name, list(shape), dt))

    def next_bank(self, pool="gen"):
        lst = self.pools[pool]
        i = self.bank_rr.get(pool, 0)
        self.bank_rr[pool] = (i + 1) % len(lst)
        return lst[i]

    def rot(self, key, n):
        i = self.rr.get(key, 0)
        self.rr[key] = (i + 1) % n
        return i

    def add(self, *a, **k):
        return self.sc.add(*a, **k)

    def _issue(self, upto):
        while self.issued < min(upto, len(self.plan)) and self.issued - self.NW < self.live_lo:
            i = self.issued
            s = i % self.NW
            src = self.plan[i][1].rearrange("(kc p) c -> p kc c", p=128)
            dst = self.wring[s]
            ent = self.add("pool", lambda h, dst=dst, src=src: h.dma_start(out=dst[:], in_=src),
                           writes=[self.wbuf[s]], dma="w%d" % s)
            if i == 0 and self.x_gate is not None and self.cfg.get("xgate", True):
                ent.deps = ent.deps + [self.x_gate]
            self.issued += 1

    def use_block(self, key, group_start=True):
        i = self.plan_i
        assert self.plan[i][0] == key, (self.plan[i][0], key)
        if group_start:
            self.live_lo = i
        self._issue(i + 3)
        self.plan_i += 1
        return i % self.NW

    def full_barrier(self):
        sc = self.sc
        last = {e: (sc.q[e][-1] if sc.q[e] else None) for e in sc.ENGS}
        for e in sc.ENGS:
            ent = sc.add(e, lambda h: h.nop())
            ent.deps = [last[o] for o in sc.ENGS if last[o] is not None]

    def build(self):
        nc = self.nc
        cfg = self.cfg
        st = self.stack
        with st:
            self.sc = sc = Sched(nc, st)
            add = self.add
            x = self.dram_in("x", [S, D])
            gains = self.dram_in("gains", [128, 5 * KC])
            identd = self.dram_in("ident_d", [128, 128])
            cst_d = self.dram_in("cst_d", [128, CST_COLS])
            w_in0 = self.dram_in("w_in0", [D, EVEN_IN])
            w_out0 = self.dram_in("w_out0", [D, D])
            w_in1 = self.dram_in("w_in1", [D, ODD_IN])
            w_out1 = self.dram_in("w_out1", [D, D])
            w1 = [self.dram_in("ffn_w1_%d" % l, [D, D_FF]) for l in range(2)]
            w2 = [self.dram_in("ffn_w2_%d" % l, [D_FF, D]) for l in range(2)]
            small_d = self.dram_in("small_d", [128, 160])
            sel_d = self.dram_in("sel_d", [2, 128])
            grow_d = self.dram_in("grow_d", [128, D])
            rope_d = self.dram_in("rope_d", [128, 2, S])
            wsT_d = self.dram_in("wsT_d", [128, 4, 128])
            sgb_d = self.dram_in("sgb_d", [128, 4, 128])
            out = nc.dram_tensor("out", [S, D], F32, kind="ExternalOutput").ap()

            plan = []
            if cfg.get("l0mix", True):
                if 'a' in cfg.get('l0p', 'ab'):
                    plan += [(("a", 0), w_in0[:, 0:512]), (("a", 1), w_in0[:, 512:1024])]
                plan += [(("wo0", h), w_out0[:, h * 512:(h + 1) * 512]) for h in range(2)]
            def ffn_plan(l):
                r = []
                for fg in range(4):
                    for blk in range(2):
                        c0 = fg * 1024 + blk * 512
                        r.append((("w1", l, fg, blk), w1[l][:, c0:c0 + 512]))
                    for half in range(2):
                        r.append((("w2", l, fg, half), w2[l][fg * 1024:(fg + 1) * 1024, half * 512:(half + 1) * 512]))
                return r
            if cfg.get("ffn0", True):
                plan += ffn_plan(0)
            if cfg.get("l1mix", True):
                plan += [(("i1", j), w_in1[:, j * 512:(j + 1) * 512]) for j in range(5)]
                plan += [(("wo1", h), w_out1[:, h * 512:(h + 1) * 512]) for h in range(2)]
            if cfg.get("ffn1", True):
                plan += ffn_plan(1)
            self.plan = plan

            ps = st.enter_context(nc.psum_tensor("ps", [128, 8, 512], F32))
            self.ps = ps
            pbuf = self.pbuf = [Buf("bank%d" % b, excl=True) for b in range(8)]
            arena_t = self.sb("arena", [128, KC * S], F32)
            hT = arena_t[:, :].rearrange("p (k t) -> p k t", k=KC)
            ar = Arena(arena_t[:, :])
            hTb = [[Buf("hT%d_%d" % (k, n)) for n in range(NT)] for k in range(KC)]
            hn = self.sb("hn", [128, KC, S], BF16)
            hnb = [[Buf("hn%d_%d" % (k, n)) for n in range(NT)] for k in range(KC)]
            mixcat = self.sb("mixcat", [128, KC, S], BF16)
            mcb = [[Buf("mc%d_%d" % (k, n)) for n in range(NT)] for k in range(KC)]
            u = mixcat
            ub = mcb
            self.wring = [self.sb("wr%d" % s_, [128, 8, 512], BF16) for s_ in range(self.NW)]
            self.wbuf = [Buf("wr%d" % s_) for s_ in range(self.NW)]
            arena2_t = self.sb("arena2", [128, 7680], F32)
            ar2 = Arena(arena2_t[:, :])
            ident = self.sb("ident", [128, 128], F32)
            identb = Buf("ident")
            cst = self.sb("cst", [128, CST_COLS], BF16)
            cstb = Buf("cst")
            identh = cst[:, 0:128]
            pswap = cst[:, 128:256]
            mask2 = cst[:, 256:768].rearrange("p (h q) -> p h q", h=2)
            Emat = cst[:, 768:772]
            trilT = cst[:, 772:900]
            sel = cst[0:2, 900:1028]
            OZ = cst[:, 1028:1220]
            ones_m = self.sb("ones_m", [128, 128], BF16)
            ones5 = self.sb("ones5", [128, 128], BF16)
            ones1 = self.sb("ones1", [128, 128], BF16)
            onesb = Buf("ones")
            g_sb = self.sb("g_sb", [128, 5 * KC], F32)
            gb_ = Buf("g")
            small = self.sb("small", [128, 160], F32)
            smallb = Buf("small")
            sel32 = self.sb("sel32", [2, 128], F32)
            sel32b = Buf("sel32")
            eps_sb = self.sb("eps_sb", [128, 1], F32)
            epsb = Buf("eps")
            NXT = 6
            xt = [arena2_t[:, i * D:(i + 1) * D] for i in range(NXT)]
            xtb = [Buf("xt%d" % i) for i in range(NXT)]
            ot, otb = xt, xtb
            sq = [self.sb("sq%d" % i, [128, TT], BF16) for i in range(3)]
            sqb = [Buf("sq%d" % i) for i in range(3)]
            rstd = [self.sb("rstd%d" % i, [128, TT], F32) for i in range(2)]
            rstdb = [Buf("rstd%d" % i) for i in range(2)]
            rl = [self.sb("rl%d" % i, [128, TT], BF16) for i in range(3)]
            rlb = [Buf("rl%d" % i) for i in range(3)]
            fss = [self.sb("fss%d" % i, [128, 4], F32) for i in range(2)]
            fssb = [Buf("fss%d" % i) for i in range(2)]
            fjunk = [rl[i] for i in range(3)]
            fjunkb = [rlb[i] for i in range(3)]
            grow = arena2_t[:, 4096:4096 + D]
            growb = Buf("grow")
            self.fin_ready = False

            add("sp", lambda h: h.dma_start(out=g_sb[:], in_=gains), writes=[gb_], dma="g")
            add("sp", lambda h: h.dma_start(out=ident[:], in_=identd), writes=[identb], dma="id")
            add("sp", lambda h: h.dma_start(out=small[:], in_=small_d), writes=[smallb], dma="sm")
            add("sp", lambda h: h.dma_start(out=sel32[:], in_=sel_d), writes=[sel32b], dma="sel")
            add("pool", lambda h: h.dma_start(out=cst[:], in_=cst_d), writes=[cstb], dma="cst")
            add("dve", lambda h: h.memset(ones_m[:], 1.0 / 1024.0), writes=[onesb])
            add("dve", lambda h: h.memset(ones5[:], 1.0 / 512.0), writes=[onesb])
            add("dve", lambda h: h.memset(ones1[:], 1.0), writes=[onesb])
            add("dve", lambda h: h.memset(eps_sb[:], EPS), writes=[epsb])

            wsT = self.sb("wsT", [128, 4, 128], BF16)
            wsTb = Buf("wsT")
            Gt = self.sb("Gt", [128, 4, 128], F32)
            Bt = self.sb("Bt", [128, 4, 128], F32)
            gbtb_ = Buf("GtBt")
            diag3 = self.sb("diag3", [128, 4, 3, 128], BF16)
            diag3b = Buf("diag3")
            if cfg.get("l1mix", True):
                oddk = small[:, 136:148].rearrange("p (c j) -> p c j", c=4)
                oddg = small[:, 148:152]
                oddb = small[:, 152:156]
                sgb = arena2_t[:, 7040:7040 + 512].rearrange("p (g t) -> p g t", g=4)
                sgbb = Buf("sgb")
                add("pool", lambda h: h.dma_start(out=wsT[:], in_=wsT_d), writes=[wsTb], dma="wsT")
                add("sp", lambda h: h.dma_start(out=sgb, in_=sgb_d), writes=[sgbb], dma="sgb")
                add("dve", lambda h: h.tensor_tensor(wsT[:], wsT[:], trilT.unsqueeze(1).broadcast_to([128, 4, 128]), ALU.mult),
                    reads=[cstb], writes=[wsTb])
                bws = self.next_bank()
                add("pe", lambda h, bws=bws: h.matmul(ps[:, bws, :], ones1[:], wsT[:].rearrange("p g t -> p (g t)"), start=True, stop=True),
                    reads=[wsTb, onesb], writes=[pbuf[bws]])
                for g_ in range(4):
                    add("dve", lambda h, g_=g_, bws=bws: h.scalar_tensor_tensor(
                        Bt[:, g_, :], ps[:, bws, g_ * 128:(g_ + 1) * 128], oddb[:, g_:g_ + 1], sgb[:, g_, :], ALU.mult, ALU.add),
                        reads=[pbuf[bws], smallb, sgbb], writes=[gbtb_])
                add("dve", lambda h: h.tensor_copy(Gt[:], oddg.unsqueeze(2).broadcast_to([128, 4, 128])), reads=[smallb], writes=[gbtb_])
                ckb3 = arena2_t[:, 7552:7552 + 8].bitcast(BF16)[:, 0:12].rearrange("p (c j) -> p c j", c=4)
                ckb3b = Buf("ckb3")
                add("dve", lambda h: h.tensor_copy(ckb3, oddk), reads=[smallb], writes=[ckb3b])
                for c in range(4):
                    add("dve", lambda h, c=c: h.tensor_tensor(
                        diag3[:, c, :, :], identh.unsqueeze(1).broadcast_to([128, 3, 128]),
                        ckb3[:, c, :].unsqueeze(2).broadcast_to([128, 3, 128]), ALU.mult),
                        reads=[cstb, ckb3b], writes=[diag3b])

            def load_x_to_hT(after_n=None, prefetched=0):
                for t in range(16):
                    i = t % NXT
                    if t >= prefetched:
                        ent_x = add("sp", lambda h, i=i, t=t: h.dma_start(out=xt[i], in_=x[t * 128:(t + 1) * 128, :]),
                                    writes=[xtb[i]], dma="xt%d" % i)
                        if t == 3 and self.x_gate is None:
                            self.x_gate = ent_x
                    n = t // 4
                    for half in range(2):
                        b = self.next_bank()
                        for j in range(4):
                            k = half * 4 + j
                            add("pe", lambda h, b=b, j=j, k=k, i=i: h.transpose(
                                ps[:, b, j * 128:(j + 1) * 128], xt[i][:, k * 128:(k + 1) * 128], ident[:]),
                                reads=[xtb[i], identb], writes=[pbuf[b]])
                        dstv = hT[:, half * 4:half * 4 + 4, t * 128:(t + 1) * 128]
                        srcv = ps[:, b, :].rearrange("p (j c) -> p j c", j=4)
                        wr = [hTb[half * 4 + j][n] for j in range(4)]
                        if half == 0:
                            add("act", lambda h, dstv=dstv, srcv=srcv: h.copy(dstv, srcv), reads=[pbuf[b]], writes=wr)
                        else:
                            add("dve", lambda h, dstv=dstv, srcv=srcv: h.tensor_copy(dstv, srcv), reads=[pbuf[b]], writes=wr)
                    if after_n is not None and t % 4 == 3 and t // 4 >= 1:
                        after_n(t // 4 - 1)
                if after_n is not None:
                    after_n(3)

            def rstd_from_bank(b, ri):
                if cfg.get("lnexp", True):
                    add("act", lambda h, b=b, ri=ri: h.activation(rstd[ri][:], ps[:, b, :], AF.Ln, bias=eps_sb[:, 0:1]),
                        reads=[pbuf[b], epsb], writes=[rstdb[ri]])
                    add("act", lambda h, ri=ri: h.activation(rstd[ri][:], rstd[ri][:], AF.Exp, scale=-0.5),
                        reads=[rstdb[ri]], writes=[rstdb[ri]])
                else:
                    add("act", lambda h, b=b, ri=ri: h.activation(rstd[ri][:], ps[:, b, :], AF.Sqrt, bias=eps_sb[:, 0:1]),
                        reads=[pbuf[b], epsb], writes=[rstdb[ri]])
                    add("dve", lambda h, ri=ri: h.reciprocal(rstd[ri][:], rstd[ri][:]),
                        reads=[rstdb[ri]], writes=[rstdb[ri]])

            def sumsq_bank(n):
                tsl = slice(n * TT, (n + 1) * TT)
                b = self.next_bank()
                for k in range(KC):
                    i = self.rot("sq", 3)
                    if k % 2 == 0:
                        add("act", lambda h, i=i, k=k, tsl=tsl: h.activation(sq[i][:], hT[:, k, tsl], AF.Square),
                            reads=[hTb[k][n]], writes=[sqb[i]])
                    else:
                        add("pool", lambda h, i=i, k=k, tsl=tsl: h.tensor_tensor(sq[i][:], hT[:, k, tsl], hT[:, k, tsl], ALU.mult),
                            reads=[hTb[k][n]], writes=[sqb[i]])
                    add("pe", lambda h, b=b, i=i, k=k: h.matmul(ps[:, b, :], ones_m[:], sq[i][:],
                                                              start=(k == 0), stop=(k == KC - 1)),
                        reads=[sqb[i], onesb], writes=[pbuf[b]])
                return b

            def rmsnorm_tile(gidx, n):
                tsl = slice(n * TT, (n + 1) * TT)
                b = sumsq_bank(n)
                ri = self.rot("rstd", 2)
                rstd_from_bank(b, ri)
                for k in range(KC):
                    add("dve", lambda h, k=k, tsl=tsl, ri=ri: h.scalar_tensor_tensor(
                        hn[:, k, tsl], hT[:, k, tsl], g_sb[:, gidx * KC + k:gidx * KC + k + 1], rstd[ri][:],
                        ALU.mult, ALU.mult),
                        reads=[hTb[k][n], rstdb[ri], gb_], writes=[hnb[k][n]])

            def rmsnorm(gidx):
                for n in range(NT):
                    rmsnorm_tile(gidx, n)

            def proj_add(keys, src, srcb, tail=None):
                pend_tail = [None]
                for half in range(2):
                    s_ = self.use_block(keys[half])
                    order = [(mc, n) for mc in range(4) for n in range(NT)]
                    if half == 1 and tail is not None:
                        order = [(mc, n) for n in range(NT) for mc in range(4)]
                    for (mc, n) in order:
                        m = half * 4 + mc
                        if True:
                            tsl = slice(n * TT, (n + 1) * TT)
                            b = self.next_bank()
                            for fc in range(8):
                                add("pe", lambda h, b=b, s_=s_, fc=fc, mc=mc, tsl=tsl: h.matmul(
                                    ps[:, b, :], self.wring[s_][:, fc, mc * 128:(mc + 1) * 128], src[:, fc, tsl],
                                    start=(fc == 0), stop=(fc == 7)),
                                    reads=[self.wbuf[s_], srcb[fc][n]], writes=[pbuf[b]])
                            add("dve", lambda h, b=b, m=m, tsl=tsl: h.tensor_tensor(
                                hT[:, m, tsl], ps[:, b, :], hT[:, m, tsl], ALU.add),
                                reads=[pbuf[b], hTb[m][n]], writes=[hTb[m][n]])
                            if half == 1 and tail is not None:
                                if mc == 1 and pend_tail[0] is not None:
                                    tail(pend_tail[0])
                                    pend_tail[0] = None
                                if mc == 3:
                                    pend_tail[0] = n
                if tail is not None and pend_tail[0] is not None:
                    tail(pend_tail[0])
                    pend_tail[0] = None

            def ffn(l, prenormed=False, tail=None):
                if not prenormed:
                    rmsnorm(2 + l)
                for fg in range(4):
                    for blk in range(2):
                        s_ = self.use_block(("w1", l, fg, blk))
                        for mc in range(4):
                            fc = blk * 4 + mc
                            for n in range(NT):
                                tsl = slice(n * TT, (n + 1) * TT)
                                b = self.next_bank()
                                for k in range(KC):
                                    add("pe", lambda h, b=b, s_=s_, k=k, mc=mc, tsl=tsl: h.matmul(
                                        ps[:, b, :], self.wring[s_][:, k, mc * 128:(mc + 1) * 128], hn[:, k, tsl],
                                        start=(k == 0), stop=(k == KC - 1)),
                                        reads=[self.wbuf[s_], hnb[k][n]], writes=[pbuf[b]])
                                ri_ = self.rot("rl", 3)
                                add("act", lambda h, b=b, ri_=ri_: h.activation(rl[ri_][:], ps[:, b, :], AF.Relu),
                                    reads=[pbuf[b]], writes=[rlb[ri_]])
                                add("pool", lambda h, fc=fc, tsl=tsl, ri_=ri_: h.tensor_tensor(
                                    u[:, fc, tsl], rl[ri_][:], rl[ri_][:], ALU.mult),
                                    reads=[rlb[ri_]], writes=[ub[fc][n]])
                    proj_add([("w2", l, fg, 0), ("w2", l, fg, 1)], u, ub, tail=(tail if fg == 3 else None))

            def final_store(only=None):
                if not self.fin_ready:
                    self.fin_ready = True
                    lastq = [sc.q[e_][-1] for e_ in ("pe", "act", "dve", "pool") if sc.q[e_]]
                    ent = add("sp", lambda h: h.dma_start(out=grow, in_=grow_d), writes=[growb], dma="grow")
                    ent.deps = ent.deps + lastq
                for n in (range(NT) if only is None else [only]):
                    for tq in range(4):
                        oi = self.rot("ot", 2)
                        t0 = n * TT + tq * 128
                        si = self.rot("fss", 2)
                        banks = []
                        for half in range(2):
                            pb = self.next_bank()
                            banks.append(pb)
                            for j in range(4):
                                k = half * 4 + j
                                add("pe", lambda h, pb=pb, j=j, k=k, t0=t0: h.transpose(
                                    ps[:, pb, j * 128:(j + 1) * 128], hT[:, k, t0:t0 + 128], ident[:]),
                                    reads=[hTb[k][n], identb], writes=[pbuf[pb]])
                            ji = self.rot("fjunk", 3)
                            add("act", lambda h, pb=pb, si=si, half=half, ji=ji: h.activation(
                                fjunk[ji][:], ps[:, pb, :], AF.Square, accum_out=fss[si][:, half:half + 1]),
                                reads=[pbuf[pb], fjunkb[ji]], writes=[fssb[si], fjunkb[ji]])
                        add("dve", lambda h, si=si: h.tensor_tensor(fss[si][:, 2:3], fss[si][:, 0:1], fss[si][:, 1:2], ALU.add),
                            reads=[fssb[si]], writes=[fssb[si]])
                        add("act", lambda h, si=si: h.activation(fss[si][:, 3:4], fss[si][:, 2:3], AF.Sqrt,
                                                                 bias=eps_sb[:, 0:1], scale=1.0 / 1024.0),
                            reads=[fssb[si], epsb], writes=[fssb[si]])
                        add("dve", lambda h, si=si: h.reciprocal(fss[si][:, 3:4], fss[si][:, 3:4]), reads=[fssb[si]], writes=[fssb[si]])
                        for half in range(2):
                            pb = banks[half]
                            add("dve", lambda h, pb=pb, oi=oi, half=half, si=si: h.scalar_tensor_tensor(
                                ot[oi][:, half * 512:(half + 1) * 512], ps[:, pb, :], fss[si][:, 3:4],
                                grow[:, half * 512:(half + 1) * 512], ALU.mult, ALU.mult),
                                reads=[pbuf[pb], fssb[si], growb], writes=[otb[oi]])
                        add("sp", lambda h, oi=oi, t0=t0: h.dma_start(out=out[t0:t0 + 128, :], in_=ot[oi]),
                            reads=[otb[oi]], writes=[], dma="xt%d" % oi)

            def l0_mixer():
                convk = small[:, 0:124].rearrange("p (c j) -> p c j", c=4)
                convb = small[:, 124:128]
                lng = small[:, 128:132]
                lnb = small[:, 132:136]
                def claim(lo_el, hi_el):
                    ks = range(lo_el // S, (hi_el - 1) // S + 1)
                    return [hTb[k][n] for k in ks for n in range(NT)]
                ar.reset()
                rope = ar.get([2, S], F32)
                ropeb = Buf("rope")
                add("sp", lambda h: h.dma_start(out=rope, in_=rope_d), writes=[ropeb] + claim(0, ar.off), dma="rope")
                base_off = ar.off
                off0 = ar.off
                a_pad = ar.get([4, 30 + S], BF16)
                apclaim = claim(off0, ar.off)
                off1 = ar.off
                apb = [[Buf("ap%d_%d" % (c, n)) for n in range(NT)] for c in range(4)]
                apz = Buf("apz")
                diag = ar.get([4, 31, 128], BF16)
                diagb = Buf("diag")
                dgclaim = claim(off1, ar.off)
                ar2.reset()
                cv = [ar2.get([TT], F32) for _ in range(4)]
                cvb = [Buf("cv%d" % i) for i in range(4)]
                ybf = [ar2.get([TT], BF16) for _ in range(4)]
                ybfb = [Buf("ybf%d" % i) for i in range(4)]
                ysq = [ar2.get([TT], BF16) for _ in range(4)]
                ysqb = [Buf("ysq%d" % i) for i in range(4)]
                sig = [ar2.get([TT], F32) for _ in range(2)]
                sigb = [Buf("sig%d" % i) for i in range(2)]
                m2 = ar2.get([TT], F32)
                m2b = Buf("m2")
                lrs = ar2.get([TT], F32)
                lrsb = Buf("lrs")
                tn = [ar2.get([TT], F32) for _ in range(2)]
                tnb = [Buf("tn%d" % i) for i in range(2)]
                add("dve", lambda h: h.memset(a_pad[:, :, 0:30], 0.0), writes=[apz] + apclaim, strict=True)
                doA = 'a' in cfg.get('l0p', 'ab')
                ckb = ar2.get([4, 31], BF16)
                ckbb = Buf("ckb")
                if doA:
                    s_lin = self.use_block(("a", 0))
                    s_gate = self.use_block(("a", 1), False)
                add("dve", lambda h: h.tensor_copy(ckb, convk), reads=[smallb], writes=[ckbb])
                def build_diag(c):
                    add("dve", lambda h, c=c: h.tensor_tensor(
                        diag[:, c, :, :], identh.unsqueeze(1).broadcast_to([128, 31, 128]),
                        ckb[:, c, :].unsqueeze(2).broadcast_to([128, 31, 128]), ALU.mult),
                        reads=[cstb, ckbb], writes=[Buf("dg")] + dgclaim, strict=True)
                    diagb.w.append(sc.q["dve"][-1])
                for c in range(4 if doA else 0):
                    if c > 0:
                        build_diag(c - 1)
                    for n in range(NT):
                        tsl = slice(n * TT, (n + 1) * TT)
                        b1 = self.next_bank()
                        b2 = self.next_bank()
                        for (bb, s_) in ((b1, s_lin), (b2, s_gate)):
                            for k in range(KC):
                                add("pe", lambda h, bb=bb, s_=s_, k=k, c=c, tsl=tsl: h.matmul(
                                    ps[:, bb, :], self.wring[s_][:, k, c * 128:(c + 1) * 128], hn[:, k, tsl],
                                    start=(k == 0), stop=(k == KC - 1)),
                                    reads=[self.wbuf[s_], hnb[k][n]], writes=[pbuf[bb]])
                        si = self.rot("sig", 2)
                        add("act", lambda h, b2=b2, si=si: h.activation(sig[si], ps[:, b2, :], AF.Sigmoid),
                            reads=[pbuf[b2]], writes=[sigb[si]])
                        add("dve", lambda h, b1=b1, si=si, c=c, n=n: h.tensor_tensor(
                            a_pad[:, c, 30 + n * TT:30 + (n + 1) * TT], ps[:, b1, :], sig[si], ALU.mult),
                            reads=[pbuf[b1], sigb[si]], writes=[apb[c][n]])
                if doA:
                    build_diag(3)
                for n in range(NT if doA else 0):
                    tsl = slice(n * TT, (n + 1) * TT)
                    for c in range(4):
                        if n == 0:
                            pass
                        b = self.next_bank()
                        rd = [apb[c][n], apz, diagb] + ([apb[c][n - 1]] if n > 0 else [])
                        for j in range(31):
                            add("pe", lambda h, b=b, j=j, c=c, n=n: h.matmul(
                                ps[:, b, :], diag[:, c, j, :], a_pad[:, c, n * TT + j:n * TT + j + TT],
                                start=(j == 0), stop=(j == 30)),
                                reads=rd, writes=[pbuf[b]])
                        add("act", lambda h, b=b, c=c: h.activation(cv[c], ps[:, b, :], AF.Identity, bias=convb[:, c:c + 1]),
                            reads=[pbuf[b], smallb], writes=[cvb[c]])
                        add("pool", lambda h, c=c: h.tensor_copy(ybf[c], cv[c]), reads=[cvb[c]], writes=[ybfb[c]])
                        add("act", lambda h, c=c: h.activation(ysq[c], cv[c], AF.Square), reads=[cvb[c]], writes=[ysqb[c]])
                    bm = self.next_bank()
                    bq = self.next_bank()
                    for c in range(4):
                        add("pe", lambda h, bm=bm, c=c: h.matmul(ps[:, bm, :], ones5[:], ybf[c], start=(c == 0), stop=(c == 3)),
                            reads=[ybfb[c], onesb], writes=[pbuf[bm]])
                    for c in range(4):
                        add("pe", lambda h, bq=bq, c=c: h.matmul(ps[:, bq, :], ones5[:], ysq[c], start=(c == 0), stop=(c == 3)),
                            reads=[ysqb[c], onesb], writes=[pbuf[bq]])
                    add("act", lambda h, bm=bm: h.activation(m2, ps[:, bm, :], AF.Square), reads=[pbuf[bm]], writes=[m2b])
                    add("dve", lambda h, bq=bq: h.tensor_tensor(lrs, ps[:, bq, :], m2, ALU.subtract),
                        reads=[pbuf[bq], m2b], writes=[lrsb])
                    add("act", lambda h: h.activation(lrs, lrs, AF.Sqrt, bias=eps_sb[:, 0:1]), reads=[lrsb, epsb], writes=[lrsb])
                    add("dve", lambda h: h.reciprocal(lrs, lrs), reads=[lrsb], writes=[lrsb])
                    for c in range(4):
                        ti = self.rot("tn", 2)
                        add("dve", lambda h, c=c, bm=bm, ti=ti: h.tensor_tensor(tn[ti], cv[c], ps[:, bm, :], ALU.subtract),
                            reads=[cvb[c], pbuf[bm]], writes=[tnb[ti]])
                        add("pool", lambda h, ti=ti: h.tensor_tensor(tn[ti], tn[ti], lrs, ALU.mult),
                            reads=[tnb[ti], lrsb], writes=[tnb[ti]])
                        add("act", lambda h, c=c, ti=ti, tsl=tsl: h.activation(
                            mixcat[:, c, tsl], tn[ti], AF.Silu, bias=lnb[:, c:c + 1], scale=lng[:, c:c + 1]),
                            reads=[tnb[ti], smallb], writes=[mcb[c][n]])

                self.full_barrier()
                ar.reset(base_off)
                ar2.reset()
                self.pools = {"S": [0, 1], "O": [3, 4], "DEN": [5], "gen": [2, 6, 7], "proj": [2, 6, 7, 0, 1]}
                self.bank_rr = {}
                acc_o = ar.get([S], F32)
                acc_ob = Buf("acc_o")
                den = ar.get([S], F32)
                denb = Buf("den")
                rden = den
                rdenb = denb
                qT = [ar.get([S], BF16) for _ in range(2)]
                kT = [ar.get([S], BF16) for _ in range(2)]
                qTb = [[Buf("qT%d_%d" % (i, n)) for n in range(NT)] for i in range(2)]
                kTb = [[Buf("kT%d_%d" % (i, n)) for n in range(NT)] for i in range(2)]
                vT = ar.get([S], BF16)
                vTb = [Buf("vT%d" % n) for n in range(NT)]
                vx = [ar.get([16, 192], BF16) for _ in range(2)]
                vxb = [[Buf("vx%d_%d" % (i, t)) for t in range(4)] for i in range(2)]
                vxz = [Buf("vxz%d" % i) for i in range(2)]
                wq = [ar2.get([8, 3, 128], BF16) for _ in range(2)]
                wqb = [[Buf("wq%d_%d" % (i, j)) for j in range(3)] for i in range(2)]
                zb = [ar2.get([TT], BF16) for _ in range(3)]
                zbb = [Buf("zb%d" % i) for i in range(3)]
                t1 = [ar2.get([TT], F32) for _ in range(2)]
                t1b = [Buf("t1%d" % i) for i in range(2)]
                t2 = [ar2.get([TT], F32) for _ in range(2)]
                t2b = [Buf("t2%d" % i) for i in range(2)]
                rcp = [ar2.get([TT], F32) for _ in range(2)]
                rcpb = [Buf("rcp%d" % i) for i in range(2)]
                pt = [ar2.get([TT], BF16) for _ in range(3)]
                ptb = [Buf("pt%d" % i) for i in range(3)]
                pthb = [[Buf("pth%d_%d" % (i, hh)) for hh in range(2)] for i in range(3)]
                for i in range(2):
                    add("dve", lambda h, i=i: h.memset(vx[i][:, :, 64:128], 0.0), writes=[vxz[i]])

                units = [(p, g) for p in range(4) for g in range(3)]

                def load_unit_w(ui):
                    p, g = units[ui]
                    wi = ui % 2
                    hd0 = g * 8 + 2 * p
                    for j, base in enumerate((1024, 2560, 4096)):
                        c0 = base + hd0 * HD
                        src = w_in0[:, c0:c0 + 128].rearrange("(kc q) c -> q kc c", q=128)
                        add("pool", lambda h, wi=wi, j=j, src=src: h.dma_start(out=wq[wi][:, :, j, :], in_=src),
                            writes=[wqb[wi][j]], dma="wq%d_%d" % (wi, j))

                if 'b' not in cfg.get('l0p', 'ab'):
                    units = units[:cfg.get('nunits', 0)]

                def unit_ctx(ui):
                    p, g = units[ui]
                    wi = ui % 2
                    d, nb = DIL[g], NBLK[g]
                    def tok(T, cnt=128):
                        r, jj = T // nb, T % nb
                        st0 = r + d * 128 * jj
                        return slice(st0, st0 + d * (cnt - 1) + 1, d)
                    return p, g, wi, d, nb, tok

                def proj_items(ui, pool):
                    p, g, wi, d, nb, tok = unit_ctx(ui)
                    tiles = [(j, n) for j in range(2) for n in range(NT)]
                    state = {}
                    def proj_stage(j, n):
                        tsl = slice(n * TT, (n + 1) * TT)
                        b = self.next_bank(pool)
                        for k in range(KC):
                            add("pe", lambda h, b=b, k=k, j=j, tsl=tsl: h.matmul(
                                ps[:, b, :], wq[wi][:, k, j, :], hn[:, k, tsl], start=(k == 0), stop=(k == KC - 1)),
                                reads=[wqb[wi][j], hnb[k][n]], writes=[pbuf[b]])
                        zi = self.rot("zb", 3)
                        add("act", lambda h, b=b, zi=zi: h.copy(zb[zi], ps[:, b, :]), reads=[pbuf[b]], writes=[zbb[zi]])
                        ti = self.rot("t12", 2)
                        add("dve", lambda h, b=b, ti=ti, tsl=tsl: h.tensor_tensor(t1[ti], ps[:, b, :], rope[:, 0, tsl], ALU.mult),
                            reads=[pbuf[b], ropeb], writes=[t1b[ti]])
                        return b, zi, ti
                    def swap_stage(j, n, b, zi, ti):
                        tsl = slice(n * TT, (n + 1) * TT)
                        dstT, dstb = (qT[wi], qTb[wi]) if j == 0 else (kT[wi], kTb[wi])
                        b2 = self.next_bank(pool)
                        add("pe", lambda h, b2=b2, zi=zi: h.matmul(ps[:, b2, :], pswap, zb[zi], start=True, stop=True),
                            reads=[zbb[zi], cstb], writes=[pbuf[b2]])
                        add("dve", lambda h, b2=b2, ti=ti, tsl=tsl: h.tensor_tensor(t2[ti], ps[:, b2, :], rope[:, 1, tsl], ALU.mult),
                            reads=[pbuf[b2], ropeb], writes=[t2b[ti]])
                        add("pool", lambda h, ti=ti, dstT=dstT, tsl=tsl: h.tensor_tensor(dstT[:, tsl], t1[ti], t2[ti], ALU.add),
                            reads=[t1b[ti], t2b[ti]], writes=[dstb[n]])
                    def qk_item(i):
                        def fn():
                            if i < len(tiles):
                                cur = tiles[i] + proj_stage(*tiles[i])
                            if i > 0:
                                swap_stage(*state["prev"])
                            if i < len(tiles):
                                state["prev"] = cur
                        return fn
                    def vT_item(n):
                        def fn():
                            tsl = slice(n * TT, (n + 1) * TT)
                            b = self.next_bank(pool)
                            for k in range(KC):
                                add("pe", lambda h, b=b, k=k, tsl=tsl: h.matmul(
                                    ps[:, b, :], wq[wi][:, k, 2, :], hn[:, k, tsl], start=(k == 0), stop=(k == KC - 1)),
                                    reads=[wqb[wi][2], hnb[k][n]], writes=[pbuf[b]])
                            add("act", lambda h, b=b, tsl=tsl: h.copy(vT[:, tsl], ps[:, b, :]), reads=[pbuf[b]], writes=[vTb[n]])
                        return fn
                    def v_item(T4):
                        def fn():
                            b = self.next_bank(pool)
                            psb = ps[:, b, :].bitcast(BF16)
                            for tt_ in range(4):
                                T = T4 * 4 + tt_
                                tk = tok(T)
                                add("pe", lambda h, psb=psb, tk=tk, tt_=tt_: h.transpose(
                                    psb[:, tt_ * 128:(tt_ + 1) * 128], vT[:, tk], identh),
                                    reads=vTb + [cstb], writes=[pbuf[b]])
                            dstv = vx[wi][:, T4 * 4:T4 * 4 + 4, :].rearrange("p t (x c) -> p t x c", c=64)[:, :, 0:3:2, :]
                            srcv = psb[:, 0:512].rearrange("p (t x c) -> p t x c", t=4, x=2)
                            add("act", lambda h, dstv=dstv, srcv=srcv: h.copy(dstv, srcv), reads=[pbuf[b]], writes=[vxb[wi][T4]])
                        return fn
                    return ([qk_item(i) for i in range(len(tiles) + 1)] + [vT_item(n) for n in range(NT)]
                            + [v_item(T4) for T4 in range(4)])

                def make_norm(p):
                    def fn(ns=range(NT)):
                        for n in ns:
                            tsl = slice(n * TT, (n + 1) * TT)
                            ti = self.rot("rcp", 2)
                            if cfg.get("lnexp", True):
                                add("act", lambda h, ti=ti, tsl=tsl: h.activation(rcp[ti], den[:, tsl], AF.Ln),
                                    reads=[denb], writes=[rcpb[ti]])
                                add("act", lambda h, ti=ti: h.activation(rcp[ti], rcp[ti], AF.Exp, scale=-1.0),
                                    reads=[rcpb[ti]], writes=[rcpb[ti]])
                            else:
                                add("dve", lambda h, ti=ti, tsl=tsl: h.reciprocal(rcp[ti], den[:, tsl]),
                                    reads=[denb], writes=[rcpb[ti]])
                            add("dve", lambda h, ti=ti, tsl=tsl, p=p: h.tensor_tensor(
                                mixcat[:, 4 + p, tsl], acc_o[:, tsl], rcp[ti], ALU.mult),
                                reads=[rcpb[ti], acc_ob], writes=[mcb[4 + p][n]])
                    return fn

                def attention(ui, extra, pending):
                    p, g, wi, d, nb, tok = unit_ctx(ui)
                    def evac(Bk, ob, db):
                        if g == 0:
                            dsl = slice(Bk * 512, (Bk + 1) * 512)
                            ov, dv = acc_o[:, dsl], den[:, dsl]
                            pso, psd = ps[:, ob, :], ps[:, db, :]
                        elif g == 1:
                            dsl = slice(Bk, Bk + 4 * 511 + 1, 4)
                            ov, dv = acc_o[:, dsl], den[:, dsl]
                            pso, psd = ps[:, ob, :], ps[:, db, :]
                        else:
                            ov = acc_o.rearrange("p (i r) -> p r i", r=16)[:, 4 * Bk:4 * Bk + 4, :]
                            dv = den.rearrange("p (i r) -> p r i", r=16)[:, 4 * Bk:4 * Bk + 4, :]
                            pso = ps[:, ob, :].rearrange("p (r i) -> p r i", r=4)
                            psd = ps[:, db, :].rearrange("p (r i) -> p r i", r=4)
                        if g == 0:
                            add("act", lambda h, dv=dv, psd=psd: h.copy(dv, psd), reads=[pbuf[db]], writes=[denb])
                            add("act", lambda h, ov=ov, pso=pso: h.copy(ov, pso), reads=[pbuf[ob]], writes=[acc_ob])
                        else:
                            add("dve", lambda h, dv=dv, psd=psd: h.tensor_tensor(dv, psd, dv, ALU.add),
                                reads=[pbuf[db], denb], writes=[denb])
                            add("dve", lambda h, ov=ov, pso=pso: h.tensor_tensor(ov, pso, ov, ALU.add),
                                reads=[pbuf[ob], acc_ob], writes=[acc_ob])

                    def qk_stage(T):
                        r, jj = T // nb, T % nb
                        nqb = 2 if jj + 1 < nb else 1
                        nq = 128 * nqb
                        ktk = tok(T)
                        qtk = tok(T, nq)
                        qn = sorted(set(range(qtk.start // TT, (qtk.stop - 1) // TT + 1)))
                        kn = sorted(set(range(ktk.start // TT, (ktk.stop - 1) // TT + 1)))
                        pi = self.rot("pt", 3)
                        sbs = [self.next_bank("S"), self.next_bank("S")]
                        for hh in range(2):
                            sb_ = sbs[hh]
                            add("pe", lambda h, sb_=sb_, hh=hh, ktk=ktk, qtk=qtk, nq=nq: h.matmul(
                                ps[:, sb_, 0:nq], kT[wi][hh * 64:(hh + 1) * 64, ktk],
                                qT[wi][hh * 64:(hh + 1) * 64, qtk], start=True, stop=True),
                                reads=[kTb[wi][n_] for n_ in kn] + [qTb[wi][n_] for n_ in qn], writes=[pbuf[sb_]])
                        for hh in range(2):
                            sb_ = sbs[hh]
                            add("act", lambda h, sb_=sb_, pi=pi, nq=nq, hh=hh: h.activation(
                                pt[pi][:, hh * nq:(hh + 1) * nq], ps[:, sb_, 0:nq], AF.Exp, scale=0.125),
                                reads=[pbuf[sb_]], writes=[pthb[pi][hh]])
                        pv = pt[pi][:, 0:2 * nq].rearrange("p (h q) -> p h q", h=2)
                        add("dve", lambda h, pv=pv, nq=nq: h.tensor_tensor(pv, pv, mask2[:, :, 0:nq], ALU.mult),
                            reads=[cstb], writes=[pthb[pi][0], pthb[pi][1]])
                        return pi, nq, nqb

                    obdb = [None, None]
                    def pv_stage(T, pi, nq, nqb):
                        r, jj = T // nb, T % nb
                        if nqb == 2 and (T % 4) != 3 and cfg.get("pv256", True):
                            slot = T % 4
                            if slot == 0 and jj == 0:
                                obdb[0] = self.next_bank("O")
                                obdb[1] = self.next_bank("DEN")
                            ob, db = obdb
                            for hh in range(2):
                                stf = (slot == 0 and jj == 0 and hh == 0)
                                add("pe", lambda h, ob=ob, slot=slot, hh=hh, T=T, pi=pi, nq=nq, stf=stf: h.matmul(
                                    ps[:, ob, slot * 128:slot * 128 + 256], vx[wi][:, T, hh * 64:hh * 64 + 128],
                                    pt[pi][:, hh * nq:hh * nq + 256],
                                    start=stf, stop=False, skip_group_check=True),
                                    reads=[vxb[wi][T // 4], vxz[wi], pthb[pi][hh]], writes=[pbuf[ob]])
                                add("pe", lambda h, db=db, slot=slot, hh=hh, pi=pi, nq=nq, stf=stf: h.matmul(
                                    ps[:, db, slot * 128:slot * 128 + 256], OZ[:, hh * 64:hh * 64 + 128],
                                    pt[pi][:, hh * nq:hh * nq + 256],
                                    start=stf, stop=False, skip_group_check=True),
                                    reads=[pthb[pi][hh], cstb], writes=[pbuf[db]])
                            return
                        for qb in range(nqb):
                            Bq = T + qb
                            slot = Bq % 4
                            fresh_block = (qb == 1) or (jj == 0)
                            if slot == 0 and fresh_block:
                                obdb[0] = self.next_bank("O")
                                obdb[1] = self.next_bank("DEN")
                            ob, db = obdb
                            for hh in range(2):
                                stf = (slot == 0 and fresh_block and hh == 0)
                                add("pe", lambda h, ob=ob, slot=slot, hh=hh, T=T, pi=pi, nq=nq, qb=qb, stf=stf: h.matmul(
                                    ps[:, ob, slot * 128:(slot + 1) * 128], vx[wi][:, T, hh * 64:hh * 64 + 128],
                                    pt[pi][:, hh * nq + qb * 128:hh * nq + qb * 128 + 128],
                                    start=stf, stop=False, skip_group_check=True),
                                    reads=[vxb[wi][T // 4], vxz[wi], pthb[pi][hh]], writes=[pbuf[ob]])
                                add("pe", lambda h, db=db, slot=slot, hh=hh, pi=pi, nq=nq, qb=qb, stf=stf: h.matmul(
                                    ps[:, db, slot * 128:(slot + 1) * 128], OZ[:, hh * 64:hh * 64 + 128],
                                    pt[pi][:, hh * nq + qb * 128:hh * nq + qb * 128 + 128],
                                    start=stf, stop=False, skip_group_check=True),
                                    reads=[pthb[pi][hh], cstb], writes=[pbuf[db]])
                            if qb == 0 and slot == 3:
                                evac(Bq // 4, ob, db)

                    extra = list(extra)
                    if cfg.get('nointer'):
                        while extra:
                            extra.pop(0)()
                    prev = None
                    for T in range(16):
                        cur = qk_stage(T)
                        if prev is not None:
                            pv_stage(T - 1, *prev)
                        prev = cur
                        if T % 4 == 1 and pending is not None:
                            pending([T // 4])
                        if extra:
                            extra.pop(0)()
                    pv_stage(15, *prev)
                    while extra:
                        extra.pop(0)()

                if units:
                    load_unit_w(0)
                    if len(units) > 1:
                        load_unit_w(1)
                    for it in proj_items(0, "proj"):
                        it()
                    pending = None
                    for ui, (p, g) in enumerate(units):
                        if ui == len(units) - 1 and len(units) > 2 and cfg.get("xpre", True):
                            lastq = [sc.q[e_][-1] for e_ in ("pe", "act", "dve", "pool") if sc.q[e_]]
                            for t in range(5):
                                ent = add("sp", lambda h, t=t: h.dma_start(out=xt[t], in_=x[t * 128:(t + 1) * 128, :]),
                                          writes=[xtb[t]], dma="xt%d" % t)
                                ent.deps = ent.deps + lastq
                            self.x_prefetched = 5
                        if ui + 2 < len(units):
                            load_unit_w(ui + 2)
                        extra = proj_items(ui + 1, "gen") if ui + 1 < len(units) else []
                        attention(ui, extra, pending)
                        pending = make_norm(p) if g == 2 else None
                    if pending is not None:
                        pending()
                self.pools = {"gen": list(range(8))}
                self.bank_rr = {}
                self.full_barrier()
                load_x_to_hT(prefetched=getattr(self, "x_prefetched", 0))
                proj_add([("wo0", 0), ("wo0", 1)], mixcat, mcb, tail=self.tails.get("l0"))

            def l1_mixer(prenormed=False):
                if not prenormed:
                    rmsnorm(1)
                ar2.reset()
                mark = ar2.off
                ppad = [ar2.get([2 + S], BF16) for _ in range(2)]
                ppb = [[Buf("pp%d_%d" % (i, n)) for n in range(NT)] for i in range(2)]
                ppz = [Buf("ppz%d" % i) for i in range(2)]
                gct = [ar2.get([TT], F32) for _ in range(2)]
                gctb = [Buf("gct%d" % i) for i in range(2)]
                gbt = [ar2.get([TT], F32) for _ in range(3)]
                gbtb = [Buf("gbt%d" % i) for i in range(3)]
                cend = ar2.off
                ar2.reset(mark)
                usb = ar2.get([4, TT], F32)
                usbb = [Buf("usb%d" % c) for c in range(4)]
                ga = [ar2.get([TT], F32) for _ in range(2)]
                gab = [Buf("ga%d" % i) for i in range(2)]
                vsb = [ar2.get([512], F32) for _ in range(4)]
                vsbb = [Buf("vsb%d" % i) for i in range(4)]
                vln = [ar2.get([512], BF16) for _ in range(4)]
                vlnb = [Buf("vln%d" % i) for i in range(4)]
                stt = ar2.get([4, 8], F32)
                sttb = [Buf("stt%d" % i) for i in range(4)]
                dtm = [ar2.get([4, 128], F32) for _ in range(2)]
                dtmb = [Buf("dtm%d" % i) for i in range(2)]
                for i in range(2):
                    add("pool", lambda h, i=i: h.memset(ppad[i][:, 0:2], 0.0), writes=[ppz[i]])

                def gelu(eng_out_fn, b, outv, reads_extra, wbufs):
                    gi = self.rot("ga", 2)
                    add("act", lambda h, b=b, gi=gi: h.activation(ga[gi], ps[:, b, :], AF.Square), reads=[pbuf[b]], writes=[gab[gi]])
                    add("pool", lambda h, gi=gi: h.tensor_scalar(ga[gi], ga[gi], 0.044715, 1.0, ALU.mult, ALU.add),
                        reads=[gab[gi]], writes=[gab[gi]])
                    add("dve", lambda h, b=b, gi=gi: h.tensor_tensor(ga[gi], ga[gi], ps[:, b, :], ALU.mult),
                        reads=[gab[gi], pbuf[b]], writes=[gab[gi]])
                    add("act", lambda h, gi=gi: h.activation(ga[gi], ga[gi], AF.Sigmoid, scale=GELU_K), reads=[gab[gi]], writes=[gab[gi]])
                    add("dve", lambda h, b=b, gi=gi, outv=outv: h.tensor_tensor(outv, ga[gi], ps[:, b, :], ALU.mult),
                        reads=[gab[gi], pbuf[b]], writes=wbufs)

                s_gb = self.use_block(("i1", 0))
                s_gc = self.use_block(("i1", 1), False)
                s_xs = self.use_block(("i1", 2), False)
                pend_c = [None]
                for c in range(4 if 'c' in cfg.get('l1p', 'cd') else 0):
                    pi_ = c % 2
                    for n in range(NT):
                        tsl = slice(n * TT, (n + 1) * TT)
                        bgb, bgc, bxs = self.next_bank(), self.next_bank(), self.next_bank()
                        for (bb, s_) in ((bgc, s_gc), (bxs, s_xs), (bgb, s_gb)):
                            for k in range(KC):
                                add("pe", lambda h, bb=bb, s_=s_, k=k, c=c, tsl=tsl: h.matmul(
                                    ps[:, bb, :], self.wring[s_][:, k, c * 128:(c + 1) * 128], hn[:, k, tsl],
                                    start=(k == 0), stop=(k == KC - 1)),
                                    reads=[self.wbuf[s_], hnb[k][n]], writes=[pbuf[bb]])
                        gi = self.rot("gct", 2)
                        add("act", lambda h, bgc=bgc, gi=gi: h.copy(gct[gi], ps[:, bgc, :]), reads=[pbuf[bgc]], writes=[gctb[gi]])
                        add("dve", lambda h, bxs=bxs, gi=gi, pi_=pi_, n=n: h.tensor_tensor(
                            ppad[pi_][:, 2 + n * TT:2 + (n + 1) * TT], gct[gi], ps[:, bxs, :], ALU.mult),
                            reads=[gctb[gi], pbuf[bxs]], writes=[ppb[pi_][n]])
                        bi = self.rot("gbt", 3)
                        add("act", lambda h, bgb=bgb, bi=bi: h.copy(gbt[bi], ps[:, bgb, :]), reads=[pbuf[bgb]], writes=[gbtb[bi]])
                        if pend_c[0] is not None:
                            pend_c[0]()
                        def conv_stage(c=c, n=n, pi_=pi_, bi=bi, tsl=tsl):
                            bc = self.next_bank()
                            rd = [ppb[pi_][n], ppz[pi_], diag3b] + ([ppb[pi_][n - 1]] if n > 0 else [])
                            for j in range(3):
                                add("pe", lambda h, bc=bc, j=j: h.matmul(
                                    ps[:, bc, :], diag3[:, c, j, :], ppad[pi_][:, n * TT + j:n * TT + j + TT],
                                    start=(j == 0), stop=(j == 2)),
                                    reads=rd, writes=[pbuf[bc]])
                            add("dve", lambda h, bc=bc: h.tensor_tensor(
                                mixcat[:, c, tsl], gbt[bi], ps[:, bc, :], ALU.mult),
                                reads=[gbtb[bi], pbuf[bc]], writes=[mcb[c][n]])
                        pend_c[0] = conv_stage
                if pend_c[0] is not None:
                    pend_c[0]()
                self.full_barrier()
                s_u = self.use_block(("i1", 3))
                s_v = self.use_block(("i1", 4), False)
                for n in range(NT if 'd' in cfg.get('l1p', 'cd') else 0):
                    tsl = slice(n * TT, (n + 1) * TT)
                    for tq in range(4):
                        t0 = n * TT + tq * 128
                        b = self.next_bank()
                        for k in range(KC):
                            add("pe", lambda h, b=b, k=k, t0=t0: h.matmul(
                                ps[:, b, :], hn[:, k, t0:t0 + 128], self.wring[s_v][:, k, :],
                                start=(k == 0), stop=(k == KC - 1)),
                                reads=[self.wbuf[s_v], hnb[k][n]], writes=[pbuf[b]])
                        add("act", lambda h, b=b, tq=tq: h.activation(vsb[tq], ps[:, b, :], AF.Gelu_apprx_tanh),
                            reads=[pbuf[b]], writes=[vsbb[tq]])
                        add("dve", lambda h, tq=tq: h.bn_stats(stt[:, tq, 0:6], vsb[tq]), reads=[vsbb[tq]], writes=[sttb[tq]])
                        add("dve", lambda h, tq=tq: h.bn_aggr(stt[:, tq, 6:8], stt[:, tq, 0:6]), reads=[sttb[tq]], writes=[sttb[tq]])
                    for c in range(4):
                        b = self.next_bank()
                        for k in range(KC):
                            add("pe", lambda h, b=b, k=k, c=c, tsl=tsl: h.matmul(
                                ps[:, b, :], self.wring[s_u][:, k, c * 128:(c + 1) * 128], hn[:, k, tsl],
                                start=(k == 0), stop=(k == KC - 1)),
                                reads=[self.wbuf[s_u], hnb[k][n]], writes=[pbuf[b]])
                        add("act", lambda h, b=b, c=c: h.activation(usb[:, c, :], ps[:, b, :], AF.Gelu_apprx_tanh),
                            reads=[pbuf[b]], writes=[usbb[c]])
                    add("act", lambda h: h.activation(stt[:, :, 7], stt[:, :, 7], AF.Sqrt, bias=eps_sb[:, 0:1]),
                        reads=sttb + [epsb], writes=sttb)
                    add("dve", lambda h: h.reciprocal(stt[:, :, 7], stt[:, :, 7]), reads=sttb, writes=sttb)
                    sp_banks = []
                    for tq in range(4):
                        vi = tq
                        add("dve", lambda h, tq=tq, vi=vi: h.tensor_scalar(vln[vi], vsb[tq], stt[:, tq, 6:7], stt[:, tq, 7:8], ALU.subtract, ALU.mult),
                            reads=[vsbb[tq], sttb[tq]], writes=[vlnb[vi]])
                        b2 = self.next_bank()
                        sp_banks.append(b2)
                        for g_ in range(4):
                            add("pe", lambda h, b2=b2, g_=g_, vi=vi: h.matmul(
                                ps[:, b2, g_ * 128:(g_ + 1) * 128], vln[vi][:, g_ * 128:(g_ + 1) * 128], wsT[:, g_, :],
                                start=True, stop=True, skip_group_check=True),
                                reads=[vlnb[vi], wsTb], writes=[pbuf[b2]])
                    for tq in range(4):
                        t0 = n * TT + tq * 128
                        b2 = sp_banks[tq]
                        di = self.rot("dtm", 2)
                        add("dve", lambda h, b2=b2, di=di: h.tensor_tensor(
                            dtm[di], ps[:, b2, :].rearrange("p (g t) -> p g t", g=4), Gt[:], ALU.mult),
                            reads=[pbuf[b2], gbtb_], writes=[dtmb[di]])
                        add("pool", lambda h, di=di: h.tensor_tensor(
                            dtm[di].rearrange("p g t -> p (g t)"), dtm[di].rearrange("p g t -> p (g t)"),
                            Bt[:].rearrange("p g t -> p (g t)"), ALU.add),
                            reads=[dtmb[di], gbtb_], writes=[dtmb[di]])
                        add("dve", lambda h, di=di, tq=tq, t0=t0: h.tensor_tensor(
                            mixcat[:, 4:8, t0:t0 + 128], dtm[di], usb[:, :, tq * 128:(tq + 1) * 128], ALU.mult),
                            reads=[dtmb[di]] + usbb, writes=[mcb[4 + g_][n] for g_ in range(4)])
                proj_add([("wo1", 0), ("wo1", 1)], mixcat, mcb, tail=self.tails.get("l1"))

            en = [cfg.get("l0mix", True), cfg.get("ffn0", True), cfg.get("l1mix", True), cfg.get("ffn1", True)]
            fuse = cfg.get("fuse_tails", True)
            self.tails = {}
            if fuse and en[0] and en[1]:
                self.tails["l0"] = lambda n: rmsnorm_tile(2, n)
            if fuse and en[2] and en[3]:
                self.tails["l1"] = lambda n: rmsnorm_tile(3, n)
            if en[0]:
                load_x_to_hT(after_n=lambda n: rmsnorm_tile(0, n))
                l0_mixer()
            else:
                load_x_to_hT()
            if en[1]:
                t0_ = (lambda n: rmsnorm_tile(1, n)) if (fuse and en[2]) else None
                ffn(0, prenormed=("l0" in self.tails), tail=t0_)
            if en[2]:
                l1_mixer(prenormed=(fuse and en[1]))
            fused_final = False
            if en[3]:
                fused_final = fuse
                ffn(1, prenormed=("l1" in self.tails), tail=((lambda n: final_store(only=n)) if fuse else None))
            if not fused_final:
                self.full_barrier()
                final_store()
            sc.emit(final_waits=["xt%d" % i for i in range(NXT)])
        return nc


def _consts():
    cst = np.zeros((128, CST_COLS), np.float32)
    cst[:, 0:128] = np.eye(128, dtype=np.float32)
    m = np.arange(128)
    sw = np.where((m % 64) < 32, m + 32, m - 32)
    cst[sw, 128 + m] = 1.0
    k = np.arange(128)[:, None]
    i = np.arange(128)[None, :]
    diag = (i >= k).astype(np.float32)
    nxt = (i <= k).astype(np.float32)
    m2 = np.concatenate([diag, nxt], axis=1)
    cst[:, 256:512] = m2
    cst[:, 512:768] = m2
    cst[:, 768] = 1.0
    cst[:, 771] = 1.0
    cst[:, 772:900] = (k <= i).astype(np.float32)
    cst[0, 900:964] = 1.0
    cst[1, 964:1028] = 1.0
    cst[:, 1028:1092] = 1.0
    cst[:, 1156:1220] = 1.0
    half = 32
    cos = sin = None
    try:
        import jax
        import jax.numpy as jnp
        with jax.default_device(jax.devices("cpu")[0]):
            inv_j = 10000.0 ** (-jnp.arange(half, dtype=jnp.float32) / half)
            ang_j = jnp.arange(S, dtype=jnp.float32)[:, None] * inv_j[None, :]
            cos = np.asarray(jnp.cos(ang_j), dtype=np.float32).T
            sin = np.asarray(jnp.sin(ang_j), dtype=np.float32).T
    except Exception:
        cos = sin = None
    if cos is None:
        inv = (np.float32(10000.0) ** (-np.arange(half, dtype=np.float32) / np.float32(half))).astype(np.float32)
        ang = (np.arange(S, dtype=np.float32)[:, None] * inv[None, :]).astype(np.float32)
        cos = np.cos(ang).astype(np.float32).T
        sin = np.sin(ang).astype(np.float32).T
    rope = np.zeros((128, 2, S), np.float32)
    for p in range(128):
        rope[p, 0] = cos[p % 32]
        rope[p, 1] = -sin[p % 32] if (p % 64) < 32 else sin[p % 32]
    return cst, rope


def prep_inputs(inputs):
    f = lambda a: np.ascontiguousarray(np.asarray(a, dtype=np.float32))
    g_all = np.stack([f(inputs["norm_mix_g"])[0], f(inputs["norm_mix_g"])[1],
                      f(inputs["norm_ffn_g"])[0], f(inputs["norm_ffn_g"])[1],
                      f(inputs["final_g"])], axis=0)
    gains = np.ascontiguousarray(g_all.reshape(5, KC, 128).transpose(2, 0, 1).reshape(128, 5 * KC))
    chunk4 = lambda v: f(v).reshape(4, 128).T
    convk_t = f(inputs["even_conv_k"])[0].reshape(31, 4, 128).transpose(2, 1, 0).reshape(128, 124)
    oddk_t = f(inputs["odd_conv_k"])[0].reshape(3, 4, 128).transpose(2, 1, 0).reshape(128, 12)
    small = np.concatenate([convk_t, chunk4(inputs["even_conv_b"][0]), chunk4(inputs["even_ln_g"][0]),
                            chunk4(inputs["even_ln_b"][0]), oddk_t, chunk4(inputs["odd_ln_g"][0]), chunk4(inputs["odd_ln_b"][0]),
                            np.zeros((128, 4), np.float32)], axis=1)
    wsT = f(inputs["odd_sg_w"])[0].transpose(2, 0, 1)
    sgb = np.broadcast_to(f(inputs["odd_sg_b"])[0][None], (128, 4, 128))
    cst, rope = _consts()
    shared = {
        "gains": gains, "ident_d": np.eye(128, dtype=np.float32), "cst_d": cst, "rope_d": rope,
        "small_d": np.ascontiguousarray(small), "sel_d": np.ascontiguousarray(cst[0:2, 900:1028]),
        "grow_d": np.ascontiguousarray(np.broadcast_to(f(inputs["final_g"])[None, :], (128, D))),
        "wsT_d": np.ascontiguousarray(wsT), "sgb_d": np.ascontiguousarray(sgb),
        "w_in0": f(inputs["even_w_in"][0]), "w_out0": f(inputs["even_w_out"][0]),
        "w_in1": f(inputs["odd_w_in"][0]), "w_out1": f(inputs["odd_w_out"][0]),
        "ffn_w1_0": f(inputs["ffn_w1"][0]), "ffn_w1_1": f(inputs["ffn_w1"][1]),
        "ffn_w2_0": f(inputs["ffn_w2"][0]), "ffn_w2_1": f(inputs["ffn_w2"][1]),
    }
    x = f(inputs["x"])
    maps = []
    for c in range(N_CORES):
        m = dict(shared)
        m["x"] = x[c]
        maps.append(m)
    return maps


_CFG = {}


def run(inputs, cfg=None, trace=False, cores=N_CORES):
    cfg = _CFG if cfg is None else cfg
    nc = Builder(cfg).build()
    maps = prep_inputs(inputs)[:cores]
    res = run_bass_kernel_spmd(nc, maps, core_ids=list(range(cores)), trace=trace)
    outs = np.stack([np.asarray(r["out"]) for r in res.results], axis=0)
    return outs, res


def kernel(**inputs):
    outs, _ = run(inputs)
    return outs.astype(np.float32)
```

```python
import numpy as np
import concourse.bass as bass
import concourse.mybir as mybir
from concourse.bass_utils import run_bass_kernel_spmd
from contextlib import ExitStack

F32 = mybir.dt.float32
BF16 = mybir.dt.bfloat16
ALU = mybir.AluOpType
AF = mybir.ActivationFunctionType

D = 1024
S = 2048
NT = 4
TT = 512
KC = 8
EPS = 1e-6
D_FF = 4096
EVEN_IN = 5632
ODD_IN = 2560
N_CORES = 8


class Buf:
    __slots__ = ("name", "w", "r", "excl")

    def __init__(self, name, excl=False):
        self.name = name
        self.w = []
        self.r = []
        self.excl = excl


class Ent:
    __slots__ = ("eng", "fn", "deps", "dma", "inc", "cum", "idx", "waits")

    def __init__(self, eng, fn, dma):
        self.eng = eng
        self.fn = fn
        self.deps = []
        self.dma = dma
        self.inc = False
        self.cum = 0
        self.idx = 0
        self.waits = []


class Sched:
    ENGS = ("pe", "act", "dve", "pool", "sp")

    def __init__(self, nc, stack):
        self.nc = nc
        self.stack = stack
        self.q = {e: [] for e in self.ENGS}
        self.esem = {e: stack.enter_context(nc.semaphore("s_" + e)) for e in self.ENGS}
        self.dsem = {}
        self.dcount = {}
        self.all_ents = []

    def _dsem(self, key):
        if key not in self.dsem:
            self.dsem[key] = self.stack.enter_context(self.nc.semaphore("d_" + key))
            self.dcount[key] = 0
        return self.dsem[key]

    def add(self, eng, fn, reads=(), writes=(), dma=None, strict=False):
        e = Ent(eng, fn, dma)
        deps = []
        for b in reads:
            deps.extend(b.w)
            if b.excl:
                deps.extend(b.r)
        for b in writes:
            deps.extend(b.w)
            deps.extend(b.r)
        seen = set()
        for d in deps:
            if id(d) in seen:
                continue
            seen.add(id(d))
            if d.dma is None and d.eng == eng and dma is None and not strict:
                if eng == "pe":
                    continue
                is_raw = any(d in b.w for b in reads)
                if not is_raw:
                    continue
            e.deps.append(d)
        for b in writes:
            b.w = [e]
            b.r = []
        for b in reads:
            if b.excl:
                b.w = [e]
                b.r = []
            else:
                b.r.append(e)
        e.idx = len(self.q[eng])
        self.q[eng].append(e)
        if dma is not None:
            self._dsem(dma)
            self.dcount[dma] += 16
            e.cum = self.dcount[dma]
        self.all_ents.append(e)
        return e

    def barrier(self, bufs):
        last = [self.q[e][-1] for e in self.ENGS if self.q[e]]
        for b in bufs:
            b.w = list(last)
            b.r = []

    def finalize(self):
        for e in self.all_ents:
            for d in e.deps:
                if d.dma is None:
                    d.inc = True
        for eng in self.ENGS:
            c = 0
            for e in self.q[eng]:
                if e.dma is None:
                    if e.inc:
                        c += 1
                    e.cum = c
        for eng in self.ENGS:
            seen = {}
            for e in self.q[eng]:
                need = {}
                for d in e.deps:
                    key = ("d", d.dma) if d.dma is not None else ("e", d.eng)
                    if d.cum > need.get(key, 0):
                        need[key] = d.cum
                for key, v in need.items():
                    if seen.get(key, 0) >= v:
                        continue
                    seen[key] = v
                    sem = self.dsem[key[1]] if key[0] == "d" else self.esem[key[1]]
                    e.waits.append((sem, v))

    def emit(self, final_waits):
        nc = self.nc
        self.finalize()

        def run(eng, h):
            for e in self.q[eng]:
                for sem, v in e.waits:
                    h.wait_ge(sem, v)
                ins = e.fn(h)
                if e.dma is not None:
                    ins.then_inc(self.dsem[e.dma], 16)
                elif e.inc:
                    ins.then_inc(self.esem[eng], 1)
            if eng == "sp":
                for key in final_waits:
                    h.wait_ge(self.dsem[key], self.dcount[key])

        with nc.Block() as block:
            @block.tensor
            def _(h):
                run("pe", h)

            @block.scalar
            def _(h):
                run("act", h)

            @block.vector
            def _(h):
                run("dve", h)

            @block.gpsimd
            def _(h):
                run("pool", h)

            @block.sync
            def _(h):
                run("sp", h)


HD = 64
DIL = (1, 4, 16)
NBLK = (16, 4, 1)
CST_COLS = 128 + 128 + 512 + 4 + 128 + 128 + 192
GELU_K = 1.5957691216057308


def _prod(xs):
    r = 1
    for v in xs:
        r *= v
    return r


class Arena:
    def __init__(self, ap32):
        self.ap = ap32
        self.n = ap32.shape[1]
        self.off = 0

    def reset(self, off=0):
        self.off = off

    def get(self, shape, dt):
        n = _prod(shape)
        nb = n * (4 if dt == F32 else 2)
        n32 = (nb + 31) // 32 * 8
        assert self.off + n32 <= self.n, ("arena overflow", self.off, n32, self.n)
        v = self.ap[:, self.off:self.off + n32]
        self.off += n32
        if dt != F32:
            v = v.bitcast(dt)
        v = v[:, 0:n]
        if len(shape) == 2:
            v = v.rearrange("p (a b) -> p a b", a=shape[0])
        elif len(shape) == 3:
            v = v.rearrange("p (a b c) -> p a b c", a=shape[0], b=shape[1])
        return v


class Builder:
    def __init__(self, cfg):
        self.cfg = cfg
        self.nc = bass.Bass("TRN2", target_bir_lowering=False)
        self.stack = ExitStack()
        self.sc = None
        self.bank_rr = {}
        self.pools = {"gen": list(range(8))}
        self.plan = []
        self.plan_i = 0
        self.issued = 0
        self.NW = 3
        self.x_gate = None
        self.live_lo = 0
        self.rr = {}

    def dram_in(self, name, shape):
        return self.nc.dram_tensor(name, list(shape), F32, kind="ExternalInput").ap()

    def sb(self, name, shape, dt):
        return self.stack.enter_context(self.nc.sbuf_tensor(name, list(shape), dt))

    def next_bank(self, pool="gen"):
        lst = self.pools[pool]
        i = self.bank_rr.get(pool, 0)
        self.bank_rr[pool] = (i + 1) % len(lst)
        return lst[i]

    def rot(self, key, n):
        i = self.rr.get(key, 0)
        self.rr[key] = (i + 1) % n
        return i

    def add(self, *a, **k):
        return self.sc.add(*a, **k)

    def _issue(self, upto):
        while self.issued < min(upto, len(self.plan)) and self.issued - self.NW < self.live_lo:
            i = self.issued
            s = i % self.NW
            src = self.plan[i][1].rearrange("(kc p) c -> p kc c", p=128)
            dst = self.wring[s]
            ent = self.add("pool", lambda h, dst=dst, src=src: h.dma_start(out=dst[:], in_=src),
                           writes=[self.wbuf[s]], dma="w%d" % s)
            if i == 0 and self.x_gate is not None and self.cfg.get("xgate", True):
                ent.deps = ent.deps + [self.x_gate]
            self.issued += 1

    def use_block(self, key, group_start=True):
        i = self.plan_i
        assert self.plan[i][0] == key, (self.plan[i][0], key)
        if group_start:
            self.live_lo = i
        self._issue(i + 3)
        self.plan_i += 1
        return i % self.NW

    def full_barrier(self):
        sc = self.sc
        last = {e: (sc.q[e][-1] if sc.q[e] else None) for e in sc.ENGS}
        for e in sc.ENGS:
            ent = sc.add(e, lambda h: h.nop())
            ent.deps = [last[o] for o in sc.ENGS if last[o] is not None]

    def build(self):
        nc = self.nc
        cfg = self.cfg
        st = self.stack
        with st:
            self.sc = sc = Sched(nc, st)
            add = self.add
            x = self.dram_in("x", [S, D])
            gains = self.dram_in("gains", [128, 5 * KC])
            identd = self.dram_in("ident_d", [128, 128])
            cst_d = self.dram_in("cst_d", [128, CST_COLS])
            w_in0 = self.dram_in("w_in0", [D, EVEN_IN])
            w_out0 = self.dram_in("w_out0", [D, D])
            w_in1 = self.dram_in("w_in1", [D, ODD_IN])
            w_out1 = self.dram_in("w_out1", [D, D])
            w1 = [self.dram_in("ffn_w1_%d" % l, [D, D_FF]) for l in range(2)]
            w2 = [self.dram_in("ffn_w2_%d" % l, [D_FF, D]) for l in range(2)]
            small_d = self.dram_in("small_d", [128, 160])
            sel_d = self.dram_in("sel_d", [2, 128])
            grow_d = self.dram_in("grow_d", [128, D])
            rope_d = self.dram_in("rope_d", [128, 2, S])
            wsT_d = self.dram_in("wsT_d", [128, 4, 128])
            sgb_d = self.dram_in("sgb_d", [128, 4, 128])
            out = nc.dram_tensor("out", [S, D], F32, kind="ExternalOutput").ap()

            plan = []
            if cfg.get("l0mix", True):
                if 'a' in cfg.get('l0p', 'ab'):
                    plan += [(("a", 0), w_in0[:, 0:512]), (("a", 1), w_in0[:, 512:1024])]
                plan += [(("wo0", h), w_out0[:, h * 512:(h + 1) * 512]) for h in range(2)]
            def ffn_plan(l):
                r = []
                for fg in range(4):
                    for blk in range(2):
                        c0 = fg * 1024 + blk * 512
                        r.append((("w1", l, fg, blk), w1[l][:, c0:c0 + 512]))
                    for half in range(2):
                        r.append((("w2", l, fg, half), w2[l][fg * 1024:(fg + 1) * 1024, half * 512:(half + 1) * 512]))
                return r
            if cfg.get("ffn0", True):
                plan += ffn_plan(0)
            if cfg.get("l1mix", True):
                plan += [(("i1", j), w_in1[:, j * 512:(j + 1) * 512]) for j in range(5)]
                plan += [(("wo1", h), w_out1[:, h * 512:(h + 1) * 512]) for h in range(2)]
            if cfg.get("ffn1", True):
                plan += ffn_plan(1)
            self.plan = plan

            ps = st.enter_context(nc.psum_tensor("ps", [128, 8, 512], F32))
            self.ps = ps
            pbuf = self.pbuf = [Buf("bank%d" % b, excl=True) for b in range(8)]
            arena_t = self.sb("arena", [128, KC * S], F32)
            hT = arena_t[:, :].rearrange("p (k t) -> p k t", k=KC)
            ar = Arena(arena_t[:, :])
            hTb = [[Buf("hT%d_%d" % (k, n)) for n in range(NT)] for k in range(KC)]
            hn = self.sb("hn", [128, KC, S], BF16)
            hnb = [[Buf("hn%d_%d" % (k, n)) for n in range(NT)] for k in range(KC)]
            mixcat = self.sb("mixcat", [128, KC, S], BF16)
            mcb = [[Buf("mc%d_%d" % (k, n)) for n in range(NT)] for k in range(KC)]
            u = mixcat
            ub = mcb
            self.wring = [self.sb("wr%d" % s_, [128, 8, 512], BF16) for s_ in range(self.NW)]
            self.wbuf = [Buf("wr%d" % s_) for s_ in range(self.NW)]
            arena2_t = self.sb("arena2", [128, 7680], F32)
            ar2 = Arena(arena2_t[:, :])
            ident = self.sb("ident", [128, 128], F32)
            identb = Buf("ident")
            cst = self.sb("cst", [128, CST_COLS], BF16)
            cstb = Buf("cst")
            identh = cst[:, 0:128]
            pswap = cst[:, 128:256]
            mask2 = cst[:, 256:768].rearrange("p (h q) -> p h q", h=2)
            Emat = cst[:, 768:772]
            trilT = cst[:, 772:900]
            sel = cst[0:2, 900:1028]
            OZ = cst[:, 1028:1220]
            ones_m = self.sb("ones_m", [128, 128], BF16)
            ones5 = self.sb("ones5", [128, 128], BF16)
            ones1 = self.sb("ones1", [128, 128], BF16)
            onesb = Buf("ones")
            g_sb = self.sb("g_sb", [128, 5 * KC], F32)
            gb_ = Buf("g")
            small = self.sb("small", [128, 160], F32)
            smallb = Buf("small")
            sel32 = self.sb("sel32", [2, 128], F32)
            sel32b = Buf("sel32")
            eps_sb = self.sb("eps_sb", [128, 1], F32)
            epsb = Buf("eps")
            NXT = 6
            xt = [arena2_t[:, i * D:(i + 1) * D] for i in range(NXT)]
            xtb = [Buf("xt%d" % i) for i in range(NXT)]
            ot, otb = xt, xtb
            sq = [self.sb("sq%d" % i, [128, TT], BF16) for i in range(3)]
            sqb = [Buf("sq%d" % i) for i in range(3)]
            rstd = [self.sb("rstd%d" % i, [128, TT], F32) for i in range(2)]
            rstdb = [Buf("rstd%d" % i) for i in range(2)]
            rl = [self.sb("rl%d" % i, [128, TT], BF16) for i in range(3)]
            rlb = [Buf("rl%d" % i) for i in range(3)]
            fss = [self.sb("fss%d" % i, [128, 4], F32) for i in range(2)]
            fssb = [Buf("fss%d" % i) for i in range(2)]
            fjunk = [rl[i] for i in range(3)]
            fjunkb = [rlb[i] for i in range(3)]
            grow = arena2_t[:, 4096:4096 + D]
            growb = Buf("grow")
            self.fin_ready = False

            add("sp", lambda h: h.dma_start(out=g_sb[:], in_=gains), writes=[gb_], dma="g")
            add("sp", lambda h: h.dma_start(out=ident[:], in_=identd), writes=[identb], dma="id")
            add("sp", lambda h: h.dma_start(out=small[:], in_=small_d), writes=[smallb], dma="sm")
            add("sp", lambda h: h.dma_start(out=sel32[:], in_=sel_d), writes=[sel32b], dma="sel")
            add("pool", lambda h: h.dma_start(out=cst[:], in_=cst_d), writes=[cstb], dma="cst")
            add("dve", lambda h: h.memset(ones_m[:], 1.0 / 1024.0), writes=[onesb])
            add("dve", lambda h: h.memset(ones5[:], 1.0 / 512.0), writes=[onesb])
            add("dve", lambda h: h.memset(ones1[:], 1.0), writes=[onesb])
            add("dve", lambda h: h.memset(eps_sb[:], EPS), writes=[epsb])

            wsT = self.sb("wsT", [128, 4, 128], BF16)
            wsTb = Buf("wsT")
            Gt = self.sb("Gt", [128, 4, 128], F32)
            Bt = self.sb("Bt", [128, 4, 128], F32)
            gbtb_ = Buf("GtBt")
            diag3 = self.sb("diag3", [128, 4, 3, 128], BF16)
            diag3b = Buf("diag3")
            if cfg.get("l1mix", True):
                oddk = small[:, 136:148].rearrange("p (c j) -> p c j", c=4)
                oddg = small[:, 148:152]
                oddb = small[:, 152:156]
                sgb = arena2_t[:, 7040:7040 + 512].rearrange("p (g t) -> p g t", g=4)
                sgbb = Buf("sgb")
                add("pool", lambda h: h.dma_start(out=wsT[:], in_=wsT_d), writes=[wsTb], dma="wsT")
                add("sp", lambda h: h.dma_start(out=sgb, in_=sgb_d), writes=[sgbb], dma="sgb")
                add("dve", lambda h: h.tensor_tensor(wsT[:], wsT[:], trilT.unsqueeze(1).broadcast_to([128, 4, 128]), ALU.mult),
                    reads=[cstb], writes=[wsTb])
                bws = self.next_bank()
                add("pe", lambda h, bws=bws: h.matmul(ps[:, bws, :], ones1[:], wsT[:].rearrange("p g t -> p (g t)"), start=True, stop=True),
                    reads=[wsTb, onesb], writes=[pbuf[bws]])
                for g_ in range(4):
                    add("dve", lambda h, g_=g_, bws=bws: h.scalar_tensor_tensor(
                        Bt[:, g_, :], ps[:, bws, g_ * 128:(g_ + 1) * 128], oddb[:, g_:g_ + 1], sgb[:, g_, :], ALU.mult, ALU.add),
                        reads=[pbuf[bws], smallb, sgbb], writes=[gbtb_])
                add("dve", lambda h: h.tensor_copy(Gt[:], oddg.unsqueeze(2).broadcast_to([128, 4, 128])), reads=[smallb], writes=[gbtb_])
                ckb3 = arena2_t[:, 7552:7552 + 8].bitcast(BF16)[:, 0:12].rearrange("p (c j) -> p c j", c=4)
                ckb3b = Buf("ckb3")
                add("dve", lambda h: h.tensor_copy(ckb3, oddk), reads=[smallb], writes=[ckb3b])
                for c in range(4):
                    add("dve", lambda h, c=c: h.tensor_tensor(
                        diag3[:, c, :, :], identh.unsqueeze(1).broadcast_to([128, 3, 128]),
                        ckb3[:, c, :].unsqueeze(2).broadcast_to([128, 3, 128]), ALU.mult),
                        reads=[cstb, ckb3b], writes=[diag3b])

            def load_x_to_hT(after_n=None, prefetched=0):
                for t in range(16):
                    i = t % NXT
                    if t >= prefetched:
                        ent_x = add("sp", lambda h, i=i, t=t: h.dma_start(out=xt[i], in_=x[t * 128:(t + 1) * 128, :]),
                                    writes=[xtb[i]], dma="xt%d" % i)
                        if t == 3 and self.x_gate is None:
                            self.x_gate = ent_x
                    n = t // 4
                    for half in range(2):
                        b = self.next_bank()
                        for j in range(4):
                            k = half * 4 + j
                            add("pe", lambda h, b=b, j=j, k=k, i=i: h.transpose(
                                ps[:, b, j * 128:(j + 1) * 128], xt[i][:, k * 128:(k + 1) * 128], ident[:]),
                                reads=[xtb[i], identb], writes=[pbuf[b]])
                        dstv = hT[:, half * 4:half * 4 + 4, t * 128:(t + 1) * 128]
                        srcv = ps[:, b, :].rearrange("p (j c) -> p j c", j=4)
                        wr = [hTb[half * 4 + j][n] for j in range(4)]
                        if half == 0:
                            add("act", lambda h, dstv=dstv, srcv=srcv: h.copy(dstv, srcv), reads=[pbuf[b]], writes=wr)
                        else:
                            add("dve", lambda h, dstv=dstv, srcv=srcv: h.tensor_copy(dstv, srcv), reads=[pbuf[b]], writes=wr)
                    if after_n is not None and t % 4 == 3 and t // 4 >= 1:
                        after_n(t // 4 - 1)
                if after_n is not None:
                    after_n(3)

            def rstd_from_bank(b, ri):
                if cfg.get("lnexp", True):
                    add("act", lambda h, b=b, ri=ri: h.activation(rstd[ri][:], ps[:, b, :], AF.Ln, bias=eps_sb[:, 0:1]),
                        reads=[pbuf[b], epsb], writes=[rstdb[ri]])
                    add("act", lambda h, ri=ri: h.activation(rstd[ri][:], rstd[ri][:], AF.Exp, scale=-0.5),
                        reads=[rstdb[ri]], writes=[rstdb[ri]])
                else:
                    add("act", lambda h, b=b, ri=ri: h.activation(rstd[ri][:], ps[:, b, :], AF.Sqrt, bias=eps_sb[:, 0:1]),
                        reads=[pbuf[b], epsb], writes=[rstdb[ri]])
                    add("dve", lambda h, ri=ri: h.reciprocal(rstd[ri][:], rstd[ri][:]),
                        reads=[rstdb[ri]], writes=[rstdb[ri]])

            def sumsq_bank(n):
                tsl = slice(n * TT, (n + 1) * TT)
                b = self.next_bank()
                for k in range(KC):
                    i = self.rot("sq", 3)
                    if k % 2 == 0:
                        add("act", lambda h, i=i, k=k, tsl=tsl: h.activation(sq[i][:], hT[:, k, tsl], AF.Square),
                            reads=[hTb[k][n]], writes=[sqb[i]])
                    else:
                        add("pool", lambda h, i=i, k=k, tsl=tsl: h.tensor_tensor(sq[i][:], hT[:, k, tsl], hT[:, k, tsl], ALU.mult),
                            reads=[hTb[k][n]], writes=[sqb[i]])
                    add("pe", lambda h, b=b, i=i, k=k: h.matmul(ps[:, b, :], ones_m[:], sq[i][:],
                                                              start=(k == 0), stop=(k == KC - 1)),
                        reads=[sqb[i], onesb], writes=[pbuf[b]])
                return b

            def rmsnorm_tile(gidx, n):
                tsl = slice(n * TT, (n + 1) * TT)
                b = sumsq_bank(n)
                ri = self.rot("rstd", 2)
                rstd_from_bank(b, ri)
                for k in range(KC):
                    add("dve", lambda h, k=k, tsl=tsl, ri=ri: h.scalar_tensor_tensor(
                        hn[:, k, tsl], hT[:, k, tsl], g_sb[:, gidx * KC + k:gidx * KC + k + 1], rstd[ri][:],
                        ALU.mult, ALU.mult),
                        reads=[hTb[k][n], rstdb[ri], gb_], writes=[hnb[k][n]])

            def rmsnorm(gidx):
                for n in range(NT):
                    rmsnorm_tile(gidx, n)

            def proj_add(keys, src, srcb, tail=None):
                pend_tail = [None]
                for half in range(2):
                    s_ = self.use_block(keys[half])
                    order = [(mc, n) for mc in range(4) for n in range(NT)]
                    if half == 1 and tail is not None:
                        order = [(mc, n) for n in range(NT) for mc in range(4)]
                    for (mc, n) in order:
                        m = half * 4 + mc
                        if True:
                            tsl = slice(n * TT, (n + 1) * TT)
                            b = self.next_bank()
                            for fc in range(8):
                                add("pe", lambda h, b=b, s_=s_, fc=fc, mc=mc, tsl=tsl: h.matmul(
                                    ps[:, b, :], self.wring[s_][:, fc, mc * 128:(mc + 1) * 128], src[:, fc, tsl],
                                    start=(fc == 0), stop=(fc == 7)),
                                    reads=[self.wbuf[s_], srcb[fc][n]], writes=[pbuf[b]])
                            add("dve", lambda h, b=b, m=m, tsl=tsl: h.tensor_tensor(
                                hT[:, m, tsl], ps[:, b, :], hT[:, m, tsl], ALU.add),
                                reads=[pbuf[b], hTb[m][n]], writes=[hTb[m][n]])
                            if half == 1 and tail is not None:
                                if mc == 1 and pend_tail[0] is not None:
                                    tail(pend_tail[0])
                                    pend_tail[0] = None
                                if mc == 3:
                                    pend_tail[0] = n
                if tail is not None and pend_tail[0] is not None:
                    tail(pend_tail[0])
                    pend_tail[0] = None

            def ffn(l, prenormed=False, tail=None):
                if not prenormed:
                    rmsnorm(2 + l)
                for fg in range(4):
                    for blk in range(2):
                        s_ = self.use_block(("w1", l, fg, blk))
                        for mc in range(4):
                            fc = blk * 4 + mc
                            for n in range(NT):
                                tsl = slice(n * TT, (n + 1) * TT)
                                b = self.next_bank()
                                for k in range(KC):
                                    add("pe", lambda h, b=b, s_=s_, k=k, mc=mc, tsl=tsl: h.matmul(
                                        ps[:, b, :], self.wring[s_][:, k, mc * 128:(mc + 1) * 128], hn[:, k, tsl],
                                        start=(k == 0), stop=(k == KC - 1)),
                                        reads=[self.wbuf[s_], hnb[k][n]], writes=[pbuf[b]])
                                ri_ = self.rot("rl", 3)
                                add("act", lambda h, b=b, ri_=ri_: h.activation(rl[ri_][:], ps[:, b, :], AF.Relu),
                                    reads=[pbuf[b]], writes=[rlb[ri_]])
                                add("pool", lambda h, fc=fc, tsl=tsl, ri_=ri_: h.tensor_tensor(
                                    u[:, fc, tsl], rl[ri_][:], rl[ri_][:], ALU.mult),
                                    reads=[rlb[ri_]], writes=[ub[fc][n]])
                    proj_add([("w2", l, fg, 0), ("w2", l, fg, 1)], u, ub, tail=(tail if fg == 3 else None))

            def final_store(only=None):
                if not self.fin_ready:
                    self.fin_ready = True
                    lastq = [sc.q[e_][-1] for e_ in ("pe", "act", "dve", "pool") if sc.q[e_]]
                    ent = add("sp", lambda h: h.dma_start(out=grow, in_=grow_d), writes=[growb], dma="grow")
                    ent.deps = ent.deps + lastq
                for n in (range(NT) if only is None else [only]):
                    for tq in range(4):
                        oi = self.rot("ot", 2)
                        t0 = n * TT + tq * 128
                        si = self.rot("fss", 2)
                        banks = []
                        for half in range(2):
                            pb = self.next_bank()
                            banks.append(pb)
                            for j in range(4):
                                k = half * 4 + j
                                add("pe", lambda h, pb=pb, j=j, k=k, t0=t0: h.transpose(
                                    ps[:, pb, j * 128:(j + 1) * 128], hT[:, k, t0:t0 + 128], ident[:]),
                                    reads=[hTb[k][n], identb], writes=[pbuf[pb]])
                            ji = self.rot("fjunk", 3)
                            add("act", lambda h, pb=pb, si=si, half=half, ji=ji: h.activation(
                                fjunk[ji][:], ps[:, pb, :], AF.Square, accum_out=fss[si][:, half:half + 1]),
                                reads=[pbuf[pb], fjunkb[ji]], writes=[fssb[si], fjunkb[ji]])
                        add("dve", lambda h, si=si: h.tensor_tensor(fss[si][:, 2:3], fss[si][:, 0:1], fss[si][:, 1:2], ALU.add),
                            reads=[fssb[si]], writes=[fssb[si]])
                        add("act", lambda h, si=si: h.activation(fss[si][:, 3:4], fss[si][:, 2:3], AF.Sqrt,
                                                                 bias=eps_sb[:, 0:1], scale=1.0 / 1024.0),
                            reads=[fssb[si], epsb], writes=[fssb[si]])
                        add("dve", lambda h, si=si: h.reciprocal(fss[si][:, 3:4], fss[si][:, 3:4]), reads=[fssb[si]], writes=[fssb[si]])
                        for half in range(2):
                            pb = banks[half]
                            add("dve", lambda h, pb=pb, oi=oi, half=half, si=si: h.scalar_tensor_tensor(
                                ot[oi][:, half * 512:(half + 1) * 512], ps[:, pb, :], fss[si][:, 3:4],
                                grow[:, half * 512:(half + 1) * 512], ALU.mult, ALU.mult),
                                reads=[pbuf[pb], fssb[si], growb], writes=[otb[oi]])
                        add("sp", lambda h, oi=oi, t0=t0: h.dma_start(out=out[t0:t0 + 128, :], in_=ot[oi]),
                            reads=[otb[oi]], writes=[], dma="xt%d" % oi)

            def l0_mixer():
                convk = small[:, 0:124].rearrange("p (c j) -> p c j", c=4)
                convb = small[:, 124:128]
                lng = small[:, 128:132]
                lnb = small[:, 132:136]
                def claim(lo_el, hi_el):
                    ks = range(lo_el // S, (hi_el - 1) // S + 1)
                    return [hTb[k][n] for k in ks for n in range(NT)]
                ar.reset()
                rope = ar.get([2, S], F32)
                ropeb = Buf("rope")
                add("sp", lambda h: h.dma_start(out=rope, in_=rope_d), writes=[ropeb] + claim(0, ar.off), dma="rope")
                base_off = ar.off
                off0 = ar.off
                a_pad = ar.get([4, 30 + S], BF16)
                apclaim = claim(off0, ar.off)
                off1 = ar.off
                apb = [[Buf("ap%d_%d" % (c, n)) for n in range(NT)] for c in range(4)]
                apz = Buf("apz")
                diag = ar.get([4, 31, 128], BF16)
                diagb = Buf("diag")
                dgclaim = claim(off1, ar.off)
                ar2.reset()
                cv = [ar2.get([TT], F32) for _ in range(4)]
                cvb = [Buf("cv%d" % i) for i in range(4)]
                ybf = [ar2.get([TT], BF16) for _ in range(4)]
                ybfb = [Buf("ybf%d" % i) for i in range(4)]
                ysq = [ar2.get([TT], BF16) for _ in range(4)]
                ysqb = [Buf("ysq%d" % i) for i in range(4)]
                sig = [ar2.get([TT], F32) for _ in range(2)]
                sigb = [Buf("sig%d" % i) for i in range(2)]
                m2 = ar2.get([TT], F32)
                m2b = Buf("m2")
                lrs = ar2.get([TT], F32)
                lrsb = Buf("lrs")
                tn = [ar2.get([TT], F32) for _ in range(2)]
                tnb = [Buf("tn%d" % i) for i in range(2)]
                add("dve", lambda h: h.memset(a_pad[:, :, 0:30], 0.0), writes=[apz] + apclaim, strict=True)
                doA = 'a' in cfg.get('l0p', 'ab')
                ckb = ar2.get([4, 31], BF16)
                ckbb = Buf("ckb")
                if doA:
                    s_lin = self.use_block(("a", 0))
                    s_gate = self.use_block(("a", 1), False)
                add("dve", lambda h: h.tensor_copy(ckb, convk), reads=[smallb], writes=[ckbb])
                def build_diag(c):
                    add("dve", lambda h, c=c: h.tensor_tensor(
                        diag[:, c, :, :], identh.unsqueeze(1).broadcast_to([128, 31, 128]),
                        ckb[:, c, :].unsqueeze(2).broadcast_to([128, 31, 128]), ALU.mult),
                        reads=[cstb, ckbb], writes=[Buf("dg")] + dgclaim, strict=True)
                    diagb.w.append(sc.q["dve"][-1])
                for c in range(4 if doA else 0):
                    if c > 0:
                        build_diag(c - 1)
                    for n in range(NT):
                        tsl = slice(n * TT, (n + 1) * TT)
                        b1 = self.next_bank()
                        b2 = self.next_bank()
                        for (bb, s_) in ((b1, s_lin), (b2, s_gate)):
                            for k in range(KC):
                                add("pe", lambda h, bb=bb, s_=s_, k=k, c=c, tsl=tsl: h.matmul(
                                    ps[:, bb, :], self.wring[s_][:, k, c * 128:(c + 1) * 128], hn[:, k, tsl],
                                    start=(k == 0), stop=(k == KC - 1)),
                                    reads=[self.wbuf[s_], hnb[k][n]], writes=[pbuf[bb]])
                        si = self.rot("sig", 2)
                        add("act", lambda h, b2=b2, si=si: h.activation(sig[si], ps[:, b2, :], AF.Sigmoid),
                            reads=[pbuf[b2]], writes=[sigb[si]])
                        add("dve", lambda h, b1=b1, si=si, c=c, n=n: h.tensor_tensor(
                            a_pad[:, c, 30 + n * TT:30 + (n + 1) * TT], ps[:, b1, :], sig[si], ALU.mult),
                            reads=[pbuf[b1], sigb[si]], writes=[apb[c][n]])
                if doA:
                    build_diag(3)
                for n in range(NT if doA else 0):
                    tsl = slice(n * TT, (n + 1) * TT)
                    for c in range(4):
                        if n == 0:
                            pass
                        b = self.next_bank()
                        rd = [apb[c][n], apz, diagb] + ([apb[c][n - 1]] if n > 0 else [])
                        for j in range(31):
                            add("pe", lambda h, b=b, j=j, c=c, n=n: h.matmul(
                                ps[:, b, :], diag[:, c, j, :], a_pad[:, c, n * TT + j:n * TT + j + TT],
                                start=(j == 0), stop=(j == 30)),
                                reads=rd, writes=[pbuf[b]])
                        add("act", lambda h, b=b, c=c: h.activation(cv[c], ps[:, b, :], AF.Identity, bias=convb[:, c:c + 1]),
                            reads=[pbuf[b], smallb], writes=[cvb[c]])
                        add("pool", lambda h, c=c: h.tensor_copy(ybf[c], cv[c]), reads=[cvb[c]], writes=[ybfb[c]])
                        add("act", lambda h, c=c: h.activation(ysq[c], cv[c], AF.Square), reads=[cvb[c]], writes=[ysqb[c]])
                    bm = self.next_bank()
                    bq = self.next_bank()
                    for c in range(4):
                        add("pe", lambda h, bm=bm, c=c: h.matmul(ps[:, bm, :], ones5[:], ybf[c], start=(c == 0), stop=(c == 3)),
                            reads=[ybfb[c], onesb], writes=[pbuf[bm]])
                    for c in range(4):
                        add("pe", lambda h, bq=bq, c=c: h.matmul(ps[:, bq, :], ones5[:], ysq[c], start=(c == 0), stop=(c == 3)),
                            reads=[ysqb[c], onesb], writes=[pbuf[bq]])
                    add("act", lambda h, bm=bm: h.activation(m2, ps[:, bm, :], AF.Square), reads=[pbuf[bm]], writes=[m2b])
                    add("dve", lambda h, bq=bq: h.tensor_tensor(lrs, ps[:, bq, :], m2, ALU.subtract),
                        reads=[pbuf[bq], m2b], writes=[lrsb])
                    add("act", lambda h: h.activation(lrs, lrs, AF.Ln, bias=eps_sb[:, 0:1]), reads=[lrsb, epsb], writes=[lrsb])
                    add("act", lambda h: h.activation(lrs, lrs, AF.Exp, scale=-0.5), reads=[lrsb], writes=[lrsb])
                    for c in range(4):
                        ti = self.rot("tn", 2)
                        add("dve", lambda h, c=c, bm=bm, ti=ti: h.tensor_tensor(tn[ti], cv[c], ps[:, bm, :], ALU.subtract),
                            reads=[cvb[c], pbuf[bm]], writes=[tnb[ti]])
                        add("pool", lambda h, ti=ti: h.tensor_tensor(tn[ti], tn[ti], lrs, ALU.mult),
                            reads=[tnb[ti], lrsb], writes=[tnb[ti]])
                        add("act", lambda h, c=c, ti=ti, tsl=tsl: h.activation(
                            mixcat[:, c, tsl], tn[ti], AF.Silu, bias=lnb[:, c:c + 1], scale=lng[:, c:c + 1]),
                            reads=[tnb[ti], smallb], writes=[mcb[c][n]])

                self.full_barrier()
                ar.reset(base_off)
                ar2.reset()
                self.pools = {"S": [0, 1], "O": [3, 4], "DEN": [5], "gen": [2, 6, 7], "proj": [2, 6, 7, 0, 1]}
                self.bank_rr = {}
                acc_o = ar.get([S], F32)
                acc_ob = Buf("acc_o")
                den = ar.get([S], F32)
                denb = Buf("den")
                rden = den
                rdenb = denb
                qT = [ar.get([S], BF16) for _ in range(2)]
                kT = [ar.get([S], BF16) for _ in range(2)]
                qTb = [[Buf("qT%d_%d" % (i, n)) for n in range(NT)] for i in range(2)]
                kTb = [[Buf("kT%d_%d" % (i, n)) for n in range(NT)] for i in range(2)]
                vT = ar.get([S], BF16)
                vTb = [Buf("vT%d" % n) for n in range(NT)]
                vx = [ar.get([16, 192], BF16) for _ in range(2)]
                vxb = [[Buf("vx%d_%d" % (i, t)) for t in range(4)] for i in range(2)]
                vxz = [Buf("vxz%d" % i) for i in range(2)]
                wq = [ar2.get([8, 3, 128], BF16) for _ in range(2)]
                wqb = [[Buf("wq%d_%d" % (i, j)) for j in range(3)] for i in range(2)]
                zb = [ar2.get([TT], BF16) for _ in range(3)]
                zbb = [Buf("zb%d" % i) for i in range(3)]
                t1 = [ar2.get([TT], F32) for _ in range(2)]
                t1b = [Buf("t1%d" % i) for i in range(2)]
                t2 = [ar2.get([TT], F32) for _ in range(2)]
                t2b = [Buf("t2%d" % i) for i in range(2)]
                rcp = [ar2.get([TT], F32) for _ in range(2)]
                rcpb = [Buf("rcp%d" % i) for i in range(2)]
                pt = [ar2.get([TT], BF16) for _ in range(3)]
                ptb = [Buf("pt%d" % i) for i in range(3)]
                pthb = [[Buf("pth%d_%d" % (i, hh)) for hh in range(2)] for i in range(3)]
                for i in range(2):
                    add("dve", lambda h, i=i: h.memset(vx[i][:, :, 64:128], 0.0), writes=[vxz[i]])

                units = [(p, g) for p in range(4) for g in range(3)]

                def load_unit_w(ui):
                    p, g = units[ui]
                    wi = ui % 2
                    hd0 = g * 8 + 2 * p
                    for j, base in enumerate((1024, 2560, 4096)):
                        c0 = base + hd0 * HD
                        src = w_in0[:, c0:c0 + 128].rearrange("(kc q) c -> q kc c", q=128)
                        add("pool", lambda h, wi=wi, j=j, src=src: h.dma_start(out=wq[wi][:, :, j, :], in_=src),
                            writes=[wqb[wi][j]], dma="wq%d_%d" % (wi, j))

                if 'b' not in cfg.get('l0p', 'ab'):
                    units = units[:cfg.get('nunits', 0)]

                def unit_ctx(ui):
                    p, g = units[ui]
                    wi = ui % 2
                    d, nb = DIL[g], NBLK[g]
                    def tok(T, cnt=128):
                        r, jj = T // nb, T % nb
                        st0 = r + d * 128 * jj
                        return slice(st0, st0 + d * (cnt - 1) + 1, d)
                    return p, g, wi, d, nb, tok

                def proj_items(ui, pool):
                    p, g, wi, d, nb, tok = unit_ctx(ui)
                    tiles = [(j, n) for j in range(2) for n in range(NT)]
                    state = {}
                    def proj_stage(j, n):
                        tsl = slice(n * TT, (n + 1) * TT)
                        b = self.next_bank(pool)
                        for k in range(KC):
                            add("pe", lambda h, b=b, k=k, j=j, tsl=tsl: h.matmul(
                                ps[:, b, :], wq[wi][:, k, j, :], hn[:, k, tsl], start=(k == 0), stop=(k == KC - 1)),
                                reads=[wqb[wi][j], hnb[k][n]], writes=[pbuf[b]])
                        zi = self.rot("zb", 3)
                        add("act", lambda h, b=b, zi=zi: h.copy(zb[zi], ps[:, b, :]), reads=[pbuf[b]], writes=[zbb[zi]])
                        ti = self.rot("t12", 2)
                        add("dve", lambda h, b=b, ti=ti, tsl=tsl: h.tensor_tensor(t1[ti], ps[:, b, :], rope[:, 0, tsl], ALU.mult),
                            reads=[pbuf[b], ropeb], writes=[t1b[ti]])
                        return b, zi, ti
                    def swap_stage(j, n, b, zi, ti):
                        tsl = slice(n * TT, (n + 1) * TT)
                        dstT, dstb = (qT[wi], qTb[wi]) if j == 0 else (kT[wi], kTb[wi])
                        b2 = self.next_bank(pool)
                        add("pe", lambda h, b2=b2, zi=zi: h.matmul(ps[:, b2, :], pswap, zb[zi], start=True, stop=True),
                            reads=[zbb[zi], cstb], writes=[pbuf[b2]])
                        add("dve", lambda h, b2=b2, ti=ti, tsl=tsl: h.tensor_tensor(t2[ti], ps[:, b2, :], rope[:, 1, tsl], ALU.mult),
                            reads=[pbuf[b2], ropeb], writes=[t2b[ti]])
                        add("pool", lambda h, ti=ti, dstT=dstT, tsl=tsl: h.tensor_tensor(dstT[:, tsl], t1[ti], t2[ti], ALU.add),
                            reads=[t1b[ti], t2b[ti]], writes=[dstb[n]])
                    def qk_item(i):
                        def fn():
                            if i < len(tiles):
                                cur = tiles[i] + proj_stage(*tiles[i])
                            if i > 0:
                                swap_stage(*state["prev"])
                            if i < len(tiles):
                                state["prev"] = cur
                        return fn
                    def vT_item(n):
                        def fn():
                            tsl = slice(n * TT, (n + 1) * TT)
                            b = self.next_bank(pool)
                            for k in range(KC):
                                add("pe", lambda h, b=b, k=k, tsl=tsl: h.matmul(
                                    ps[:, b, :], wq[wi][:, k, 2, :], hn[:, k, tsl], start=(k == 0), stop=(k == KC - 1)),
                                    reads=[wqb[wi][2], hnb[k][n]], writes=[pbuf[b]])
                            add("act", lambda h, b=b, tsl=tsl: h.copy(vT[:, tsl], ps[:, b, :]), reads=[pbuf[b]], writes=[vTb[n]])
                        return fn
                    def v_item(T4):
                        def fn():
                            b = self.next_bank(pool)
                            psb = ps[:, b, :].bitcast(BF16)
                            for tt_ in range(4):
                                T = T4 * 4 + tt_
                                tk = tok(T)
                                add("pe", lambda h, psb=psb, tk=tk, tt_=tt_: h.transpose(
                                    psb[:, tt_ * 128:(tt_ + 1) * 128], vT[:, tk], identh),
                                    reads=vTb + [cstb], writes=[pbuf[b]])
                            dstv = vx[wi][:, T4 * 4:T4 * 4 + 4, :].rearrange("p t (x c) -> p t x c", c=64)[:, :, 0:3:2, :]
                            srcv = psb[:, 0:512].rearrange("p (t x c) -> p t x c", t=4, x=2)
                            add("act", lambda h, dstv=dstv, srcv=srcv: h.copy(dstv, srcv), reads=[pbuf[b]], writes=[vxb[wi][T4]])
                        return fn
                    return ([qk_item(i) for i in range(len(tiles) + 1)] + [vT_item(n) for n in range(NT)]
                            + [v_item(T4) for T4 in range(4)])

                def make_norm(p):
                    def fn(ns=range(NT)):
                        for n in ns:
                            tsl = slice(n * TT, (n + 1) * TT)
                            ti = self.rot("rcp", 2)
                            if cfg.get("lnexp", True):
                                add("act", lambda h, ti=ti, tsl=tsl: h.activation(rcp[ti], den[:, tsl], AF.Ln),
                                    reads=[denb], writes=[rcpb[ti]])
                                add("act", lambda h, ti=ti: h.activation(rcp[ti], rcp[ti], AF.Exp, scale=-1.0),
                                    reads=[rcpb[ti]], writes=[rcpb[ti]])
                            else:
                                add("dve", lambda h, ti=ti, tsl=tsl: h.reciprocal(rcp[ti], den[:, tsl]),
                                    reads=[denb], writes=[rcpb[ti]])
                            add("dve", lambda h, ti=ti, tsl=tsl, p=p: h.tensor_tensor(
                                mixcat[:, 4 + p, tsl], acc_o[:, tsl], rcp[ti], ALU.mult),
                                reads=[rcpb[ti], acc_ob], writes=[mcb[4 + p][n]])
                    return fn

                def attention(ui, extra, pending):
                    p, g, wi, d, nb, tok = unit_ctx(ui)
                    def evac(Bk, ob, db):
                        if g == 0:
                            dsl = slice(Bk * 512, (Bk + 1) * 512)
                            ov, dv = acc_o[:, dsl], den[:, dsl]
                            pso, psd = ps[:, ob, :], ps[:, db, :]
                        elif g == 1:
                            dsl = slice(Bk, Bk + 4 * 511 + 1, 4)
                            ov, dv = acc_o[:, dsl], den[:, dsl]
                            pso, psd = ps[:, ob, :], ps[:, db, :]
                        else:
                            ov = acc_o.rearrange("p (i r) -> p r i", r=16)[:, 4 * Bk:4 * Bk + 4, :]
                            dv = den.rearrange("p (i r) -> p r i", r=16)[:, 4 * Bk:4 * Bk + 4, :]
                            pso = ps[:, ob, :].rearrange("p (r i) -> p r i", r=4)
                            psd = ps[:, db, :].rearrange("p (r i) -> p r i", r=4)
                        if g == 0:
                            add("act", lambda h, dv=dv, psd=psd: h.copy(dv, psd), reads=[pbuf[db]], writes=[denb])
                            add("act", lambda h, ov=ov, pso=pso: h.copy(ov, pso), reads=[pbuf[ob]], writes=[acc_ob])
                        else:
                            add("dve", lambda h, dv=dv, psd=psd: h.tensor_tensor(dv, psd, dv, ALU.add),
                                reads=[pbuf[db], denb], writes=[denb])
                            add("dve", lambda h, ov=ov, pso=pso: h.tensor_tensor(ov, pso, ov, ALU.add),
                                reads=[pbuf[ob], acc_ob], writes=[acc_ob])

                    def qk_stage(T):
                        r, jj = T // nb, T % nb
                        nqb = 2 if jj + 1 < nb else 1
                        nq = 128 * nqb
                        ktk = tok(T)
                        qtk = tok(T, nq)
                        qn = sorted(set(range(qtk.start // TT, (qtk.stop - 1) // TT + 1)))
                        kn = sorted(set(range(ktk.start // TT, (ktk.stop - 1) // TT + 1)))
                        pi = self.rot("pt", 3)
                        sbs = [self.next_bank("S"), self.next_bank("S")]
                        for hh in range(2):
                            sb_ = sbs[hh]
                            add("pe", lambda h, sb_=sb_, hh=hh, ktk=ktk, qtk=qtk, nq=nq: h.matmul(
                                ps[:, sb_, 0:nq], kT[wi][hh * 64:(hh + 1) * 64, ktk],
                                qT[wi][hh * 64:(hh + 1) * 64, qtk], start=True, stop=True),
                                reads=[kTb[wi][n_] for n_ in kn] + [qTb[wi][n_] for n_ in qn], writes=[pbuf[sb_]])
                        for hh in range(2):
                            sb_ = sbs[hh]
                            add("act", lambda h, sb_=sb_, pi=pi, nq=nq, hh=hh: h.activation(
                                pt[pi][:, hh * nq:(hh + 1) * nq], ps[:, sb_, 0:nq], AF.Exp, scale=0.125),
                                reads=[pbuf[sb_]], writes=[pthb[pi][hh]])
                        pv = pt[pi][:, 0:2 * nq].rearrange("p (h q) -> p h q", h=2)
                        add("dve", lambda h, pv=pv, nq=nq: h.tensor_tensor(pv, pv, mask2[:, :, 0:nq], ALU.mult),
                            reads=[cstb], writes=[pthb[pi][0], pthb[pi][1]])
                        return pi, nq, nqb

                    obdb = [None, None]
                    def pv_stage(T, pi, nq, nqb):
                        r, jj = T // nb, T % nb
                        if nqb == 2 and (T % 4) != 3 and cfg.get("pv256", True):
                            slot = T % 4
                            if slot == 0 and jj == 0:
                                obdb[0] = self.next_bank("O")
                                obdb[1] = self.next_bank("DEN")
                            ob, db = obdb
                            for hh in range(2):
                                stf = (slot == 0 and jj == 0 and hh == 0)
                                add("pe", lambda h, ob=ob, slot=slot, hh=hh, T=T, pi=pi, nq=nq, stf=stf: h.matmul(
                                    ps[:, ob, slot * 128:slot * 128 + 256], vx[wi][:, T, hh * 64:hh * 64 + 128],
                                    pt[pi][:, hh * nq:hh * nq + 256],
                                    start=stf, stop=False, skip_group_check=True),
                                    reads=[vxb[wi][T // 4], vxz[wi], pthb[pi][hh]], writes=[pbuf[ob]])
                                add("pe", lambda h, db=db, slot=slot, hh=hh, pi=pi, nq=nq, stf=stf: h.matmul(
                                    ps[:, db, slot * 128:slot * 128 + 256], OZ[:, hh * 64:hh * 64 + 128],
                                    pt[pi][:, hh * nq:hh * nq + 256],
                                    start=stf, stop=False, skip_group_check=True),
                                    reads=[pthb[pi][hh], cstb], writes=[pbuf[db]])
                            return
                        for qb in range(nqb):
                            Bq = T + qb
                            slot = Bq % 4
                            fresh_block = (qb == 1) or (jj == 0)
                            if slot == 0 and fresh_block:
                                obdb[0] = self.next_bank("O")
                                obdb[1] = self.next_bank("DEN")
                            ob, db = obdb
                            for hh in range(2):
                                stf = (slot == 0 and fresh_block and hh == 0)
                                add("pe", lambda h, ob=ob, slot=slot, hh=hh, T=T, pi=pi, nq=nq, qb=qb, stf=stf: h.matmul(
                                    ps[:, ob, slot * 128:(slot + 1) * 128], vx[wi][:, T, hh * 64:hh * 64 + 128],
                                    pt[pi][:, hh * nq + qb * 128:hh * nq + qb * 128 + 128],
                                    start=stf, stop=False, skip_group_check=True),
                                    reads=[vxb[wi][T // 4], vxz[wi], pthb[pi][hh]], writes=[pbuf[ob]])
                                add("pe", lambda h, db=db, slot=slot, hh=hh, pi=pi, nq=nq, qb=qb, stf=stf: h.matmul(
                                    ps[:, db, slot * 128:(slot + 1) * 128], OZ[:, hh * 64:hh * 64 + 128],
                                    pt[pi][:, hh * nq + qb * 128:hh * nq + qb * 128 + 128],
                                    start=stf, stop=False, skip_group_check=True),
                                    reads=[pthb[pi][hh], cstb], writes=[pbuf[db]])
                            if qb == 0 and slot == 3:
                                evac(Bq // 4, ob, db)

                    extra = list(extra)
                    if cfg.get('nointer'):
                        while extra:
                            extra.pop(0)()
                    prev = None
                    for T in range(16):
                        cur = qk_stage(T)
                        if prev is not None:
                            pv_stage(T - 1, *prev)
                        prev = cur
                        if T % 4 == 1 and pending is not None:
                            pending([T // 4])
                        if extra:
                            extra.pop(0)()
                    pv_stage(15, *prev)
                    while extra:
                        extra.pop(0)()

                if units:
                    load_unit_w(0)
                    if len(units) > 1:
                        load_unit_w(1)
                    for it in proj_items(0, "proj"):
                        it()
                    pending = None
                    for ui, (p, g) in enumerate(units):
                        if ui == len(units) - 1 and len(units) > 2 and cfg.get("xpre", True):
                            lastq = [sc.q[e_][-1] for e_ in ("pe", "act", "dve", "pool") if sc.q[e_]]
                            for t in range(5):
                                ent = add("sp", lambda h, t=t: h.dma_start(out=xt[t], in_=x[t * 128:(t + 1) * 128, :]),
                                          writes=[xtb[t]], dma="xt%d" % t)
                                ent.deps = ent.deps + lastq
                            self.x_prefetched = 5
                        if ui + 2 < len(units):
                            load_unit_w(ui + 2)
                        extra = proj_items(ui + 1, "gen") if ui + 1 < len(units) else []
                        attention(ui, extra, pending)
                        pending = make_norm(p) if g == 2 else None
                    if pending is not None:
                        pending()
                self.pools = {"gen": list(range(8))}
                self.bank_rr = {}
                self.full_barrier()
                load_x_to_hT(prefetched=getattr(self, "x_prefetched", 0))
                proj_add([("wo0", 0), ("wo0", 1)], mixcat, mcb, tail=self.tails.get("l0"))

            def l1_mixer(prenormed=False):
                if not prenormed:
                    rmsnorm(1)
                ar2.reset()
                mark = ar2.off
                ppad = [ar2.get([2 + S], BF16) for _ in range(2)]
                ppb = [[Buf("pp%d_%d" % (i, n)) for n in range(NT)] for i in range(2)]
                ppz = [Buf("ppz%d" % i) for i in range(2)]
                gct = [ar2.get([TT], F32) for _ in range(2)]
                gctb = [Buf("gct%d" % i) for i in range(2)]
                gbt = [ar2.get([TT], F32) for _ in range(3)]
                gbtb = [Buf("gbt%d" % i) for i in range(3)]
                cend = ar2.off
                ar2.reset(mark)
                usb = ar2.get([4, TT], F32)
                usbb = [Buf("usb%d" % c) for c in range(4)]
                ga = [ar2.get([TT], F32) for _ in range(2)]
                gab = [Buf("ga%d" % i) for i in range(2)]
                vsb = [ar2.get([512], F32) for _ in range(4)]
                vsbb = [Buf("vsb%d" % i) for i in range(4)]
                vln = [ar2.get([512], BF16) for _ in range(4)]
                vlnb = [Buf("vln%d" % i) for i in range(4)]
                stt = ar2.get([4, 8], F32)
                sttb = [Buf("stt%d" % i) for i in range(4)]
                dtm = [ar2.get([4, 128], F32) for _ in range(2)]
                dtmb = [Buf("dtm%d" % i) for i in range(2)]
                for i in range(2):
                    add("pool", lambda h, i=i: h.memset(ppad[i][:, 0:2], 0.0), writes=[ppz[i]])

                def gelu(eng_out_fn, b, outv, reads_extra, wbufs):
                    gi = self.rot("ga", 2)
                    add("act", lambda h, b=b, gi=gi: h.activation(ga[gi], ps[:, b, :], AF.Square), reads=[pbuf[b]], writes=[gab[gi]])
                    add("pool", lambda h, gi=gi: h.tensor_scalar(ga[gi], ga[gi], 0.044715, 1.0, ALU.mult, ALU.add),
                        reads=[gab[gi]], writes=[gab[gi]])
                    add("dve", lambda h, b=b, gi=gi: h.tensor_tensor(ga[gi], ga[gi], ps[:, b, :], ALU.mult),
                        reads=[gab[gi], pbuf[b]], writes=[gab[gi]])
                    add("act", lambda h, gi=gi: h.activation(ga[gi], ga[gi], AF.Sigmoid, scale=GELU_K), reads=[gab[gi]], writes=[gab[gi]])
                    add("dve", lambda h, b=b, gi=gi, outv=outv: h.tensor_tensor(outv, ga[gi], ps[:, b, :], ALU.mult),
                        reads=[gab[gi], pbuf[b]], writes=wbufs)

                s_gb = self.use_block(("i1", 0))
                s_gc = self.use_block(("i1", 1), False)
                s_xs = self.use_block(("i1", 2), False)
                pend_c = [None]
                for c in range(4 if 'c' in cfg.get('l1p', 'cd') else 0):
                    pi_ = c % 2
                    for n in range(NT):
                        tsl = slice(n * TT, (n + 1) * TT)
                        bgb, bgc, bxs = self.next_bank(), self.next_bank(), self.next_bank()
                        for (bb, s_) in ((bgc, s_gc), (bxs, s_xs), (bgb, s_gb)):
                            for k in range(KC):
                                add("pe", lambda h, bb=bb, s_=s_, k=k, c=c, tsl=tsl: h.matmul(
                                    ps[:, bb, :], self.wring[s_][:, k, c * 128:(c + 1) * 128], hn[:, k, tsl],
                                    start=(k == 0), stop=(k == KC - 1)),
                                    reads=[self.wbuf[s_], hnb[k][n]], writes=[pbuf[bb]])
                        gi = self.rot("gct", 2)
                        add("act", lambda h, bgc=bgc, gi=gi: h.copy(gct[gi], ps[:, bgc, :]), reads=[pbuf[bgc]], writes=[gctb[gi]])
                        add("dve", lambda h, bxs=bxs, gi=gi, pi_=pi_, n=n: h.tensor_tensor(
                            ppad[pi_][:, 2 + n * TT:2 + (n + 1) * TT], gct[gi], ps[:, bxs, :], ALU.mult),
                            reads=[gctb[gi], pbuf[bxs]], writes=[ppb[pi_][n]])
                        bi = self.rot("gbt", 3)
                        add("act", lambda h, bgb=bgb, bi=bi: h.copy(gbt[bi], ps[:, bgb, :]), reads=[pbuf[bgb]], writes=[gbtb[bi]])
                        if pend_c[0] is not None:
                            pend_c[0]()
                        def conv_stage(c=c, n=n, pi_=pi_, bi=bi, tsl=tsl):
                            bc = self.next_bank()
                            rd = [ppb[pi_][n], ppz[pi_], diag3b] + ([ppb[pi_][n - 1]] if n > 0 else [])
                            for j in range(3):
                                add("pe", lambda h, bc=bc, j=j: h.matmul(
                                    ps[:, bc, :], diag3[:, c, j, :], ppad[pi_][:, n * TT + j:n * TT + j + TT],
                                    start=(j == 0), stop=(j == 2)),
                                    reads=rd, writes=[pbuf[bc]])
                            add("dve", lambda h, bc=bc: h.tensor_tensor(
                                mixcat[:, c, tsl], gbt[bi], ps[:, bc, :], ALU.mult),
                                reads=[gbtb[bi], pbuf[bc]], writes=[mcb[c][n]])
                        pend_c[0] = conv_stage
                if pend_c[0] is not None:
                    pend_c[0]()
                self.full_barrier()
                s_u = self.use_block(("i1", 3))
                s_v = self.use_block(("i1", 4), False)
                for n in range(NT if 'd' in cfg.get('l1p', 'cd') else 0):
                    tsl = slice(n * TT, (n + 1) * TT)
                    for tq in range(4):
                        t0 = n * TT + tq * 128
                        b = self.next_bank()
                        for k in range(KC):
                            add("pe", lambda h, b=b, k=k, t0=t0: h.matmul(
                                ps[:, b, :], hn[:, k, t0:t0 + 128], self.wring[s_v][:, k, :],
                                start=(k == 0), stop=(k == KC - 1)),
                                reads=[self.wbuf[s_v], hnb[k][n]], writes=[pbuf[b]])
                        add("act", lambda h, b=b, tq=tq: h.activation(vsb[tq], ps[:, b, :], AF.Gelu_apprx_tanh),
                            reads=[pbuf[b]], writes=[vsbb[tq]])
                        add("dve", lambda h, tq=tq: h.bn_stats(stt[:, tq, 0:6], vsb[tq]), reads=[vsbb[tq]], writes=[sttb[tq]])
                        add("dve", lambda h, tq=tq: h.bn_aggr(stt[:, tq, 6:8], stt[:, tq, 0:6]), reads=[sttb[tq]], writes=[sttb[tq]])
                    for c in range(4):
                        b = self.next_bank()
                        for k in range(KC):
                            add("pe", lambda h, b=b, k=k, c=c, tsl=tsl: h.matmul(
                                ps[:, b, :], self.wring[s_u][:, k, c * 128:(c + 1) * 128], hn[:, k, tsl],
                                start=(k == 0), stop=(k == KC - 1)),
                                reads=[self.wbuf[s_u], hnb[k][n]], writes=[pbuf[b]])
                        add("act", lambda h, b=b, c=c: h.activation(usb[:, c, :], ps[:, b, :], AF.Gelu_apprx_tanh),
                            reads=[pbuf[b]], writes=[usbb[c]])
                    add("act", lambda h: h.activation(stt[:, :, 7], stt[:, :, 7], AF.Sqrt, bias=eps_sb[:, 0:1]),
                        reads=sttb + [epsb], writes=sttb)
                    add("dve", lambda h: h.reciprocal(stt[:, :, 7], stt[:, :, 7]), reads=sttb, writes=sttb)
                    sp_banks = []
                    for tq in range(4):
                        vi = tq
                        add("dve", lambda h, tq=tq, vi=vi: h.tensor_scalar(vln[vi], vsb[tq], stt[:, tq, 6:7], stt[:, tq, 7:8], ALU.subtract, ALU.mult),
                            reads=[vsbb[tq], sttb[tq]], writes=[vlnb[vi]])
                        b2 = self.next_bank()
                        sp_banks.append(b2)
                        for g_ in range(4):
                            add("pe", lambda h, b2=b2, g_=g_, vi=vi: h.matmul(
                                ps[:, b2, g_ * 128:(g_ + 1) * 128], vln[vi][:, g_ * 128:(g_ + 1) * 128], wsT[:, g_, :],
                                start=True, stop=True, skip_group_check=True),
                                reads=[vlnb[vi], wsTb], writes=[pbuf[b2]])
                    for tq in range(4):
                        t0 = n * TT + tq * 128
                        b2 = sp_banks[tq]
                        di = self.rot("dtm", 2)
                        add("dve", lambda h, b2=b2, di=di: h.tensor_tensor(
                            dtm[di], ps[:, b2, :].rearrange("p (g t) -> p g t", g=4), Gt[:], ALU.mult),
                            reads=[pbuf[b2], gbtb_], writes=[dtmb[di]])
                        add("pool", lambda h, di=di: h.tensor_tensor(
                            dtm[di].rearrange("p g t -> p (g t)"), dtm[di].rearrange("p g t -> p (g t)"),
                            Bt[:].rearrange("p g t -> p (g t)"), ALU.add),
                            reads=[dtmb[di], gbtb_], writes=[dtmb[di]])
                        add("dve", lambda h, di=di, tq=tq, t0=t0: h.tensor_tensor(
                            mixcat[:, 4:8, t0:t0 + 128], dtm[di], usb[:, :, tq * 128:(tq + 1) * 128], ALU.mult),
                            reads=[dtmb[di]] + usbb, writes=[mcb[4 + g_][n] for g_ in range(4)])
                proj_add([("wo1", 0), ("wo1", 1)], mixcat, mcb, tail=self.tails.get("l1"))

            en = [cfg.get("l0mix", True), cfg.get("ffn0", True), cfg.get("l1mix", True), cfg.get("ffn1", True)]
            fuse = cfg.get("fuse_tails", True)
            self.tails = {}
            if fuse and en[0] and en[1]:
                self.tails["l0"] = lambda n: rmsnorm_tile(2, n)
            if fuse and en[2] and en[3]:
                self.tails["l1"] = lambda n: rmsnorm_tile(3, n)
            if en[0]:
                load_x_to_hT(after_n=lambda n: rmsnorm_tile(0, n))
                l0_mixer()
            else:
                load_x_to_hT()
            if en[1]:
                t0_ = (lambda n: rmsnorm_tile(1, n)) if (fuse and en[2]) else None
                ffn(0, prenormed=("l0" in self.tails), tail=t0_)
            if en[2]:
                l1_mixer(prenormed=(fuse and en[1]))
            fused_final = False
            if en[3]:
                fused_final = fuse
                ffn(1, prenormed=("l1" in self.tails), tail=((lambda n: final_store(only=n)) if fuse else None))
            if not fused_final:
                self.full_barrier()
                final_store()
            sc.emit(final_waits=["xt%d" % i for i in range(NXT)])
        return nc


def _consts():
    cst = np.zeros((128, CST_COLS), np.float32)
    cst[:, 0:128] = np.eye(128, dtype=np.float32)
    m = np.arange(128)
    sw = np.where((m % 64) < 32, m + 32, m - 32)
    cst[sw, 128 + m] = 1.0
    k = np.arange(128)[:, None]
    i = np.arange(128)[None, :]
    diag = (i >= k).astype(np.float32)
    nxt = (i <= k).astype(np.float32)
    m2 = np.concatenate([diag, nxt], axis=1)
    cst[:, 256:512] = m2
    cst[:, 512:768] = m2
    cst[:, 768] = 1.0
    cst[:, 771] = 1.0
    cst[:, 772:900] = (k <= i).astype(np.float32)
    cst[0, 900:964] = 1.0
    cst[1, 964:1028] = 1.0
    cst[:, 1028:1092] = 1.0
    cst[:, 1156:1220] = 1.0
    half = 32
    cos = sin = None
    try:
        import jax
        import jax.numpy as jnp
        with jax.default_device(jax.devices("cpu")[0]):
            inv_j = 10000.0 ** (-jnp.arange(half, dtype=jnp.float32) / half)
            ang_j = jnp.arange(S, dtype=jnp.float32)[:, None] * inv_j[None, :]
            cos = np.asarray(jnp.cos(ang_j), dtype=np.float32).T
            sin = np.asarray(jnp.sin(ang_j), dtype=np.float32).T
    except Exception:
        cos = sin = None
    if cos is None:
        inv = (np.float32(10000.0) ** (-np.arange(half, dtype=np.float32) / np.float32(half))).astype(np.float32)
        ang = (np.arange(S, dtype=np.float32)[:, None] * inv[None, :]).astype(np.float32)
        cos = np.cos(ang).astype(np.float32).T
        sin = np.sin(ang).astype(np.float32).T
    rope = np.zeros((128, 2, S), np.float32)
    for p in range(128):
        rope[p, 0] = cos[p % 32]
        rope[p, 1] = -sin[p % 32] if (p % 64) < 32 else sin[p % 32]
    return cst, rope


def prep_inputs(inputs):
    f = lambda a: np.ascontiguousarray(np.asarray(a, dtype=np.float32))
    g_all = np.stack([f(inputs["norm_mix_g"])[0], f(inputs["norm_mix_g"])[1],
                      f(inputs["norm_ffn_g"])[0], f(inputs["norm_ffn_g"])[1],
                      f(inputs["final_g"])], axis=0)
    gains = np.ascontiguousarray(g_all.reshape(5, KC, 128).transpose(2, 0, 1).reshape(128, 5 * KC))
    chunk4 = lambda v: f(v).reshape(4, 128).T
    convk_t = f(inputs["even_conv_k"])[0].reshape(31, 4, 128).transpose(2, 1, 0).reshape(128, 124)
    oddk_t = f(inputs["odd_conv_k"])[0].reshape(3, 4, 128).transpose(2, 1, 0).reshape(128, 12)
    small = np.concatenate([convk_t, chunk4(inputs["even_conv_b"][0]), chunk4(inputs["even_ln_g"][0]),
                            chunk4(inputs["even_ln_b"][0]), oddk_t, chunk4(inputs["odd_ln_g"][0]), chunk4(inputs["odd_ln_b"][0]),
                            np.zeros((128, 4), np.float32)], axis=1)
    wsT = f(inputs["odd_sg_w"])[0].transpose(2, 0, 1)
    sgb = np.broadcast_to(f(inputs["odd_sg_b"])[0][None], (128, 4, 128))
    cst, rope = _consts()
    shared = {
        "gains": gains, "ident_d": np.eye(128, dtype=np.float32), "cst_d": cst, "rope_d": rope,
        "small_d": np.ascontiguousarray(small), "sel_d": np.ascontiguousarray(cst[0:2, 900:1028]),
        "grow_d": np.ascontiguousarray(np.broadcast_to(f(inputs["final_g"])[None, :], (128, D))),
        "wsT_d": np.ascontiguousarray(wsT), "sgb_d": np.ascontiguousarray(sgb),
        "w_in0": f(inputs["even_w_in"][0]), "w_out0": f(inputs["even_w_out"][0]),
        "w_in1": f(inputs["odd_w_in"][0]), "w_out1": f(inputs["odd_w_out"][0]),
        "ffn_w1_0": f(inputs["ffn_w1"][0]), "ffn_w1_1": f(inputs["ffn_w1"][1]),
        "ffn_w2_0": f(inputs["ffn_w2"][0]), "ffn_w2_1": f(inputs["ffn_w2"][1]),
    }
    x = f(inputs["x"])
    maps = []
    for c in range(N_CORES):
        m = dict(shared)
        m["x"] = x[c]
        maps.append(m)
    return maps


_CFG = {}


def run(inputs, cfg=None, trace=False, cores=N_CORES):
    cfg = _CFG if cfg is None else cfg
    nc = Builder(cfg).build()
    maps = prep_inputs(inputs)[:cores]
    res = run_bass_kernel_spmd(nc, maps, core_ids=list(range(cores)), trace=trace)
    outs = np.stack([np.asarray(r["out"]) for r in res.results], axis=0)
    return outs, res


def kernel(**inputs):
    outs, _ = run(inputs)
    return outs.astype(np.float32)
```
